# Optimizing a Trainium2 kernel written in Bass

```python
import jax
import jax.numpy as jnp
from jax import lax
import numpy as np

D_MODEL = 2048
BATCH = 16
SEQ = 256
DEPTH = 1
DEC_BATCH = 8
DEC_SEQ = 4096
PAST_LEN = 256

GRID_W = 64
D_MIX = D_MODEL
D_CONV = D_MIX // 2
D_RNN = D_MIX - D_CONV
D_IN = 2 * D_CONV + 2 * D_RNN
CONV_W = 31
LRU_CONV_W = 4
LRU_BLOCKS = 16
LRU_BLK = D_RNN // LRU_BLOCKS
LRU_C = 8.0
N_KEYS = 128
N_EXPERTS = N_KEYS * N_KEYS
PEER_HEADS = 8
PEER_TOPK = 16
D_KEY = 128
TOKEN_BLOCK = 128
ALPHA = (2.0 * DEPTH) ** 0.25
BETA = (8.0 * DEPTH) ** -0.25

kernel_name = "hymba_conformer_rglru_peer_step"


def _layer_norm(x, g=None, b=None, eps=1e-6):
    xf = x.astype(jnp.float32)
    mu = jnp.mean(xf, axis=-1, keepdims=True)
    var = jnp.mean(jnp.square(xf - mu), axis=-1, keepdims=True)
    y = (xf - mu) * lax.rsqrt(var + eps)
    if g is not None:
        y = y * g.astype(jnp.float32) + b.astype(jnp.float32)
    return y.astype(x.dtype)


def _dwconv(x, w, b, pad):
    y = lax.conv_general_dilated(x, w[:, None, :], window_strides=(1,), padding=[pad],
                                 dimension_numbers=('NWC', 'WIO', 'NWC'),
                                 feature_group_count=x.shape[-1])
    return y + b


def _grid_pos_emb(rows, cols):
    quarter = D_MODEL // 4
    omega = 1.0 / (10000.0 ** (jnp.arange(quarter, dtype=jnp.float32) / quarter))
    pr = jnp.arange(rows, dtype=jnp.float32)[:, None] * omega
    pc = jnp.arange(cols, dtype=jnp.float32)[:, None] * omega
    er = jnp.concatenate([jnp.sin(pr), jnp.cos(pr)], axis=-1)
    ec = jnp.concatenate([jnp.sin(pc), jnp.cos(pc)], axis=-1)
    emb = jnp.concatenate([jnp.broadcast_to(er[:, None, :], (rows, cols, D_MODEL // 2)),
                           jnp.broadcast_to(ec[None, :, :], (rows, cols, D_MODEL // 2))], axis=-1)
    return emb.reshape(rows * cols, D_MODEL)


def _linear_scan(a, b, h0, reverse):
    if reverse:
        b = b.at[:, -1].add(a[:, -1] * h0)
    else:
        b = b.at[:, 0].add(a[:, 0] * h0)

    def combine(e1, e2):
        a1, b1 = e1
        a2, b2 = e2
        return a1 * a2, a2 * b1 + b2

    _, h = lax.associative_scan(combine, (a, b), axis=1, reverse=reverse)
    return h


def _rglru(xr, h0, w_a, b_a, w_x, b_x, lam, reverse):
    bsz, t = xr.shape[0], xr.shape[1]
    xb = xr.reshape(bsz, t, LRU_BLOCKS, LRU_BLK)
    r = jax.nn.sigmoid(jnp.einsum('btnk,nkj->btnj', xb, w_a).reshape(bsz, t, D_RNN) + b_a)
    i = jax.nn.sigmoid(jnp.einsum('btnk,nkj->btnj', xb, w_x).reshape(bsz, t, D_RNN) + b_x)
    log_a = -LRU_C * r.astype(jnp.float32) * jax.nn.softplus(-lam.astype(jnp.float32))
    a = jnp.exp(log_a)
    bterm = jnp.sqrt(-jnp.expm1(2.0 * log_a)) * (i * xr).astype(jnp.float32)
    return _linear_scan(a, bterm, h0.astype(jnp.float32), reverse)


def _mixer(u, h0, w_in, conv_w, conv_b, conv_ln_g, conv_ln_b, lru_conv_w, lru_conv_b,
           lru_wa, lru_ba, lru_wx, lru_bx, lru_lam, w_out):
    p = u @ w_in
    c_val, c_gate, r_gate, r_x = jnp.split(p, [D_CONV, 2 * D_CONV, 2 * D_CONV + D_RNN], axis=-1)
    z = c_val * jax.nn.sigmoid(c_gate)
    z = _dwconv(z, conv_w, conv_b, (CONV_W // 2, CONV_W // 2))
    z = jax.nn.silu(_layer_norm(z, conv_ln_g, conv_ln_b))
    xr = _dwconv(r_x, lru_conv_w, lru_conv_b, (2, 1))
    h_f = _rglru(xr, h0[:, 0], lru_wa[0], lru_ba[0], lru_wx[0], lru_bx[0], lru_lam[0], False)
    h_b = _rglru(xr, h0[:, 1], lru_wa[1], lru_ba[1], lru_wx[1], lru_bx[1], lru_lam[1], True)
    y_r = (h_f + h_b).astype(u.dtype) * jax.nn.gelu(r_gate)
    y = jnp.concatenate([z, y_r], axis=-1) @ w_out
    h_final = jnp.stack([h_f[:, -1], h_b[:, 0]], axis=1).astype(u.dtype)
    return y, h_final


def _peer(u, w_query, sub_keys, peer_u, peer_v):
    bsz, t, d = u.shape
    blocks = u.reshape(-1, TOKEN_BLOCK, d)

    def one_block(xb):
        q = (xb @ w_query).reshape(TOKEN_BLOCK, PEER_HEADS, 2, D_KEY)
        s = jnp.einsum('chpd,hpnd->chpn', q, sub_keys).astype(jnp.float32)
        sv, si = lax.top_k(s, PEER_TOPK)
        cand = (sv[:, :, 0, :, None] + sv[:, :, 1, None, :]).reshape(TOKEN_BLOCK, PEER_HEADS, PEER_TOPK * PEER_TOPK)
        cand_id = (si[:, :, 0, :, None] * N_KEYS + si[:, :, 1, None, :]).reshape(TOKEN_BLOCK, PEER_HEADS, PEER_TOPK * PEER_TOPK)
        top_s, top_pos = lax.top_k(cand, PEER_TOPK)
        ids = jnp.take_along_axis(cand_id, top_pos, axis=-1)
        g = jax.nn.softmax(top_s, axis=-1).astype(xb.dtype)
        ue = jnp.take(peer_u, ids, axis=0)
        act = jax.nn.gelu(jnp.einsum('chkd,cd->chk', ue, xb))
        ve = jnp.take(peer_v, ids, axis=0)
        return jnp.einsum('chk,chkd->cd', g * act, ve)

    return lax.map(one_block, blocks).reshape(bsz, t, d)


def _layer(x, mod, h0, w_in, conv_w, conv_b, conv_ln_g, conv_ln_b, lru_conv_w, lru_conv_b,
           lru_wa, lru_ba, lru_wx, lru_bx, lru_lam, w_out, ln1_g, ln1_b,
           w_query, sub_keys, peer_u, peer_v, ln2_g, ln2_b):
    sh1, sc1, g1, sh2, sc2, g2 = jnp.split(mod[:, None, :], 6, axis=-1)
    u = _layer_norm(x) * (1.0 + sc1) + sh1
    y, h_final = _mixer(u, h0, w_in, conv_w, conv_b, conv_ln_g, conv_ln_b, lru_conv_w, lru_conv_b,
                        lru_wa, lru_ba, lru_wx, lru_bx, lru_lam, w_out)
    x = _layer_norm(ALPHA * x + g1 * y, ln1_g, ln1_b)
    u = _layer_norm(x) * (1.0 + sc2) + sh2
    x = _layer_norm(ALPHA * x + g2 * _peer(u, w_query, sub_keys, peer_u, peer_v), ln2_g, ln2_b)
    return x, h_final


def setup_inputs(seed: int = 0) -> dict:
    key = jax.random.key(seed)
    ks = jax.random.split(key, 32)
    f32 = jnp.float32
    n = lambda k, shape, s: jax.random.normal(k, shape, f32) * s
    p_a = jax.random.uniform(ks[16], (DEPTH, 2, D_RNN), f32, 0.9, 0.999) ** (1.0 / LRU_C)
    return {
        'x_prompt': n(ks[0], (BATCH, SEQ, D_MODEL), 1.0),
        'x_sample': n(ks[1], (DEC_BATCH, DEC_SEQ, D_MODEL), 1.0),
        'state_rglru': n(ks[2], (DEC_BATCH, DEPTH, 2, D_RNN), 0.5),
        'c': n(ks[3], (DEC_BATCH, D_MODEL), 1.0),
        'c_ctx': n(ks[4], (D_MODEL,), 1.0),
        'w_ada': n(ks[5], (DEPTH, D_MODEL, 6 * D_MODEL), D_MODEL ** -0.5),
        'b_ada': n(ks[6], (DEPTH, 6 * D_MODEL), 0.01),
        'w_in': n(ks[7], (DEPTH, D_MODEL, D_IN), D_MODEL ** -0.5),
        'conv_w': n(ks[8], (DEPTH, CONV_W, D_CONV), CONV_W ** -0.5),
        'conv_b': n(ks[9], (DEPTH, D_CONV), 0.01),
        'conv_ln_g': 1.0 + n(ks[10], (DEPTH, D_CONV), 0.01),
        'conv_ln_b': n(ks[11], (DEPTH, D_CONV), 0.01),
        'lru_conv_w': n(ks[12], (DEPTH, LRU_CONV_W, D_RNN), LRU_CONV_W ** -0.5),
        'lru_conv_b': n(ks[13], (DEPTH, D_RNN), 0.01),
        'lru_wa': n(ks[14], (DEPTH, 2, LRU_BLOCKS, LRU_BLK, LRU_BLK), LRU_BLK ** -0.5),
        'lru_ba': n(ks[15], (DEPTH, 2, D_RNN), 0.01),
        'lru_wx': n(ks[17], (DEPTH, 2, LRU_BLOCKS, LRU_BLK, LRU_BLK), LRU_BLK ** -0.5),
        'lru_bx': n(ks[18], (DEPTH, 2, D_RNN), 0.01),
        'lru_lam': jnp.log(p_a) - jnp.log1p(-p_a),
        'w_out': n(ks[19], (DEPTH, D_MIX, D_MODEL), BETA * D_MIX ** -0.5),
        'ln1_g': 1.0 + n(ks[20], (DEPTH, D_MODEL), 0.01),
        'ln1_b': n(ks[21], (DEPTH, D_MODEL), 0.01),
        'w_query': n(ks[22], (DEPTH, D_MODEL, PEER_HEADS * 2 * D_KEY), D_MODEL ** -0.5),
        'sub_keys': n(ks[23], (DEPTH, PEER_HEADS, 2, N_KEYS, D_KEY), D_KEY ** -0.5),
        'peer_u': n(ks[24], (DEPTH, N_EXPERTS, D_MODEL), D_MODEL ** -0.5),
        'peer_v': n(ks[25], (DEPTH, N_EXPERTS, D_MODEL), BETA * PEER_HEADS ** -0.5),
        'ln2_g': 1.0 + n(ks[26], (DEPTH, D_MODEL), 0.01),
        'ln2_b': n(ks[27], (DEPTH, D_MODEL), 0.01),
    }


def reference(x_prompt, x_sample, state_rglru, c, c_ctx, w_ada, b_ada, w_in, conv_w, conv_b,
              conv_ln_g, conv_ln_b, lru_conv_w, lru_conv_b, lru_wa, lru_ba, lru_wx, lru_bx,
              lru_lam, w_out, ln1_g, ln1_b, w_query, sub_keys, peer_u, peer_v, ln2_g, ln2_b):
    x_p = x_prompt
    h_zero = jnp.zeros((x_prompt.shape[0], 2, D_RNN), x_prompt.dtype)
    rows = x_sample.shape[1] // GRID_W
    x_s = x_sample + _grid_pos_emb(rows, GRID_W).astype(x_sample.dtype)
    finals = []
    for l in range(DEPTH):
        lw = (w_in[l], conv_w[l], conv_b[l], conv_ln_g[l], conv_ln_b[l], lru_conv_w[l], lru_conv_b[l],
              lru_wa[l], lru_ba[l], lru_wx[l], lru_bx[l], lru_lam[l], w_out[l], ln1_g[l], ln1_b[l],
              w_query[l], sub_keys[l], peer_u[l], peer_v[l], ln2_g[l], ln2_b[l])
        mod_ctx = (jax.nn.silu(c_ctx) @ w_ada[l] + b_ada[l])[None, :]
        mod_lat = jax.nn.silu(c) @ w_ada[l] + b_ada[l]
        x_p, h_fin = _layer(x_p, mod_ctx, h_zero, *lw)
        finals.append(h_fin)
        x_s, _ = _layer(x_s, mod_lat, state_rglru[:, l], *lw)
    new_state_rglru = jnp.stack(finals, axis=1)
    return (x_p, x_s, new_state_rglru)
```

```python
import os
from contextlib import ExitStack
import numpy as np
import concourse.bass as bass
import concourse.mybir as mybir
from concourse.bass_utils import run_bass_kernel_spmd

F32 = mybir.dt.float32
BF16 = mybir.dt.bfloat16
U32 = mybir.dt.uint32
I32 = mybir.dt.int32
AF = mybir.ActivationFunctionType
ALU = mybir.AluOpType
AX = mybir.AxisListType

D = 2048
NT = 4608
SEQS = [(0, 4096, 1), (4096, 256, 0), (4352, 256, 0)]
ALPHA = 2.0 ** 0.25
EPS = 1e-6
NCORES = 8

COLS = {}
_off = 0
for _n, _w in [("cvT", 32), ("badaT", 96), ("conv_w", 8 * 31), ("conv_b", 8), ("cln_g", 8), ("cln_b", 8),
               ("w4", 32), ("b4", 8), ("ba", 16), ("bx", 16), ("lam", 16), ("h0", 16)]:
    COLS[_n] = (_off, _w)
    _off += _w
NCOL = _off


class Buf:
    __slots__ = ("w", "r", "multi", "ws")

    def __init__(self, multi=False):
        self.w = None
        self.r = {}
        self.multi = multi
        self.ws = {}


class Sched:
    EPOCH = 20000

    def __init__(self, nc, stack):
        self.nc = nc
        self.stack = stack
        self.engs = {"pe": nc.tensor, "dve": nc.vector, "act": nc.scalar, "pool": nc.gpsimd, "sp": nc.sync}
        self.csem = {}
        self.ccnt = {}
        self.waited = {e: {} for e in self.engs}
        self.nsem = 0
        self.allsems = []
        self.sem_owner = {}
        for e in ("pe", "dve", "act", "pool"):
            self._new_csem(e)
        self.slots = {}
        self.slot_i = {}
        for q, k in (("sp", 16), ("pool", 8), ("act", 4)):
            self.slots[q] = [[self._sem(f"d{q}{i}"), 0] for i in range(k)]
            self.slot_i[q] = 0

    def _same_eng(self, e, ev):
        return self.sem_owner.get(id(ev[0])) == e

    def _sem(self, name):
        self.nsem += 1
        s = self.stack.enter_context(self.nc.semaphore(f"{name}_{self.nsem}"))
        self.allsems.append(s)
        return s

    def _new_csem(self, e):
        self.csem[e] = self._sem("c" + e)
        self.sem_owner[id(self.csem[e])] = e
        self.ccnt[e] = 0

    def _wait(self, e, ev):
        sem, val = ev
        k = id(sem)
        if self.waited[e].get(k, 0) >= val:
            return
        self.engs[e].wait_ge(sem, val)
        self.waited[e][k] = val

    def _deps(self, e, reads, writes):
        for b in reads:
            if b.multi:
                for ev in list(b.ws.values()):
                    self._wait(e, ev)
                continue
            if b.w is not None:
                if not (e == "pe" and self._same_eng(e, b.w)):
                    self._wait(e, b.w)
        for b in writes:
            if b.w is not None and not b.multi:
                if not self._same_eng(e, b.w):
                    self._wait(e, b.w)
            for ev in list(b.r.values()):
                if self._same_eng(e, ev):
                    continue
                self._wait(e, ev)

    def _mark(self, ev, reads, writes):
        for b in reads:
            k = id(ev[0])
            if k not in b.r or b.r[k][1] < ev[1]:
                b.r[k] = ev
        for b in writes:
            if b.multi:
                k = id(ev[0])
                if k not in b.ws or b.ws[k][1] < ev[1]:
                    b.ws[k] = ev
                continue
            b.w = ev
            b.r = {}

    def op(self, e, fn, reads=(), writes=()):
        self._deps(e, reads, writes)
        inst = fn(self.engs[e])
        self.ccnt[e] += 1
        inst.then_inc(self.csem[e], 1)
        ev = (self.csem[e], self.ccnt[e])
        if self.ccnt[e] >= self.EPOCH:
            self._new_csem(e)
        self._mark(ev, reads, writes)
        return ev

    def dma(self, e, fn, reads=(), writes=()):
        self._deps(e, reads, writes)
        sl = self.slots[e]
        slot = sl[self.slot_i[e] % len(sl)]
        self.slot_i[e] += 1
        sem, uses = slot
        if uses > 0:
            self._wait(e, (sem, 16 * uses))
        inst = fn(self.engs[e])
        inst.then_inc(sem, 16)
        slot[1] = uses + 1
        ev = (sem, 16 * (uses + 1))
        self._mark(ev, reads, writes)
        return ev

    def barrier(self):
        evs = []
        for e in ("pe", "dve", "act", "pool"):
            if self.ccnt[e] > 0:
                evs.append((self.csem[e], self.ccnt[e]))
        for q in self.slots:
            for sem, uses in self.slots[q]:
                if uses > 0:
                    evs.append((sem, 16 * uses))
        for e in self.engs:
            for ev in evs:
                self._wait(e, ev)


def cap(full_ap, off, dims):
    return bass.AP(full_ap.tensor, full_ap.offset + off, [list(full_ap.ap[0])] + [list(d) for d in dims])


def dram_ap(full_ap, off, dims):
    return bass.AP(full_ap.tensor, full_ap.offset + off, [list(d) for d in dims])


def build_nc(stop_after=99, debug=False):
    nc = bass.Bass("TRN2", target_bir_lowering=False)
    okind = "ExternalOutput" if debug else "Internal"

    def din(name, shape, dt=F32):
        return nc.dram_tensor(name, list(shape), dt, kind="ExternalInput").ap()

    x_all = din("x_all", [NT, D])
    cols_d = din("cols", [128, NCOL])
    w_ada = din("w_ada", [D, 6 * D])
    wih = din("wih", [32, 128, 2048])
    gw_d = din("gw", [32, 128, 128])
    w_out = din("w_out", [D, D])
    lnp = din("lnp", [4, D])
    wqh = din("wqh", [16, 128, 2048])
    skt_d = din("skt", [128, 16 * 128])
    uth = din("uth", [128, 128, 2048])
    pv = din("pv", [16384, D])
    ident_d = din("ident", [128, 128])
    sel_d = din("sel", [64, 4096])
    iota_d = din("iota", [128, 128])

    y_all = nc.dram_tensor("y_all", [NT, D], F32, kind="ExternalOutput").ap()
    ns_d = nc.dram_tensor("ns", [4, 1024], F32, kind="ExternalOutput").ap()

    Zs = nc.dram_tensor("Zs", [1024, NT], F32, kind=okind).ap()
    CATC = nc.dram_tensor("CATC", [1024, NT], BF16, kind=okind).ap()
    bCATC = Buf(multi=True)
    RXs = nc.dram_tensor("RXs", [1024, NT], F32, kind=okind).ap()
    GRs = nc.dram_tensor("GRs", [1024, NT], F32, kind=okind).ap()
    CATR = nc.dram_tensor("CATR", [1024, NT], BF16, kind=okind).ap()
    X1s = nc.dram_tensor("X1s", [NT, D], F32, kind=okind).ap()
    U2T = nc.dram_tensor("U2T", [D, NT], BF16, kind=okind).ap()
    Gs = nc.dram_tensor("Gs", [128, 128, NT], BF16, kind="Internal").ap()
    UTB = nc.dram_tensor("UTB", [128, 128, 2048], BF16, kind="Internal").ap()
    PVB = nc.dram_tensor("PVB", [16384, D], BF16, kind="Internal").ap()
    bUTB = Buf(multi=True)
    bPVB = Buf(multi=True)
    ERs = nc.dram_tensor("ERs", [64, 1024], F32, kind="Internal").ap()
    bERs = Buf()
    bZ, bRX, bGR, bCATR, bX1, bU2T, bG, bY, bNS = (Buf(multi=True) for _ in range(9))

    with ExitStack() as top:
        top.enter_context(nc.allow_non_contiguous_dma(reason="small column/state transfers"))
        S = Sched(nc, top)

        uniq = [0]

        def sbt(stack, name, shape, dt=F32):
            uniq[0] += 1
            return stack.enter_context(nc.sbuf_tensor(f"s{uniq[0]}_{name}", list(shape), dt))

        cols = sbt(top, "cols", [128, NCOL])
        b_cols = Buf()
        ident = sbt(top, "ident", [128, 128])
        b_ident = Buf()
        modT = sbt(top, "modT", [128, 96, 2])
        b_modT = Buf()
        cdec = sbt(top, "cdec", [128, 32])
        b_cdec = Buf()
        NSt = sbt(top, "NSt", [128, 32])
        b_NSt = Buf()
        EC = sbt(top, "EC", [128, 1024])
        b_EC = Buf()
        ER = sbt(top, "ER", [64, 1024])
        b_ER = Buf()
        ERBS = {"t": [], "b": []}

        def make_erb(stack, n):
            ERBS["t"] = [sbt(stack, f"ERB{i}", [128, 1024]) for i in range(n)]
            ERBS["b"] = [Buf() for _ in range(n)]
        sel_ctr = [0]
        psb = [top.enter_context(nc.psum_tensor(f"ps{i}", [128, 512], F32)) for i in range(8)]
        b_ps = [Buf() for _ in range(8)]

        def col(name, j0=0, n=1):
            o, w = COLS[name]
            return cols[:, o + j0:o + j0 + n]

        S.dma("sp", lambda e: e.dma_start(out=cols[:], in_=cols_d[:, :]), writes=[b_cols])
        S.dma("sp", lambda e: e.dma_start(out=ident[:], in_=ident_d[:, :]), writes=[b_ident])

        with ExitStack() as ph:
            scT = sbt(ph, "scT", [128, 32])
            b_scT = Buf()
            WA = [sbt(ph, f"WA{i}", [128, 16, 512]) for i in range(2)]
            b_WA = [Buf() for _ in range(2)]
            tmp0 = sbt(ph, "tmp0", [128, 1024])
            b_tmp0 = Buf()
            tmp1 = sbt(ph, "tmp1", [128, 1024])
            b_tmp1 = Buf()
            tmpi = sbt(ph, "tmpi", [128, 1024], I32)
            b_tmpi = Buf()
            iot = sbt(ph, "iot", [128, 128])
            b_iot = Buf()
            pidx = sbt(ph, "pidx", [128, 4])
            b_pidx = Buf()

            S.op("act", lambda e: e.activation(out=scT[:], in_=col("cvT", 0, 32), func=AF.Silu),
                 reads=[b_cols], writes=[b_scT])
            w_ada_v = w_ada.rearrange("(kk p) n -> p kk n", p=128)
            Rrow = [sbt(ph, f"Rrow{i}", [2, 512]) for i in range(2)]
            b_Rrow = [Buf() for _ in range(2)]
            for nb in range(24):
                wa = WA[nb % 2]
                bwa = b_WA[nb % 2]
                S.dma("sp", lambda e, wa=wa, nb=nb: e.dma_start(out=wa[:], in_=w_ada_v[:, :, nb * 512:(nb + 1) * 512]),
                      writes=[bwa])
                pb = nb % 2
                for kk in range(16):
                    S.op("pe", lambda e, wa=wa, kk=kk, pb=pb: e.matmul(
                        psb[pb][0:2, :], lhsT=scT[:, kk * 2:kk * 2 + 2], rhs=wa[:, kk, :],
                        start=(kk == 0), stop=(kk == 15)), reads=[bwa, b_scT], writes=[b_ps[pb]])
                rr, brr = Rrow[nb % 2], b_Rrow[nb % 2]
                S.op("act", lambda e, rr=rr, pb=pb: e.activation(out=rr[:], in_=psb[pb][0:2, :], func=AF.Copy),
                     reads=[b_ps[pb]], writes=[brr])
                pt = 2 + (nb % 2)
                for nn in range(4):
                    S.op("pe", lambda e, rr=rr, nn=nn, pt=pt: e.transpose(
                        out=psb[pt][:, nn * 2:nn * 2 + 2], in_=rr[0:2, nn * 128:(nn + 1) * 128],
                        identity=ident[0:2, 0:2]), reads=[brr, b_ident], writes=[b_ps[pt]])
                for nn in range(4):
                    nch = nb * 4 + nn
                    S.op("dve", lambda e, nch=nch, nn=nn, pt=pt: e.tensor_scalar(
                        out=modT[:, nch, :], in0=psb[pt][:, nn * 2:nn * 2 + 2], scalar1=col("badaT", nch),
                        scalar2=None, op0=ALU.add), reads=[b_ps[pt], b_cols], writes=[b_modT])
            for base in (16, 64):
                S.op("dve", lambda e, base=base: e.tensor_scalar(
                    out=modT[:, base:base + 16, :], in0=modT[:, base:base + 16, :], scalar1=1.0, scalar2=None,
                    op0=ALU.add), reads=[b_modT], writes=[b_modT])
            S.op("act", lambda e: e.activation(out=tmp0[:, 0:16], in_=col("lam", 0, 16), func=AF.Exp, scale=-1.0),
                 reads=[b_cols], writes=[b_tmp0])
            S.op("act", lambda e: e.activation(out=tmp0[:, 16:32], in_=tmp0[:, 0:16], func=AF.Ln, bias=1.0),
                 reads=[b_tmp0], writes=[b_tmp0])
            S.op("dve", lambda e: e.tensor_scalar(out=cdec[:, 0:16], in0=tmp0[:, 16:32], scalar1=-8.0, scalar2=None,
                                                  op0=ALU.mult), reads=[b_tmp0], writes=[b_cdec])
            S.op("dve", lambda e: e.tensor_scalar(out=cdec[:, 16:32], in0=tmp0[:, 16:32], scalar1=-16.0, scalar2=None,
                                                  op0=ALU.mult), reads=[b_tmp0], writes=[b_cdec])
            S.dma("sp", lambda e: e.dma_start(out=iot[:], in_=iota_d[:, :]), writes=[b_iot])
            S.op("pe", lambda e: e.transpose(out=psb[4][:, 0:128], in_=iot[:], identity=ident[:]),
                 reads=[b_iot, b_ident], writes=[b_ps[4]])
            S.op("dve", lambda e: e.tensor_copy(out=pidx[:, 0:1], in_=psb[4][:, 0:1]), reads=[b_ps[4]], writes=[b_pidx])
            S.op("dve", lambda e: e.tensor_scalar(out=pidx[:, 1:2], in0=pidx[:, 0:1], scalar1=64.0, scalar2=-64.0,
                                                  op0=ALU.is_ge, op1=ALU.mult), reads=[b_pidx], writes=[b_pidx])
            S.op("dve", lambda e: e.tensor_tensor(out=pidx[:, 2:3], in0=pidx[:, 0:1], in1=pidx[:, 1:2], op=ALU.add),
                 reads=[b_pidx], writes=[b_pidx])
            for h in range(4):
                S.op("act", lambda e, h=h: e.activation(
                    out=tmp0[:, 512 + h * 128:512 + (h + 1) * 128], in_=iot[:], func=AF.Exp,
                    scale=-float(np.log(10000.0) / 512.0), bias=-float(np.log(10000.0) / 512.0) * 128.0 * h),
                    reads=[b_iot], writes=[b_tmp0])
            S.op("dve", lambda e: e.tensor_scalar(out=tmp0[:, 512:1024], in0=tmp0[:, 512:1024],
                                                  scalar1=float(1.0 / (2.0 * np.pi)), scalar2=None, op0=ALU.mult),
                 reads=[b_tmp0], writes=[b_tmp0])

            def sincos(dst, bdst, npart, idxcol):
                for part, shift in ((0, 0.0), (1, 0.25)):
                    S.op("dve", lambda e, shift=shift: e.tensor_scalar(
                        out=tmp1[:npart, 0:512], in0=tmp0[:npart, 512:1024], scalar1=idxcol, scalar2=shift,
                        op0=ALU.mult, op1=ALU.add), reads=[b_tmp0, b_pidx], writes=[b_tmp1])
                    S.op("dve", lambda e: e.tensor_copy(out=tmpi[:npart, 0:512], in_=tmp1[:npart, 0:512]),
                         reads=[b_tmp1], writes=[b_tmpi])
                    S.op("dve", lambda e: e.tensor_copy(out=tmp1[:npart, 512:1024], in_=tmpi[:npart, 0:512]),
                         reads=[b_tmpi], writes=[b_tmp1])
                    S.op("dve", lambda e: e.tensor_tensor(out=tmp1[:npart, 0:512], in0=tmp1[:npart, 0:512],
                                                          in1=tmp1[:npart, 512:1024], op=ALU.subtract),
                         reads=[b_tmp1], writes=[b_tmp1])
                    S.op("dve", lambda e: e.tensor_scalar(out=tmp1[:npart, 512:1024], in0=tmp1[:npart, 0:512],
                                                          scalar1=0.5, scalar2=None, op0=ALU.is_gt),
                         reads=[b_tmp1], writes=[b_tmp1])
                    S.op("dve", lambda e: e.tensor_tensor(out=tmp1[:npart, 0:512], in0=tmp1[:npart, 0:512],
                                                          in1=tmp1[:npart, 512:1024], op=ALU.subtract),
                         reads=[b_tmp1], writes=[b_tmp1])
                    S.op("dve", lambda e: e.tensor_scalar(out=tmp1[:npart, 512:1024], in0=tmp1[:npart, 0:512],
                                                          scalar1=-0.5, scalar2=None, op0=ALU.is_lt),
                         reads=[b_tmp1], writes=[b_tmp1])
                    S.op("dve", lambda e: e.tensor_tensor(out=tmp1[:npart, 0:512], in0=tmp1[:npart, 0:512],
                                                          in1=tmp1[:npart, 512:1024], op=ALU.add),
                         reads=[b_tmp1], writes=[b_tmp1])
                    S.op("dve", lambda e: e.tensor_scalar(out=tmp1[:npart, 0:512], in0=tmp1[:npart, 0:512],
                                                          scalar1=0.49999, scalar2=-0.49999, op0=ALU.min, op1=ALU.max),
                         reads=[b_tmp1], writes=[b_tmp1])
                    S.op("act", lambda e, part=part: e.activation(
                        out=dst[:npart, part * 512:(part + 1) * 512], in_=tmp1[:npart, 0:512], func=AF.Sin,
                        scale=float(2.0 * np.pi)), reads=[b_tmp1], writes=[bdst])

            sincos(EC, b_EC, 128, pidx[:, 2:3])
            sincos(ER, b_ER, 64, pidx[:64, 0:1])
            S.dma("sp", lambda e: e.dma_start(out=ERs[:, :], in_=ER[:]), reads=[b_ER], writes=[bERs])
            S.op("pool", lambda e: e.memset(NSt[:], 0.0), writes=[b_NSt])
            S.barrier()

        def bcast_mod(dst, bdst, chunk0, r, tmpt, btmp):
            for q in range(4):
                pb = 4 + (q % 2)
                for j in range(4):
                    kk = q * 4 + j
                    S.op("dve", lambda e, kk=kk: e.tensor_copy(
                        out=tmpt[:], in_=cap(modT[:], (chunk0 + kk) * 2 + r, [[0, 128]])),
                        reads=[b_modT], writes=[btmp])
                    S.op("pe", lambda e, j=j, pb=pb: e.matmul(psb[pb][:, j * 128:(j + 1) * 128], lhsT=tmpt[:],
                                                             rhs=ident[:], start=True, stop=True),
                         reads=[btmp, b_ident], writes=[b_ps[pb]])
                S.op("act", lambda e, q=q, pb=pb: e.activation(out=dst[:, q * 512:(q + 1) * 512], in_=psb[pb][:],
                                                              func=AF.Copy), reads=[b_ps[pb]], writes=[bdst])

        def ln_stats(xap, bx, st, bst):
            for q in range(4):
                S.op("dve", lambda e, q=q: e.bn_stats(out=st[:, 8 + q * 6:8 + (q + 1) * 6], in_=xap[:, q * 512:(q + 1) * 512]),
                     reads=[bx], writes=[bst])
            S.op("dve", lambda e: e.bn_aggr(out=st[:, 0:2], in_=st[:, 8:32]), reads=[bst], writes=[bst])
            S.op("act", lambda e: e.activation(out=st[:, 2:3], in_=st[:, 1:2], func=AF.Sqrt, bias=EPS, scale=1.0),
                 reads=[bst], writes=[bst])
            S.op("dve", lambda e: e.reciprocal(out=st[:, 2:3], in_=st[:, 2:3]), reads=[bst], writes=[bst])

        def load_x_dma(xt, bxt, tok0, mrow, tile_in_seq):
            S.dma("sp", lambda e: e.dma_start(out=xt[:], in_=x_all[tok0:tok0 + 128, :]), writes=[bxt])
            if mrow != 1:
                return None
            k_ = sel_ctr[0] % len(ERBS["t"])
            sel_ctr[0] += 1
            erb, berb = ERBS["t"][k_], ERBS["b"][k_]
            for hh in range(2):
                S.dma("sp", lambda e, hh=hh: e.dma_start(
                    out=erb[hh * 64:(hh + 1) * 64, :],
                    in_=dram_ap(ERs, (2 * tile_in_seq + hh) * 1024, [[0, 64], [1, 1024]])),
                    reads=[bERs], writes=[berb])
            return (erb, berb)

        def load_x_add(xt, bxt, eh, ec_eng="pool"):
            if eh is None:
                return
            erb, berb = eh
            S.op("dve", lambda e: e.tensor_tensor(out=xt[:, 0:1024], in0=xt[:, 0:1024], in1=erb[:],
                                                  op=ALU.add), reads=[berb, bxt], writes=[bxt])
            S.op(ec_eng, lambda e: e.tensor_tensor(out=xt[:, 1024:2048], in0=xt[:, 1024:2048], in1=EC[:],
                                                   op=ALU.add), reads=[b_EC, bxt], writes=[bxt])

        def load_x_tile(xt, bxt, tok0, mrow, tile_in_seq, ec_eng="pool"):
            eh = load_x_dma(xt, bxt, tok0, mrow, tile_in_seq)
            load_x_add(xt, bxt, eh, ec_eng)

        def transpose_mod(xh, bxh, dstT, bdst, c0, sc_chunk, sh_chunk, r, tb0=4):
            for q in range(4):
                pb = tb0 + (q % 2)
                for j in range(4):
                    kk = q * 4 + j
                    S.op("pe", lambda e, kk=kk, j=j, pb=pb: e.transpose(
                        out=psb[pb][:, j * 128:(j + 1) * 128], in_=xh[:, kk * 128:(kk + 1) * 128], identity=ident[:]),
                        reads=[bxh, b_ident], writes=[b_ps[pb]])
                for j in range(4):
                    kk = q * 4 + j
                    S.op("dve", lambda e, kk=kk, j=j, pb=pb: e.tensor_scalar(
                        out=dstT[:, kk, c0:c0 + 128], in0=psb[pb][:, j * 128:(j + 1) * 128],
                        scalar1=modT[:, sc_chunk + kk, r:r + 1], scalar2=modT[:, sh_chunk + kk, r:r + 1],
                        op0=ALU.mult, op1=ALU.add), reads=[b_ps[pb], b_modT], writes=[bdst])

        blocks = []
        for (g0, L, mrow) in SEQS:
            T = 512 if L >= 512 else L
            for s in range(0, L, T):
                blocks.append((g0, L, mrow, s, T))

        blocks1 = []
        for (g0, L, mrow) in SEQS:
            T = 1024 if L >= 1024 else L
            for s_ in range(0, L, T):
                blocks1.append((g0, L, mrow, s_, T))

        def uvconv_gen(ph):
            stg = [sbt(ph, f"stg{i}", [128, D], BF16) for i in range(2)]
            b_stg = [Buf() for _ in range(2)]
            k = 0
            for i in range(128):
                for which in range(2):
                    st_, bst_ = stg[k % 2], b_stg[k % 2]
                    k += 1
                    if which == 0:
                        S.dma("pool", lambda e: e.dma_start(out=st_[:], in_=uth[i, :, :]), writes=[bst_])
                        S.dma("sp", lambda e: e.dma_start(out=UTB[i, :, :], in_=st_[:]), reads=[bst_], writes=[bUTB])
                    else:
                        S.dma("pool", lambda e: e.dma_start(out=st_[:], in_=pv[i * 128:(i + 1) * 128, :]),
                              writes=[bst_])
                        S.dma("sp", lambda e: e.dma_start(out=PVB[i * 128:(i + 1) * 128, :], in_=st_[:]),
                              reads=[bst_], writes=[bPVB])
                    yield

        def p2b1_gen(ph):
            zb = [sbt(ph, f"zb{i}", [128, 512 + 32]) for i in range(4)]
            b_zb = [Buf() for _ in range(4)]
            ct = sbt(ph, "ct", [128, 8, 512])
            b_ctl = [Buf() for _ in range(8)]
            sq = [sbt(ph, f"sq{i}", [128, 512]) for i in range(2)]
            b_sq = [Buf() for _ in range(2)]
            mu = sbt(ph, "mu", [128, 512])
            b_mu = Buf()
            rs = sbt(ph, "rs", [128, 512])
            b_rs = Buf()
            catc = [sbt(ph, f"catc{i}", [128, 8, 512], BF16) for i in range(1)]
            b_catc = [Buf() for _ in range(1)]
            ones = sbt(ph, "ones", [128, 128])
            b_ones = Buf()
            S.op("pool", lambda e: e.memset(ones[:], 1.0), writes=[b_ones])
            yield "alloc"
            zi = 0
            for bi, (g0, L, mrow, s, T) in enumerate(blocks):
                lo = max(s - 15, 0)
                hi = min(s + T + 15, L)
                doff = lo - (s - 15)
                while zready.get(g0, 0) < hi:
                    yield "wait"
                for cp in range(0, 8, 2):
                    zs_ = []
                    for cc in (cp, cp + 1):
                        z, bz = zb[zi % 4], b_zb[zi % 4]
                        zi += 1
                        zs_.append((z, bz))
                        if lo > s - 15 or hi < s + T + 15:
                            S.op("dve", lambda e, z=z: e.memset(z[:], 0.0), writes=[bz])
                        S.dma("sp", lambda e, z=z, cc=cc, lo=lo, hi=hi, doff=doff, g0=g0: e.dma_start(
                            out=z[:, doff:doff + hi - lo], in_=Zs[cc * 128:(cc + 1) * 128, g0 + lo:g0 + hi]),
                            reads=[bZ], writes=[bz])
                    for k in range(31):
                        for q_, cc in enumerate((cp, cp + 1)):
                            z, bz = zs_[q_]
                            if k == 0:
                                S.op("dve", lambda e, z=z, cc=cc, T=T: e.tensor_scalar(
                                    out=ct[:, cc, 0:T], in0=z[:, 0:T], scalar1=col("conv_w", cc * 31),
                                    scalar2=col("conv_b", cc), op0=ALU.mult, op1=ALU.add), reads=[bz, b_cols],
                                    writes=[b_ctl[cc]])
                            else:
                                S.op("dve", lambda e, z=z, cc=cc, T=T, k=k: e.scalar_tensor_tensor(
                                    out=ct[:, cc, 0:T], in0=z[:, k:k + T], scalar=col("conv_w", cc * 31 + k),
                                    in1=ct[:, cc, 0:T], op0=ALU.mult, op1=ALU.add),
                                    reads=[bz, b_cols, b_ctl[cc]], writes=[b_ctl[cc]])
                            yield
                    for cc in (cp, cp + 1):
                        S.op("act", lambda e, cc=cc, T=T: e.activation(out=sq[cc % 2][:, 0:T], in_=ct[:, cc, 0:T],
                                                                       func=AF.Square), reads=[b_ctl[cc]],
                             writes=[b_sq[cc % 2]])
                        S.op("pe", lambda e, cc=cc, T=T: e.matmul(psb[6][:, 0:T], lhsT=ones[:], rhs=ct[:, cc, 0:T],
                                                                  start=True, stop=True),
                             reads=[b_ones, b_ctl[cc]], writes=[b_ps[6]])
                        if cc == 0:
                            S.op("dve", lambda e, T=T: e.tensor_copy(out=mu[:, 0:T], in_=psb[6][:, 0:T]),
                                 reads=[b_ps[6]], writes=[b_mu])
                        else:
                            S.op("dve", lambda e, T=T: e.tensor_tensor(out=mu[:, 0:T], in0=psb[6][:, 0:T],
                                                                       in1=mu[:, 0:T], op=ALU.add),
                                 reads=[b_ps[6], b_mu], writes=[b_mu])
                        S.op("pe", lambda e, cc=cc, T=T: e.matmul(psb[7][:, 0:T], lhsT=ones[:],
                                                                  rhs=sq[cc % 2][:, 0:T], start=True, stop=True),
                             reads=[b_ones, b_sq[cc % 2]], writes=[b_ps[7]])
                        if cc == 0:
                            S.op("dve", lambda e, T=T: e.tensor_copy(out=rs[:, 0:T], in_=psb[7][:, 0:T]),
                                 reads=[b_ps[7]], writes=[b_rs])
                        else:
                            S.op("dve", lambda e, T=T: e.tensor_tensor(out=rs[:, 0:T], in0=psb[7][:, 0:T],
                                                                       in1=rs[:, 0:T], op=ALU.add),
                                 reads=[b_ps[7], b_rs], writes=[b_rs])
                S.op("dve", lambda e, T=T: e.tensor_scalar(out=mu[:, 0:T], in0=mu[:, 0:T], scalar1=1.0 / 1024.0,
                                                           scalar2=None, op0=ALU.mult), reads=[b_mu],
                     writes=[b_mu])
                S.op("dve", lambda e, T=T: e.tensor_tensor(out=sq[0][:, 0:T], in0=mu[:, 0:T], in1=mu[:, 0:T],
                                                           op=ALU.mult), reads=[b_mu], writes=[b_sq[0]])
                S.op("dve", lambda e, T=T: e.scalar_tensor_tensor(
                    out=rs[:, 0:T], in0=rs[:, 0:T], scalar=1.0 / 1024.0, in1=sq[0][:, 0:T], op0=ALU.mult,
                    op1=ALU.subtract), reads=[b_rs, b_sq[0]], writes=[b_rs])
                S.op("act", lambda e, T=T: e.activation(out=rs[:, 0:T], in_=rs[:, 0:T], func=AF.Sqrt, bias=EPS,
                                                        scale=1.0), reads=[b_rs], writes=[b_rs])
                S.op("dve", lambda e, T=T: e.reciprocal(out=rs[:, 0:T], in_=rs[:, 0:T]), reads=[b_rs],
                     writes=[b_rs])
                cco, bcco = catc[0], b_catc[0]
                for cc in range(8):
                    S.op("dve", lambda e, cc=cc, T=T: e.tensor_tensor(
                        out=ct[:, cc, 0:T], in0=ct[:, cc, 0:T], in1=mu[:, 0:T], op=ALU.subtract),
                        reads=[b_ctl[cc], b_mu], writes=[b_ctl[cc]])
                    S.op("dve", lambda e, cc=cc, T=T: e.tensor_tensor(
                        out=ct[:, cc, 0:T], in0=ct[:, cc, 0:T], in1=rs[:, 0:T], op=ALU.mult),
                        reads=[b_ctl[cc], b_rs], writes=[b_ctl[cc]])
                    S.op("act", lambda e, cc=cc, T=T, cco=cco: e.activation(
                        out=cco[:, cc, 0:T], in_=ct[:, cc, 0:T], func=AF.Silu, scale=col("cln_g", cc),
                        bias=col("cln_b", cc)), reads=[b_ctl[cc], b_cols], writes=[bcco])
                S.dma("sp", lambda e, g0=g0, s=s, T=T, cco=cco: e.dma_start(
                    out=CATC[:, g0 + s:g0 + s + T].rearrange("(c p) t -> p c t", p=128), in_=cco[:, :, 0:T]),
                    reads=[bcco], writes=[bCATC])

        zready = {}
        cst = ExitStack()
        cgen0 = [None]
        if stop_after >= 3:
            cgen0[0] = p2b1_gen(cst)
            assert next(cgen0[0]) == "alloc"

        if stop_after >= 1:
            with ExitStack() as ph:
                xts = [sbt(ph, f"xt{i}", [128, D]) for i in range(3)]
                b_xts = [Buf() for _ in range(3)]
                sts = [sbt(ph, f"st{i}", [128, 32]) for i in range(2)]
                b_sts = [Buf() for _ in range(2)]
                uTs = [sbt(ph, f"uT{i}", [128, 16, 1024], BF16) for i in range(2)]
                b_uTs = [Buf() for _ in range(2)]
                Wb = [sbt(ph, f"Wb{i}", [128, 16, 128], BF16) for i in range(4)]
                b_Wb = [Buf() for _ in range(4)]
                sg = [sbt(ph, f"sg{i}", [128, 512]) for i in range(3)]
                b_sg = [Buf() for _ in range(3)]
                zo = [sbt(ph, f"zo{i}", [128, 512], BF16) for i in range(2)]
                b_zo = [Buf() for _ in range(2)]
                ot = [sbt(ph, f"ot{i}", [128, 512]) for i in range(4)]
                b_ot = [Buf() for _ in range(4)]
                xi = 0
                wi = 0
                oi = 0
                gi = 0
                xi_ = [0]
                lnq = []
                make_erb(ph, 2)

                def ln_tile(bi, ti):
                    g0, L, mrow, s, T = blocks1[bi]
                    uT, b_uT = uTs[bi % 2], b_uTs[bi % 2]
                    xt, bxt = xts[xi_[0] % 3], b_xts[xi_[0] % 3]
                    st, bst = sts[xi_[0] % 2], b_sts[xi_[0] % 2]
                    xi_[0] += 1
                    tok0 = g0 + s + ti * 128
                    load_x_tile(xt, bxt, tok0, mrow, (s + ti * 128) // 128, ec_eng="dve")
                    ln_stats(xt, bxt, st, bst)
                    S.op("dve", lambda e, xt=xt, st=st: e.tensor_scalar(
                        out=xt[:], in0=xt[:], scalar1=st[:, 0:1], scalar2=st[:, 2:3], op0=ALU.subtract,
                        op1=ALU.mult), reads=[bxt, bst], writes=[bxt])
                    lnq.append((xt, bxt, uT, b_uT, ti * 128, mrow))

                def ln_b():
                    if lnq:
                        xt, bxt, uT, b_uT, c0, mrow = lnq.pop(0)
                        transpose_mod(xt, bxt, uT, b_uT, c0, 16, 0, mrow, tb0=6)

                order = []
                for cc in range(8):
                    order += [cc, 8 + cc]
                order += list(range(16, 32))
                for bi, (g0, L, mrow, s, T) in enumerate(blocks1):
                    uT, b_uT = uTs[bi % 2], b_uTs[bi % 2]
                    if bi == 0:
                        for ti in range(T // 128):
                            ln_tile(0, ti)
                            ln_b()
                    while lnq:
                        ln_b()
                    nxt_tiles = list(range(blocks1[bi + 1][4] // 128)) if bi + 1 < len(blocks1) else []
                    pend_val = None
                    nh = max(T // 512, 1)
                    Th = min(T, 512)
                    for idx, n in enumerate(order):
                        if idx == 16:
                            zready[g0] = s + T
                        if cgen0[0] is not None and idx >= 1:
                            for _ in range(6 if T >= 1024 else 2):
                                if next(cgen0[0], None) == "wait":
                                    break
                        if idx % 4 == 1:
                            ln_b()
                            if nxt_tiles:
                                ln_tile(bi + 1, nxt_tiles.pop(0))
                        wb, bwb = Wb[wi % 4], b_Wb[wi % 4]
                        wi += 1
                        S.dma("pool", lambda e, wb=wb, n=n: e.dma_start(
                            out=wb[:].rearrange("p a b -> p (a b)"), in_=wih[n, :, :]), writes=[bwb])
                        pbs = [((wi % 3) * 2 + hh) for hh in range(nh)]
                        for kk in range(16):
                            for hh in range(nh):
                                S.op("pe", lambda e, wb=wb, kk=kk, hh=hh: e.matmul(
                                    psb[pbs[hh]][:, 0:Th], lhsT=wb[:, kk, :], rhs=uT[:, kk, hh * 512:hh * 512 + Th],
                                    start=(kk == 0), stop=(kk == 15)), reads=[bwb, b_uT], writes=[b_ps[pbs[hh]]])
                        if n < 8:
                            pend_val = pbs
                            continue
                        for hh in range(nh):
                            pb = pbs[hh]
                            tsl = slice(g0 + s + hh * 512, g0 + s + hh * 512 + Th)
                            if n < 16:
                                cc = n - 8
                                sgt, bsg = sg[gi % 3], b_sg[gi % 3]
                                gi += 1
                                o, bo = ot[oi % 4], b_ot[oi % 4]
                                oi += 1
                                S.op("act", lambda e, sgt=sgt, pb=pb: e.activation(
                                    out=sgt[:, 0:Th], in_=psb[pb][:, 0:Th], func=AF.Sigmoid),
                                    reads=[b_ps[pb]], writes=[bsg])
                                pv_ = pend_val[hh]
                                S.op("dve", lambda e, o=o, sgt=sgt, pv_=pv_: e.tensor_tensor(
                                    out=o[:, 0:Th], in0=psb[pv_][:, 0:Th], in1=sgt[:, 0:Th], op=ALU.mult),
                                    reads=[b_ps[pv_], bsg], writes=[bo])
                                S.dma("sp", lambda e, o=o, cc=cc, tsl=tsl: e.dma_start(
                                    out=Zs[cc * 128:(cc + 1) * 128, tsl], in_=o[:, 0:Th]), reads=[bo], writes=[bZ])
                            elif n < 24:
                                cc = n - 16
                                o, bo = ot[oi % 4], b_ot[oi % 4]
                                oi += 1
                                S.op("act", lambda e, o=o, pb=pb: e.activation(
                                    out=o[:, 0:Th], in_=psb[pb][:, 0:Th], func=AF.Gelu_apprx_tanh),
                                    reads=[b_ps[pb]], writes=[bo])
                                S.dma("sp", lambda e, o=o, cc=cc, tsl=tsl: e.dma_start(
                                    out=GRs[cc * 128:(cc + 1) * 128, tsl], in_=o[:, 0:Th]), reads=[bo], writes=[bGR])
                            else:
                                cc = n - 24
                                o, bo = ot[oi % 4], b_ot[oi % 4]
                                oi += 1
                                S.op("dve", lambda e, o=o, pb=pb: e.tensor_copy(out=o[:, 0:Th], in_=psb[pb][:, 0:Th]),
                                     reads=[b_ps[pb]], writes=[bo])
                                S.dma("sp", lambda e, o=o, cc=cc, tsl=tsl: e.dma_start(
                                    out=RXs[cc * 128:(cc + 1) * 128, tsl], in_=o[:, 0:Th]), reads=[bo], writes=[bRX])
                    while nxt_tiles:
                        ln_b()
                        ln_tile(bi + 1, nxt_tiles.pop(0))
                S.barrier()

        if stop_after >= 2:
            with ExitStack() as ph:
                LM = 4096
                rxp = sbt(ph, "rxp", [128, LM + 4])
                b_rxp = Buf()
                xr = sbt(ph, "xr", [128, LM])
                b_xr = Buf()
                xrb = sbt(ph, "xrb", [128, LM], BF16)
                b_xrb = Buf()
                At = sbt(ph, "At", [128, LM])
                b_A = Buf()
                Bt = sbt(ph, "Bt", [128, LM])
                b_B = Buf()
                Tm = sbt(ph, "Tm", [128, LM])
                b_T = Buf()
                HF = sbt(ph, "HF", [128, LM])
                b_HF = Buf()
                GRt = sbt(ph, "GRt", [128, LM])
                b_GRt = Buf()
                yrb = sbt(ph, "yrb", [128, LM], BF16)
                b_yrb = Buf()
                GWb = sbt(ph, "GWb", [128, 32, 128], BF16)
                b_GWb = Buf()
                b_GWf = Buf()
                zcol = sbt(ph, "zcol", [128, 1])
                b_zcol = Buf()
                S.op("pool", lambda e: e.memset(zcol[:], 0.0), writes=[b_zcol])
                inner_ = ExitStack()
                GWf = sbt(inner_, "GWf", [128, 32, 128])
                S.dma("sp", lambda e: e.dma_start(out=GWf[:], in_=gw_d.rearrange("g p m -> p g m")), writes=[b_GWf])
                S.op("act", lambda e: e.activation(out=GWb[:], in_=GWf[:], func=AF.Copy), reads=[b_GWf],
                     writes=[b_GWb])
                S.barrier()
                inner_.close()
                for (g0_, L_, m_) in SEQS:
                    zready[g0_] = L_
                cgen = cgen0[0] if cgen0[0] is not None else iter(())
                ugen = uvconv_gen(ph) if stop_after >= 5 else iter(())
                debt = [0.0]
                npull = [0]

                def cstep():
                    npull[0] += 1
                    if npull[0] % 6 == 0:
                        next(ugen, None)
                    return next(cgen, None)

                def pull(us):
                    debt[0] += us
                    while debt[0] >= 0.3:
                        debt[0] -= 0.3
                        cstep()
                pbi = 0
                for ch in range(8):
                    for si, (g0, L, mrow) in enumerate(SEQS):
                        TB = 512 if L >= 512 else L
                        S.op("pool", lambda e: e.memset(rxp[:, 0:2], 0.0), writes=[b_rxp])
                        S.op("pool", lambda e, L=L: e.memset(rxp[:, 2 + L:3 + L], 0.0), writes=[b_rxp])
                        S.dma("sp", lambda e, ch=ch, g0=g0, L=L: e.dma_start(
                            out=rxp[:, 2:2 + L], in_=RXs[ch * 128:(ch + 1) * 128, g0:g0 + L]), reads=[bRX],
                            writes=[b_rxp])
                        S.dma("sp", lambda e, ch=ch, g0=g0, L=L: e.dma_start(
                            out=GRt[:, 0:L], in_=GRs[ch * 128:(ch + 1) * 128, g0:g0 + L]), reads=[bGR],
                            writes=[b_GRt])
                        S.op("dve", lambda e, ch=ch, L=L: e.tensor_scalar(
                            out=xr[:, 0:L], in0=rxp[:, 0:L], scalar1=col("w4", ch * 4 + 0), scalar2=col("b4", ch),
                            op0=ALU.mult, op1=ALU.add), reads=[b_rxp, b_cols], writes=[b_xr])
                        for k in range(1, 4):
                            S.op("dve", lambda e, ch=ch, L=L, k=k: e.scalar_tensor_tensor(
                                out=xr[:, 0:L], in0=rxp[:, k:k + L], scalar=col("w4", ch * 4 + k), in1=xr[:, 0:L],
                                op0=ALU.mult, op1=ALU.add), reads=[b_rxp, b_cols, b_xr], writes=[b_xr])
                        S.op("act", lambda e, L=L: e.activation(out=xrb[:, 0:L], in_=xr[:, 0:L], func=AF.Copy),
                             reads=[b_xr], writes=[b_xrb])
                        pull(1.0 * L / 1100.0)
                        for d in range(2):
                            for tb in range(L // TB):
                                cs = slice(tb * TB, (tb + 1) * TB)
                                p1 = pbi % 4
                                p2 = (pbi + 1) % 4
                                pbi += 2
                                S.op("pe", lambda e, d=d, ch=ch, cs=cs, p1=p1, TB=TB: e.matmul(
                                    psb[p1][:, 0:TB], lhsT=GWb[:, d * 16 + ch, :], rhs=xrb[:, cs], start=True,
                                    stop=True), reads=[b_GWb, b_xrb], writes=[b_ps[p1]])
                                S.op("pe", lambda e, d=d, ch=ch, cs=cs, p2=p2, TB=TB: e.matmul(
                                    psb[p2][:, 0:TB], lhsT=GWb[:, d * 16 + 8 + ch, :], rhs=xrb[:, cs], start=True,
                                    stop=True), reads=[b_GWb, b_xrb], writes=[b_ps[p2]])
                                S.op("act", lambda e, d=d, ch=ch, cs=cs, p1=p1, TB=TB: e.activation(
                                    out=At[:, cs], in_=psb[p1][:, 0:TB], func=AF.Sigmoid, bias=col("ba", d * 8 + ch)),
                                    reads=[b_ps[p1], b_cols], writes=[b_A])
                                S.op("act", lambda e, d=d, ch=ch, cs=cs, p2=p2, TB=TB: e.activation(
                                    out=Bt[:, cs], in_=psb[p2][:, 0:TB], func=AF.Sigmoid, bias=col("bx", d * 8 + ch)),
                                    reads=[b_ps[p2], b_cols], writes=[b_B])
                                pull(2.0 * TB / 1100.0)
                            S.op("act", lambda e, d=d, ch=ch, L=L: e.activation(
                                out=Tm[:, 0:L], in_=At[:, 0:L], func=AF.Exp, scale=cdec[:, 16 + d * 8 + ch:17 + d * 8 + ch]),
                                reads=[b_A, b_cdec], writes=[b_T])
                            S.op("act", lambda e, L=L: e.activation(out=Tm[:, 0:L], in_=Tm[:, 0:L], func=AF.Sqrt,
                                                                    scale=-1.0, bias=1.0), reads=[b_T], writes=[b_T])
                            S.op("act", lambda e, d=d, ch=ch, L=L: e.activation(
                                out=At[:, 0:L], in_=At[:, 0:L], func=AF.Exp, scale=cdec[:, d * 8 + ch:d * 8 + ch + 1]),
                                reads=[b_A, b_cdec], writes=[b_A])
                            pull(3.0 * L / 1100.0)
                            S.op("pool", lambda e, L=L: e.tensor_tensor(out=Bt[:, 0:L], in0=Bt[:, 0:L], in1=xr[:, 0:L],
                                                                        op=ALU.mult), reads=[b_B, b_xr], writes=[b_B])
                            S.op("dve", lambda e, L=L: e.tensor_tensor(out=Bt[:, 0:L], in0=Bt[:, 0:L], in1=Tm[:, 0:L],
                                                                       op=ALU.mult), reads=[b_B, b_T], writes=[b_B])
                            if mrow == 1:
                                h0ap = col("h0", d * 8 + ch)
                                h0r = [b_cols]
                            else:
                                h0ap = zcol[:, 0:1]
                                h0r = [b_zcol]
                            if d == 0:
                                S.op("dve", lambda e, L=L, h0ap=h0ap: e.tensor_tensor_scan(
                                    out=HF[:, 0:L], data0=At[:, 0:L], data1=Bt[:, 0:L], initial=h0ap, op0=ALU.mult,
                                    op1=ALU.add), reads=[b_A, b_B] + h0r, writes=[b_HF])
                            else:
                                S.op("dve", lambda e, L=L, h0ap=h0ap: e.tensor_tensor_scan(
                                    out=cap(Tm[:], L - 1, [[-1, L]]), data0=cap(At[:], L - 1, [[-1, L]]),
                                    data1=cap(Bt[:], L - 1, [[-1, L]]), initial=h0ap, op0=ALU.mult, op1=ALU.add),
                                    reads=[b_A, b_B] + h0r, writes=[b_T])
                        if mrow == 0:
                            sidx = si - 1
                            S.op("act", lambda e, sidx=sidx, ch=ch, L=L: e.activation(
                                out=NSt[:, (sidx * 2 + 0) * 8 + ch:(sidx * 2 + 0) * 8 + ch + 1], in_=HF[:, L - 1:L],
                                func=AF.Copy), reads=[b_HF], writes=[b_NSt])
                            S.op("act", lambda e, sidx=sidx, ch=ch: e.activation(
                                out=NSt[:, (sidx * 2 + 1) * 8 + ch:(sidx * 2 + 1) * 8 + ch + 1], in_=Tm[:, 0:1],
                                func=AF.Copy), reads=[b_T], writes=[b_NSt])
                        S.op("dve", lambda e, L=L: e.tensor_tensor(out=HF[:, 0:L], in0=HF[:, 0:L], in1=Tm[:, 0:L],
                                                                   op=ALU.add), reads=[b_HF, b_T], writes=[b_HF])
                        S.op("pool", lambda e, L=L: e.tensor_tensor(out=yrb[:, 0:L], in0=HF[:, 0:L], in1=GRt[:, 0:L],
                                                                    op=ALU.mult), reads=[b_HF, b_GRt], writes=[b_yrb])
                        S.dma("sp", lambda e, ch=ch, g0=g0, L=L: e.dma_start(
                            out=CATR[ch * 128:(ch + 1) * 128, g0:g0 + L], in_=yrb[:, 0:L]), reads=[b_yrb],
                            writes=[bCATR])
                for q in range(4):
                    S.dma("sp", lambda e, q=q: e.dma_start(
                        out=ns_d[q, :].rearrange("(c p) -> p c", p=128), in_=NSt[:, q * 8:(q + 1) * 8]),
                        reads=[b_NSt], writes=[bNS])
                while True:
                    npull[0] += 1
                    if npull[0] % 6 == 0:
                        next(ugen, None)
                    try:
                        next(cgen)
                    except StopIteration:
                        break
                for _ in ugen:
                    pass
                S.barrier()

        cst.close()

        if stop_after >= 3 and os.environ.get("K_SKIP2B2", "0") != "1":
            with ExitStack() as ph:
                WO = sbt(ph, "WO", [128, 16, D], BF16)
                b_WO = Buf()
                catT = [sbt(ph, f"catT{i}", [128, 16, 512], BF16) for i in range(2)]
                b_cat = [Buf() for _ in range(2)]
                g1b = [sbt(ph, f"g1b{r}", [128, D]) for r in range(2)]
                b_g1b = [Buf() for _ in range(2)]
                lg = sbt(ph, "lg", [128, D])
                lb = sbt(ph, "lb", [128, D])
                b_lg = Buf()
                b_lb = Buf()
                tmpt = sbt(ph, "tmpt", [128, 128])
                b_tmpt = Buf()
                xts = [sbt(ph, f"xt{i}", [128, D]) for i in range(3)]
                b_xts = [Buf() for _ in range(3)]
                make_erb(ph, 3)
                ehs = {}
                x1t = [sbt(ph, f"x1t{i}", [128, D]) for i in range(2)]
                b_x1t = [Buf() for _ in range(2)]
                sts = [sbt(ph, f"st{i}", [128, 32]) for i in range(2)]
                b_sts = [Buf() for _ in range(2)]
                u2s = [sbt(ph, f"u2s{i}", [128, 16, 128], BF16) for i in range(2)]
                b_u2s = [Buf() for _ in range(2)]
                for kk in range(16):
                    S.dma("pool", lambda e, kk=kk: e.dma_start(out=WO[:, kk, :], in_=w_out[kk * 128:(kk + 1) * 128, :]),
                          writes=[b_WO])
                for r in range(2):
                    bcast_mod(g1b[r], b_g1b[r], 32, r, tmpt, b_tmpt)
                S.dma("sp", lambda e: e.dma_start(out=lg[:], in_=dram_ap(lnp, 0, [[0, 128], [1, D]])), writes=[b_lg])
                S.dma("sp", lambda e: e.dma_start(out=lb[:], in_=dram_ap(lnp, D, [[0, 128], [1, D]])), writes=[b_lb])
                tiles2 = []
                for bi, (g0, L, mrow, s, T) in enumerate(blocks):
                    for ti in range(T // 128):
                        tiles2.append((bi, ti))

                def bufs(idx):
                    k = idx % 2
                    k3 = idx % 3
                    return (xts[k3], b_xts[k3], x1t[k], b_x1t[k], sts[k], b_sts[k], u2s[k], b_u2s[k])

                def stA(idx):
                    bi, ti = tiles2[idx]
                    g0, L, mrow, s, T = blocks[bi]
                    cT, bcT = catT[bi % 2], b_cat[bi % 2]
                    xt, bxt, x1, bx1, st, bst, u2, b_u2 = bufs(idx)
                    if ti == 0:
                        S.dma("sp", lambda e: e.dma_start(
                            out=cT[:, 0:8, 0:T], in_=CATC[:, g0 + s:g0 + s + T].rearrange("(c p) t -> p c t", p=128)),
                            reads=[bCATC], writes=[bcT])
                        S.dma("sp", lambda e: e.dma_start(
                            out=cT[:, 8:16, 0:T], in_=CATR[:, g0 + s:g0 + s + T].rearrange("(c p) t -> p c t", p=128)),
                            reads=[bCATR], writes=[bcT])
                    load_x_add(xt, bxt, ehs.pop(idx))
                    for nb in range(4):
                        pb = nb
                        for kk in range(16):
                            S.op("pe", lambda e, kk=kk, nb=nb, pb=pb: e.matmul(
                                psb[pb][:], lhsT=cT[:, kk, ti * 128:(ti + 1) * 128],
                                rhs=WO[:, kk, nb * 512:(nb + 1) * 512], start=(kk == 0), stop=(kk == 15)),
                                reads=[bcT, b_WO], writes=[b_ps[pb]])

                def stA0(idx):
                    bi, ti = tiles2[idx]
                    g0, L, mrow, s, T = blocks[bi]
                    xt, bxt = bufs(idx)[0], bufs(idx)[1]
                    tok0 = g0 + s + ti * 128
                    ehs[idx] = load_x_dma(xt, bxt, tok0, mrow, (s + ti * 128) // 128)

                def stC(idx):
                    bi, ti = tiles2[idx]
                    g0, L, mrow, s, T = blocks[bi]
                    xt, bxt, x1, bx1, st, bst, u2, b_u2 = bufs(idx)
                    for nb in range(4):
                        cs = slice(nb * 512, (nb + 1) * 512)
                        S.op("dve", lambda e, nb=nb, cs=cs: e.tensor_tensor(
                            out=x1[:, cs], in0=psb[nb][:], in1=g1b[mrow][:, cs], op=ALU.mult),
                            reads=[b_ps[nb], b_g1b[mrow]], writes=[bx1])
                    S.op("dve", lambda e: e.scalar_tensor_tensor(
                        out=x1[:], in0=xt[:], scalar=ALPHA, in1=x1[:], op0=ALU.mult, op1=ALU.add),
                        reads=[bxt, bx1], writes=[bx1])

                def stB(idx):
                    bi, ti = tiles2[idx]
                    g0, L, mrow, s, T = blocks[bi]
                    xt, bxt, x1, bx1, st, bst, u2, b_u2 = bufs(idx)
                    tok0 = g0 + s + ti * 128
                    ln_stats(x1, bx1, st, bst)
                    S.op("dve", lambda e: e.tensor_scalar(
                        out=x1[:], in0=x1[:], scalar1=st[:, 0:1], scalar2=st[:, 2:3], op0=ALU.subtract,
                        op1=ALU.mult), reads=[bx1, bst], writes=[bx1])
                    S.op("dve", lambda e: e.tensor_tensor(out=x1[:], in0=x1[:], in1=lg[:], op=ALU.mult),
                         reads=[bx1, b_lg], writes=[bx1])
                    S.op("dve", lambda e: e.tensor_tensor(out=x1[:], in0=x1[:], in1=lb[:], op=ALU.add),
                         reads=[bx1, b_lb], writes=[bx1])
                    S.dma("sp", lambda e: e.dma_start(out=X1s[tok0:tok0 + 128, :], in_=x1[:]),
                          reads=[bx1], writes=[bX1])
                    ln_stats(x1, bx1, st, bst)
                    S.op("dve", lambda e: e.tensor_scalar(
                        out=xt[:], in0=x1[:], scalar1=st[:, 0:1], scalar2=st[:, 2:3], op0=ALU.subtract,
                        op1=ALU.mult), reads=[bx1, bst], writes=[bxt])
                    transpose_mod(xt, bxt, u2, b_u2, 0, 64, 48, mrow)
                    S.dma("sp", lambda e: e.dma_start(
                        out=U2T[:, tok0:tok0 + 128].rearrange("(k p) t -> p k t", p=128), in_=u2[:]),
                        reads=[b_u2], writes=[bU2T])

                N2 = len(tiles2)
                stA0(0)
                for n in range(-1, N2):
                    if n + 2 < N2:
                        stA0(n + 2)
                    if n + 1 < N2:
                        stA(n + 1)
                    if n >= 0:
                        stB(n)
                    if n + 1 < N2:
                        stC(n + 1)
                S.barrier()

        if stop_after >= 4:
            with ExitStack() as ph:
                u2 = sbt(ph, "u2p", [128, 16, 512], BF16)
                b_u2 = Buf()
                qTs = [sbt(ph, f"qT{i}", [128, 16, 512], BF16) for i in range(2)]
                b_qTs = [Buf() for _ in range(2)]
                Wq = [sbt(ph, f"Wq{i}", [128, 16, 128], BF16) for i in range(3)]
                b_Wq = [Buf() for _ in range(3)]
                SKT = sbt(ph, "SKT", [128, 16, 128], BF16)
                b_SKT = Buf()
                Sxs = [sbt(ph, f"Sx{i}", [128, 16, 128]) for i in range(2)]
                b_Sxs = [Buf() for _ in range(2)]
                S2 = sbt(ph, "S2", [128, 16, 128])
                b_S2 = Buf()
                sv = sbt(ph, "sv", [128, 16, 16])
                b_sv = Buf()
                sv2 = sbt(ph, "sv2", [128, 16, 8])
                b_sv2 = Buf()
                siu = sbt(ph, "siu", [128, 16, 16], U32)
                b_siu = Buf()
                sif = sbt(ph, "sif", [128, 16, 16])
                b_sif = Buf()
                cand = sbt(ph, "cand", [128, 8, 256])
                b_cand = Buf()
                cand2 = sbt(ph, "cand2", [128, 8, 256])
                b_cand2 = Buf()
                tops = sbt(ph, "tops", [128, 8, 16])
                b_tops = Buf()
                tops2 = sbt(ph, "tops2", [128, 8, 8])
                b_tops2 = Buf()
                posu = sbt(ph, "posu", [128, 8, 16], U32)
                b_posu = Buf()
                posf = sbt(ph, "posf", [128, 128])
                b_posf = Buf()
                thr = sbt(ph, "thr", [128, 16])
                b_thr = Buf()
                kf = sbt(ph, "kf", [128, 2, 128])
                b_kf = Buf()
                eq = sbt(ph, "eq", [128, 128, 16])
                b_eq = Buf()
                ijg = sbt(ph, "ijg", [128, 3, 128])
                b_ijg = Buf()
                zz = sbt(ph, "zz", [128, 16])
                b_zz = Buf()
                negIJG = sbt(ph, "negIJG", [128, 3, 128])
                b_negJ = Buf()
                ijgTs = [sbt(ph, f"ijgT{i}", [128, 3, 128]) for i in range(2)]
                b_ijgTs = [Buf() for _ in range(2)]
                iot = sbt(ph, "iotp", [128, 128])
                b_iot = Buf()
                OHI = [sbt(ph, f"OHI{i}", [128, 32, 128], BF16) for i in range(2)]
                b_OHI = [Buf() for _ in range(2)]
                OHJ = [sbt(ph, f"OHJ{i}", [128, 32, 128], BF16) for i in range(2)]
                b_OHJ = [Buf() for _ in range(2)]
                b_OHI_a = [Buf() for _ in range(2)]
                b_OHJ_a = [Buf() for _ in range(2)]
                ohc = 0
                Gt = sbt(ph, "Gt", [128, 128, 128], BF16)
                b_Gt = Buf()
                S.dma("sp", lambda e: e.dma_start(out=iot[:], in_=iota_d[:, :]), writes=[b_iot])
                iotb = sbt(ph, "iotb", [128, 128], BF16)
                b_iotb = Buf()
                S.op("act", lambda e: e.activation(out=iotb[:], in_=iot[:], func=AF.Copy), reads=[b_iot],
                     writes=[b_iotb])
                S.op("dve", lambda e: e.tensor_scalar(out=thr[:], in0=iot[:, 0:16], scalar1=1.0, scalar2=16.0,
                                                      op0=ALU.add, op1=ALU.mult), reads=[b_iot], writes=[b_thr])
                S.dma("pool", lambda e: e.dma_start(out=SKT[:].rearrange("p a b -> p (a b)"), in_=skt_d[:, :]),
                      writes=[b_SKT])
                ohc_ = [0]
                gp_ = [0]
                tcnt = [0]

                def phaseB(tok0, ijgT, b_ijgT, gen=None):
                    S.op("act", lambda e: e.activation(out=negIJG[:].rearrange("p a b -> p (a b)"),
                                                       in_=ijgT[:].rearrange("p a b -> p (a b)"), func=AF.Copy,
                                                       scale=-1.0), reads=[b_ijgT], writes=[b_negJ])
                    for qt in range(4):
                        c0 = qt * 32
                        ohi, bohi = OHI[ohc_[0] % 2], b_OHI[ohc_[0] % 2]
                        ohj, bohj = OHJ[ohc_[0] % 2], b_OHJ[ohc_[0] % 2]
                        bohi_a, bohj_a = b_OHI_a[ohc_[0] % 2], b_OHJ_a[ohc_[0] % 2]
                        ohc_[0] += 1
                        NACT = 8
                        for cl_ in range(32 - NACT, 32):
                            S.op("act", lambda e, c0=c0, ohi=ohi, cl_=cl_: e.activation(
                                out=ohi[:, cl_, :], in_=iot[:], func=AF.Abs,
                                bias=negIJG[:, 0, c0 + cl_:c0 + cl_ + 1], scale=1.0),
                                reads=[b_iot, b_negJ], writes=[bohi_a])
                            S.op("act", lambda e, c0=c0, ohi=ohi, cl_=cl_: e.activation(
                                out=ohi[:, cl_, :], in_=ohi[:, cl_, :], func=AF.Relu, scale=-1.0, bias=1.0),
                                reads=[bohi_a], writes=[bohi_a])
                            S.op("act", lambda e, c0=c0, ohj=ohj, cl_=cl_: e.activation(
                                out=ohj[:, cl_, :], in_=iot[:], func=AF.Abs,
                                bias=negIJG[:, 1, c0 + cl_:c0 + cl_ + 1], scale=1.0),
                                reads=[b_iot, b_negJ], writes=[bohj_a])
                            S.op("act", lambda e, c0=c0, ohj=ohj, cl_=cl_: e.activation(
                                out=ohj[:, cl_, :], in_=ohj[:, cl_, :], func=AF.Relu,
                                scale=negIJG[:, 2, c0 + cl_:c0 + cl_ + 1], bias=ijgT[:, 2, c0 + cl_:c0 + cl_ + 1]),
                                reads=[bohj_a, b_negJ, b_ijgT], writes=[bohj_a])
                        for cl_ in range(32 - NACT):
                            S.op("dve", lambda e, c0=c0, ohi=ohi, cl_=cl_: e.tensor_scalar(
                                out=ohi[:, cl_, :], in0=iotb[:], scalar1=ijgT[:, 0, c0 + cl_:c0 + cl_ + 1],
                                scalar2=None, op0=ALU.is_equal), reads=[b_iotb, b_ijgT], writes=[bohi])
                            S.op("dve", lambda e, c0=c0, ohj=ohj, cl_=cl_: e.tensor_scalar(
                                out=ohj[:, cl_, :], in0=iotb[:], scalar1=ijgT[:, 1, c0 + cl_:c0 + cl_ + 1],
                                scalar2=ijgT[:, 2, c0 + cl_:c0 + cl_ + 1], op0=ALU.is_equal, op1=ALU.mult),
                                reads=[b_iotb, b_ijgT], writes=[bohj])
                        for c4 in range(8):
                            pb = 5 + (gp_[0] % 3)
                            gp_[0] += 1
                            for j in range(4):
                                cl = c4 * 4 + j
                                S.op("pe", lambda e, cl=cl, j=j, pb=pb, ohi=ohi, ohj=ohj: e.matmul(
                                    psb[pb][:, j * 128:(j + 1) * 128], lhsT=ohj[:, cl, :], rhs=ohi[:, cl, :],
                                    start=True, stop=True),
                                    reads=([bohi_a, bohj_a] if cl >= 24 else [bohi, bohj]), writes=[b_ps[pb]])
                            cg = c0 + c4 * 4
                            S.op("act", lambda e, cg=cg, pb=pb: e.activation(
                                out=cap(Gt[:], cg, [[128, 128], [1, 4]]),
                                in_=cap(psb[pb][:], 0, [[1, 128], [128, 4]]), func=AF.Copy),
                                reads=[b_ps[pb]], writes=[b_Gt])
                        if gen is not None:
                            next(gen, None)
                    for q in range(8):
                        S.dma("sp", lambda e, q=q, tok0=tok0: e.dma_start(
                            out=Gs[q * 16:(q + 1) * 16, :, tok0:tok0 + 128].rearrange("i j c -> j i c"),
                            in_=Gt[:, q * 16:(q + 1) * 16, :]), reads=[b_Gt], writes=[bG])

                pendingB = [None]
                wqi = 0
                gp = 0
                tiles3 = [(b0_, t_) for b0_ in range(0, NT, 512) for t_ in range(4)]
                wqi_ = [0]

                def qproj(bI, hps):
                    blk0 = bI * 512
                    qT, b_qT = qTs[bI % 2], b_qTs[bI % 2]
                    if hps[0] == 0:
                        S.dma("sp", lambda e: e.dma_start(
                            out=u2[:], in_=U2T[:, blk0:blk0 + 512].rearrange("(k p) t -> p k t", p=128)),
                            reads=[bU2T], writes=[b_u2])
                    for hp in hps:
                        wq, bwq = Wq[wqi_[0] % 3], b_Wq[wqi_[0] % 3]
                        wqi_[0] += 1
                        S.dma("pool", lambda e, wq=wq, hp=hp: e.dma_start(
                            out=wq[:].rearrange("p a b -> p (a b)"), in_=wqh[hp, :, :]), writes=[bwq])
                        pb = hp % 2
                        for kk in range(16):
                            S.op("pe", lambda e, wq=wq, kk=kk, pb=pb: e.matmul(
                                psb[pb][:], lhsT=wq[:, kk, :], rhs=u2[:, kk, :], start=(kk == 0), stop=(kk == 15)),
                                reads=[bwq, b_u2], writes=[b_ps[pb]])
                        S.op("act", lambda e, hp=hp, pb=pb: e.activation(out=qT[:, hp, :], in_=psb[pb][:],
                                                                         func=AF.Copy), reads=[b_ps[pb]],
                             writes=[b_qT])

                def A1(n):
                    blk0, ti = tiles3[n]
                    Sx, b_Sx = Sxs[n % 2], b_Sxs[n % 2]
                    qT, b_qT = qTs[(n // 4) % 2], b_qTs[(n // 4) % 2]
                    if n == 0:
                        qproj(0, list(range(16)))
                    cs = slice(ti * 128, (ti + 1) * 128)
                    for q in range(4):
                        pb = 2 + (q % 2)
                        for j in range(4):
                            hp = q * 4 + j
                            S.op("pe", lambda e, hp=hp, j=j, pb=pb, cs=cs: e.matmul(
                                psb[pb][:, j * 128:(j + 1) * 128], lhsT=qT[:, hp, cs], rhs=SKT[:, hp, :],
                                start=True, stop=True), reads=[b_qT, b_SKT], writes=[b_ps[pb]])
                        S.op("act", lambda e, q=q, pb=pb: e.activation(
                            out=Sx[:, q * 4:(q + 1) * 4, :].rearrange("p a b -> p (a b)"), in_=psb[pb][:],
                            func=AF.Copy), reads=[b_ps[pb]], writes=[b_Sx])

                def A2(n):
                    blk0, ti = tiles3[n]
                    Sx, b_Sx = Sxs[n % 2], b_Sxs[n % 2]
                    ijgT, b_ijgT = ijgTs[n % 2], b_ijgTs[n % 2]
                    for hp in range(16):
                        S.op("dve", lambda e, hp=hp: e.max(out=sv[:, hp, 0:8], in_=Sx[:, hp, :]), reads=[b_Sx],
                             writes=[b_sv])
                    for hp in range(16):
                        S.op("dve", lambda e, hp=hp: e.max_index(out=siu[:, hp, 0:8], in_max=sv[:, hp, 0:8],
                                                                 in_values=Sx[:, hp, :]), reads=[b_Sx, b_sv],
                             writes=[b_siu])
                    for hp in range(16):
                        S.op("dve", lambda e, hp=hp: e.match_replace(
                            out=S2[:, hp, :], in_to_replace=sv[:, hp, 0:8], in_values=Sx[:, hp, :],
                            imm_value=-1e30), reads=[b_Sx, b_sv], writes=[b_S2])
                    yield
                    for hp in range(16):
                        S.op("dve", lambda e, hp=hp: e.max(out=sv2[:, hp, 0:8], in_=S2[:, hp, :]), reads=[b_S2],
                             writes=[b_sv2])
                    for hp in range(16):
                        S.op("dve", lambda e, hp=hp: e.max_index(out=siu[:, hp, 8:16], in_max=sv2[:, hp, 0:8],
                                                                 in_values=S2[:, hp, :]), reads=[b_S2, b_sv2],
                             writes=[b_siu])
                    S.op("dve", lambda e: e.tensor_copy(out=sv[:, :, 8:16], in_=sv2[:, :, 0:8]), reads=[b_sv2],
                         writes=[b_sv])
                    S.op("dve", lambda e: e.tensor_copy(out=sif[:], in_=siu[:]), reads=[b_siu], writes=[b_sif])
                    S.op("dve", lambda e: e.tensor_tensor(
                        out=cap(cand[:], 0, [[256, 8], [16, 16], [1, 16]]),
                        in0=cap(sv[:], 0, [[32, 8], [1, 16], [0, 16]]),
                        in1=cap(sv[:], 16, [[32, 8], [0, 16], [1, 16]]), op=ALU.add), reads=[b_sv],
                        writes=[b_cand])
                    yield
                    for h in range(8):
                        S.op("dve", lambda e, h=h: e.max(out=tops[:, h, 0:8], in_=cand[:, h, :]), reads=[b_cand],
                             writes=[b_tops])
                    for h in range(8):
                        S.op("dve", lambda e, h=h: e.max_index(out=posu[:, h, 0:8], in_max=tops[:, h, 0:8],
                                                               in_values=cand[:, h, :]), reads=[b_cand, b_tops],
                             writes=[b_posu])
                    for h in range(8):
                        S.op("dve", lambda e, h=h: e.match_replace(
                            out=cand2[:, h, :], in_to_replace=tops[:, h, 0:8], in_values=cand[:, h, :],
                            imm_value=-1e30), reads=[b_cand, b_tops], writes=[b_cand2])
                    for h in range(8):
                        S.op("dve", lambda e, h=h: e.max(out=tops2[:, h, 0:8], in_=cand2[:, h, :]),
                             reads=[b_cand2], writes=[b_tops2])
                    for h in range(8):
                        S.op("dve", lambda e, h=h: e.max_index(out=posu[:, h, 8:16], in_max=tops2[:, h, 0:8],
                                                               in_values=cand2[:, h, :]),
                             reads=[b_cand2, b_tops2], writes=[b_posu])
                    S.op("dve", lambda e: e.tensor_copy(out=tops[:, :, 8:16], in_=tops2[:, :, 0:8]),
                         reads=[b_tops2], writes=[b_tops])
                    yield
                    S.op("dve", lambda e: e.tensor_copy(out=posf[:], in_=posu[:].rearrange("p a b -> p (a b)")),
                         reads=[b_posu], writes=[b_posf])
                    S.op("dve", lambda e: e.tensor_tensor(
                        out=eq[:], in0=cap(posf[:], 0, [[1, 128], [0, 16]]),
                        in1=cap(thr[:], 0, [[0, 128], [1, 16]]), op=ALU.is_ge), reads=[b_posf, b_thr],
                        writes=[b_eq])
                    S.op("dve", lambda e: e.tensor_reduce(out=kf[:, 0, :], in_=eq[:], axis=AX.X, op=ALU.add),
                         reads=[b_eq], writes=[b_kf])
                    S.op("dve", lambda e: e.scalar_tensor_tensor(
                        out=kf[:, 1, :], in0=kf[:, 0, :], scalar=-16.0, in1=posf[:], op0=ALU.mult, op1=ALU.add),
                        reads=[b_kf, b_posf], writes=[b_kf])
                    for w in range(2):
                        S.op("dve", lambda e, w=w: e.tensor_tensor(
                            out=eq[:], in0=cap(kf[:], w * 128, [[1, 128], [0, 16]]),
                            in1=cap(iot[:], 0, [[0, 128], [1, 16]]), op=ALU.is_equal), reads=[b_kf, b_iot],
                            writes=[b_eq])
                        S.op("dve", lambda e, w=w: e.tensor_tensor(
                            out=cap(eq[:], 0, [[256, 8], [16, 16], [1, 16]]),
                            in0=cap(eq[:], 0, [[256, 8], [16, 16], [1, 16]]),
                            in1=cap(sif[:], w * 16, [[32, 8], [0, 16], [1, 16]]), op=ALU.mult),
                            reads=[b_eq, b_sif], writes=[b_eq])
                        S.op("dve", lambda e, w=w: e.tensor_reduce(out=ijg[:, w, :], in_=eq[:], axis=AX.X,
                                                                   op=ALU.add), reads=[b_eq], writes=[b_ijg])
                    S.op("dve", lambda e: e.tensor_tensor(
                        out=cap(cand2[:], 0, [[16, 8], [1, 16]]), in0=cap(tops[:], 0, [[16, 8], [1, 16]]),
                        in1=cap(tops[:], 0, [[16, 8], [0, 16]]), op=ALU.subtract), reads=[b_tops],
                        writes=[b_cand2])
                    S.op("act", lambda e: e.activation(out=cand2[:, 0, 128:256], in_=cand2[:, 0, 0:128],
                                                       func=AF.Exp), reads=[b_cand2], writes=[b_cand2])
                    S.op("dve", lambda e: e.tensor_reduce(
                        out=zz[:, 0:8], in_=cap(cand2[:], 128, [[16, 8], [1, 16]]), axis=AX.X, op=ALU.add),
                        reads=[b_cand2], writes=[b_zz])
                    S.op("dve", lambda e: e.reciprocal(out=zz[:, 8:16], in_=zz[:, 0:8]), reads=[b_zz],
                         writes=[b_zz])
                    S.op("dve", lambda e: e.tensor_tensor(
                        out=cap(ijg[:], 256, [[16, 8], [1, 16]]), in0=cap(cand2[:], 128, [[16, 8], [1, 16]]),
                        in1=cap(zz[:], 8, [[1, 8], [0, 16]]), op=ALU.mult), reads=[b_cand2, b_zz],
                        writes=[b_ijg])
                    for w in range(3):
                        S.op("pe", lambda e, w=w: e.transpose(out=psb[4][:, w * 128:(w + 1) * 128],
                                                              in_=ijg[:, w, :], identity=ident[:]),
                             reads=[b_ijg, b_ident], writes=[b_ps[4]])
                    S.op("act", lambda e: e.activation(out=ijgT[:].rearrange("p a b -> p (a b)"),
                                                       in_=psb[4][:, 0:384], func=AF.Copy), reads=[b_ps[4]],
                         writes=[b_ijgT])

                NT3 = len(tiles3)
                for n in range(-1, NT3 + 1):
                    if 0 <= n + 1 < NT3:
                        A1(n + 1)
                    gen = A2(n) if 0 <= n < NT3 else None
                    if 0 <= n - 1 < NT3:
                        phaseB(tiles3[n - 1][0] + tiles3[n - 1][1] * 128, ijgTs[(n - 1) % 2], b_ijgTs[(n - 1) % 2], gen)
                    if gen is not None:
                        for _ in gen:
                            pass
                    m_ = n + 1
                    if 0 <= m_ and (m_ // 4 + 1) * 4 < NT3:
                        qproj(m_ // 4 + 1, [4 * (m_ % 4) + x_ for x_ in range(4)])
                S.barrier()

        if stop_after >= 5:
            with ExitStack() as ph:
                TG = 512
                EG = 4
                PF = 3
                u2s4 = [sbt(ph, f"u2q{i}", [128, 16, TG], BF16) for i in range(2)]
                b_u2s4 = [Buf() for _ in range(2)]
                ACC = sbt(ph, "ACC", [128, TG // 128, D])
                b_ACC = [Buf() for _ in range(TG // 128)]
                NU = 5
                Ub = [sbt(ph, f"Ub{i}", [128, 16, 128], BF16) for i in range(NU)]
                b_Ub = [Buf() for _ in range(NU)]
                NV = 8
                Vb = [sbt(ph, f"Vb{i}", [128, D], BF16) for i in range(NV)]
                b_Vb = [Buf() for _ in range(NV)]
                Gb = [sbt(ph, f"Gb{i}", [128, TG], BF16) for i in range(NU)]
                b_Gb = [Buf() for _ in range(NU)]
                Ab = [sbt(ph, f"Ab{i}", [128, TG], BF16) for i in range(2)]
                b_Ab = [Buf() for _ in range(2)]
                Wt = [sbt(ph, f"Wt{i}", [128, EG, TG], BF16) for i in range(2)]
                b_Wt = [Buf() for _ in range(2)]
                g2b = [sbt(ph, f"g2b{r}", [128, D]) for r in range(2)]
                b_g2b = [Buf() for _ in range(2)]
                lg = sbt(ph, "lg2", [128, D])
                lb = sbt(ph, "lb2", [128, D])
                b_lg = Buf()
                b_lb = Buf()
                tmpt = sbt(ph, "tmpt2", [128, 128])
                b_tmpt = Buf()
                x1t = [sbt(ph, f"x1q{i}", [128, D]) for i in range(4)]
                b_x1t = [Buf() for _ in range(4)]
                sts = [sbt(ph, f"stq{i}", [128, 32]) for i in range(4)]
                b_sts = [Buf() for _ in range(4)]
                for r in range(2):
                    bcast_mod(g2b[r], b_g2b[r], 80, r, tmpt, b_tmpt)
                S.dma("sp", lambda e: e.dma_start(out=lg[:], in_=dram_ap(lnp, 2 * D, [[0, 128], [1, D]])),
                      writes=[b_lg])
                S.dma("sp", lambda e: e.dma_start(out=lb[:], in_=dram_ap(lnp, 3 * D, [[0, 128], [1, D]])),
                      writes=[b_lb])
                ei_ = [0]
                pend_epi = []

                def epilogue_tile(g0, t, mrow):
                    tok0 = g0 + t * 128
                    x1, bx1 = x1t[ei_[0] % 4], b_x1t[ei_[0] % 4]
                    st, bst = sts[ei_[0] % 4], b_sts[ei_[0] % 4]
                    ei_[0] += 1
                    S.dma("sp", lambda e: e.dma_start(out=x1[:], in_=X1s[tok0:tok0 + 128, :]), reads=[bX1],
                          writes=[bx1])
                    S.op("dve", lambda e: e.tensor_tensor(out=ACC[:, t, :], in0=ACC[:, t, :], in1=g2b[mrow][:],
                                                          op=ALU.mult), reads=[b_ACC[t], b_g2b[mrow]],
                         writes=[b_ACC[t]])
                    S.op("dve", lambda e: e.scalar_tensor_tensor(
                        out=x1[:], in0=x1[:], scalar=ALPHA, in1=ACC[:, t, :], op0=ALU.mult, op1=ALU.add),
                        reads=[bx1, b_ACC[t]], writes=[bx1])
                    yield
                    for q in range(4):
                        S.op("dve", lambda e, q=q: e.bn_stats(out=st[:, 8 + q * 6:8 + (q + 1) * 6],
                                                              in_=x1[:, q * 512:(q + 1) * 512]),
                             reads=[bx1], writes=[bst])
                        yield
                    S.op("dve", lambda e: e.bn_aggr(out=st[:, 0:2], in_=st[:, 8:32]), reads=[bst], writes=[bst])
                    S.op("act", lambda e: e.activation(out=st[:, 2:3], in_=st[:, 1:2], func=AF.Sqrt, bias=EPS, scale=1.0),
                         reads=[bst], writes=[bst])
                    S.op("dve", lambda e: e.reciprocal(out=st[:, 2:3], in_=st[:, 2:3]), reads=[bst], writes=[bst])
                    yield
                    S.op("dve", lambda e: e.tensor_scalar(
                        out=x1[:], in0=x1[:], scalar1=st[:, 0:1], scalar2=st[:, 2:3], op0=ALU.subtract,
                        op1=ALU.mult), reads=[bx1, bst], writes=[bx1])
                    yield
                    S.op("dve", lambda e: e.tensor_tensor(out=x1[:], in0=x1[:], in1=lg[:], op=ALU.mult),
                         reads=[bx1, b_lg], writes=[bx1])
                    yield
                    S.op("dve", lambda e: e.tensor_tensor(out=x1[:], in0=x1[:], in1=lb[:], op=ALU.add),
                         reads=[bx1, b_lb], writes=[bx1])
                    S.dma("sp", lambda e: e.dma_start(out=y_all[tok0:tok0 + 128, :], in_=x1[:]), reads=[bx1],
                          writes=[bY])

                epi_run = []
                ob_ = [0]

                def epi_step():
                    while epi_run:
                        try:
                            next(epi_run[0])
                            return
                        except StopIteration:
                            epi_run.pop(0)

                groups4 = list(range(0, NT, TG))
                NG = len(groups4)
                ntile = TG // 128
                total4 = NG * 128

                def load_u2(gi):
                    g0 = groups4[gi]
                    u2_, bu2_ = u2s4[gi % 2], b_u2s4[gi % 2]
                    S.dma("sp", lambda e: e.dma_start(
                        out=u2_[:], in_=U2T[:, g0:g0 + TG].rearrange("(k p) t -> p k t", p=128)), reads=[bU2T],
                        writes=[bu2_])

                def vstage(gj, grp):
                    wt, bwt = Wt[grp % 2], b_Wt[grp % 2]
                    for t in range(ntile):
                        for nb in range(4):
                            pb = 2 + (ob_[0] % 6)
                            ob_[0] += 1
                            for e_ in range(EG):
                                fj_ = gj * 128 + grp * EG + e_
                                vb, bvb = Vb[fj_ % NV], b_Vb[fj_ % NV]
                                S.op("pe", lambda e, e_=e_, vb=vb: e.matmul(
                                    psb[pb][:], lhsT=wt[:, e_, t * 128:(t + 1) * 128],
                                    rhs=vb[:, nb * 512:(nb + 1) * 512], start=(e_ == 0), stop=(e_ == EG - 1)),
                                    reads=[bwt, bvb], writes=[b_ps[pb]])
                            if grp == 0:
                                S.op("dve", lambda e: e.tensor_copy(
                                    out=ACC[:, t, nb * 512:(nb + 1) * 512], in_=psb[pb][:]),
                                    reads=[b_ps[pb]], writes=[b_ACC[t]])
                            else:
                                S.op("dve", lambda e: e.tensor_tensor(
                                    out=ACC[:, t, nb * 512:(nb + 1) * 512], in0=psb[pb][:],
                                    in1=ACC[:, t, nb * 512:(nb + 1) * 512], op=ALU.add),
                                    reads=[b_ps[pb], b_ACC[t]], writes=[b_ACC[t]])

                load_u2(0)
                for f in range(total4 + PF):
                    if f < total4:
                        gi, i = divmod(f, 128)
                        g0 = groups4[gi]
                        if i == 100 and gi + 1 < NG:
                            load_u2(gi + 1)
                        ub, bub = Ub[f % NU], b_Ub[f % NU]
                        vb, bvb = Vb[f % NV], b_Vb[f % NV]
                        gb, bgb = Gb[f % NU], b_Gb[f % NU]
                        S.dma("sp", lambda e: e.dma_start(
                            out=ub[:].rearrange("p a b -> p (a b)"), in_=UTB[i, :, :]), reads=[bUTB], writes=[bub])
                        S.dma("sp", lambda e: e.dma_start(
                            out=vb[:], in_=PVB[i * 128:(i + 1) * 128, :]), reads=[bPVB], writes=[bvb])
                        S.dma("sp", lambda e: e.dma_start(
                            out=gb[:], in_=Gs[i, :, g0:g0 + TG]), reads=[bG], writes=[bgb])
                    fj = f - PF
                    if fj < 0:
                        continue
                    gj, j = divmod(fj, 128)
                    u2, b_u2 = u2s4[gj % 2], b_u2s4[gj % 2]
                    if pend_epi and j < 4:
                        g_ = epilogue_tile(*pend_epi.pop(0))
                        next(g_)
                        epi_run.append(g_)
                        if j == 3:
                            while pend_epi:
                                g_ = epilogue_tile(*pend_epi.pop(0))
                                next(g_)
                                epi_run.append(g_)
                    elif j % 2 == 0:
                        epi_step()
                    ub, bub = Ub[fj % NU], b_Ub[fj % NU]
                    gb, bgb = Gb[fj % NU], b_Gb[fj % NU]
                    pa = fj % 2
                    for kk in range(16):
                        S.op("pe", lambda e, kk=kk: e.matmul(
                            psb[pa][:], lhsT=ub[:, kk, :], rhs=u2[:, kk, :], start=(kk == 0), stop=(kk == 15)),
                            reads=[bub, b_u2], writes=[b_ps[pa]])
                    ab, bab = Ab[fj % 2], b_Ab[fj % 2]
                    S.op("act", lambda e: e.activation(out=ab[:], in_=psb[pa][:], func=AF.Gelu_apprx_tanh),
                         reads=[b_ps[pa]], writes=[bab])
                    grp = j // EG
                    wt, bwt = Wt[grp % 2], b_Wt[grp % 2]
                    S.op("dve", lambda e: e.tensor_tensor(
                        out=wt[:, j % EG, :], in0=ab[:], in1=gb[:], op=ALU.mult), reads=[bab, bgb], writes=[bwt])
                    if j >= 1 and (j - 1) % EG == EG - 1:
                        vstage(gj, (j - 1) // EG)
                    if j == 127:
                        vstage(gj, 127 // EG)
                        g0j = groups4[gj]
                        pend_epi = [(g0j, t, 1 if g0j < 4096 else 0) for t in range(ntile)]
                for (eg0, et, emrow) in pend_epi:
                    g_ = epilogue_tile(eg0, et, emrow)
                    next(g_)
                    epi_run.append(g_)
                while epi_run:
                    epi_step()
                S.barrier()
        S.barrier()
    return nc


def _host_layout(inp, core):
    f = np.float32
    g = {}
    x_all = np.concatenate([inp["x_sample"][core], inp["x_prompt"][2 * core], inp["x_prompt"][2 * core + 1]], axis=0)
    g["x_all"] = np.ascontiguousarray(x_all, dtype=f)
    cols = np.zeros((128, NCOL), f)

    def put(name, arr):
        o, w = COLS[name]
        assert arr.shape == (128, w), (name, arr.shape)
        cols[:, o:o + w] = arr

    cv = np.stack([inp["c_ctx"], inp["c"][core]], axis=0)
    put("cvT", cv.reshape(2, 16, 128).transpose(2, 1, 0).reshape(128, 32))
    put("badaT", inp["b_ada"][0].reshape(96, 128).T)
    put("conv_w", inp["conv_w"][0].reshape(31, 8, 128).transpose(2, 1, 0).reshape(128, 248))
    put("conv_b", inp["conv_b"][0].reshape(8, 128).T)
    put("cln_g", inp["conv_ln_g"][0].reshape(8, 128).T)
    put("cln_b", inp["conv_ln_b"][0].reshape(8, 128).T)
    put("w4", inp["lru_conv_w"][0].reshape(4, 8, 128).transpose(2, 1, 0).reshape(128, 32))
    put("b4", inp["lru_conv_b"][0].reshape(8, 128).T)
    put("ba", inp["lru_ba"][0].reshape(2, 8, 128).transpose(2, 0, 1).reshape(128, 16))
    put("bx", inp["lru_bx"][0].reshape(2, 8, 128).transpose(2, 0, 1).reshape(128, 16))
    put("lam", inp["lru_lam"][0].reshape(2, 8, 128).transpose(2, 0, 1).reshape(128, 16))
    put("h0", inp["state_rglru"][core, 0].reshape(2, 8, 128).transpose(2, 0, 1).reshape(128, 16))
    g["cols"] = cols
    return g


_SHARED = {}


def _shared_layout(inp):
    f = np.float32
    s = {}
    s["w_ada"] = np.ascontiguousarray(inp["w_ada"][0], dtype=f)
    w_in = inp["w_in"][0]
    s["wih"] = np.ascontiguousarray(w_in.reshape(16, 128, 32, 128).transpose(2, 1, 0, 3).reshape(32, 128, 2048))
    gw = np.zeros((2, 2, 8, 128, 128), f)
    for d in range(2):
        for gi, nm in enumerate(("lru_wa", "lru_wx")):
            w = inp[nm][0, d]
            for ch in range(8):
                gw[d, gi, ch, 0:64, 0:64] = w[2 * ch]
                gw[d, gi, ch, 64:128, 64:128] = w[2 * ch + 1]
    s["gw"] = gw.reshape(32, 128, 128)
    s["w_out"] = np.ascontiguousarray(inp["w_out"][0], dtype=f)
    s["lnp"] = np.ascontiguousarray(np.stack([inp["ln1_g"][0], inp["ln1_b"][0], inp["ln2_g"][0], inp["ln2_b"][0]]))
    wq = inp["w_query"][0]
    s["wqh"] = np.ascontiguousarray(wq.reshape(16, 128, 16, 128).transpose(2, 1, 0, 3).reshape(16, 128, 2048))
    sk = inp["sub_keys"][0].reshape(16, 128, 128)
    s["skt"] = np.ascontiguousarray(sk.transpose(2, 0, 1).reshape(128, 2048))
    U = inp["peer_u"][0]
    s["uth"] = np.ascontiguousarray(U.reshape(128, 128, 16, 128).transpose(0, 3, 2, 1).reshape(128, 128, 2048))
    s["pv"] = np.ascontiguousarray(inp["peer_v"][0], dtype=f)
    s["ident"] = np.eye(128, dtype=f)
    sel = np.zeros((64, 4096), f)
    for ti in range(32):
        for m in range(128):
            sel[2 * ti + (m >> 6), ti * 128 + m] = 1.0
    s["sel"] = sel
    s["iota"] = np.tile(np.arange(128, dtype=f)[None, :], (128, 1))
    return s


def kernel(**inputs):
    inp = {k: np.asarray(v) for k, v in inputs.items()}
    stop_after = int(os.environ.get("K_STOP", "99"))
    debug = os.environ.get("K_DEBUG", "0") == "1"
    nc = build_nc(stop_after=stop_after, debug=debug)
    shared = _shared_layout(inp)
    in_maps = []
    for core in range(NCORES):
        m = dict(shared)
        m.update(_host_layout(inp, core))
        in_maps.append(m)
    res = run_bass_kernel_spmd(nc, in_maps, core_ids=list(range(NCORES)))
    if debug:
        kernel.last = res
    y_p = np.zeros((16, 256, D), np.float32)
    y_s = np.zeros((8, 4096, D), np.float32)
    ns = np.zeros((16, 1, 2, 1024), np.float32)
    for core in range(NCORES):
        r = res.results[core]
        ya = np.asarray(r["y_all"])
        y_s[core] = ya[0:4096]
        y_p[2 * core] = ya[4096:4352]
        y_p[2 * core + 1] = ya[4352:4608]
        nsr = np.asarray(r["ns"]).reshape(2, 2, 1024)
        ns[2 * core, 0] = nsr[0]
        ns[2 * core + 1, 0] = nsr[1]
    return (y_p, y_s, ns)
```

```python
import os
from contextlib import ExitStack
import numpy as np
import concourse.bass as bass
import concourse.mybir as mybir
from concourse.bass_utils import run_bass_kernel_spmd

F32 = mybir.dt.float32
BF16 = mybir.dt.bfloat16
U32 = mybir.dt.uint32
I32 = mybir.dt.int32
AF = mybir.ActivationFunctionType
ALU = mybir.AluOpType
AX = mybir.AxisListType

D = 2048
NT = 4608
SEQS = [(0, 4096, 1), (4096, 256, 0), (4352, 256, 0)]
ALPHA = 2.0 ** 0.25
EPS = 1e-6
NCORES = 8

COLS = {}
_off = 0
for _n, _w in [("cvT", 32), ("badaT", 96), ("conv_w", 8 * 31), ("conv_b", 8), ("cln_g", 8), ("cln_b", 8),
               ("w4", 32), ("b4", 8), ("ba", 16), ("bx", 16), ("lam", 16), ("h0", 16)]:
    COLS[_n] = (_off, _w)
    _off += _w
NCOL = _off


class Buf:
    __slots__ = ("w", "r", "multi", "ws")

    def __init__(self, multi=False):
        self.w = None
        self.r = {}
        self.multi = multi
        self.ws = {}


class Sched:
    EPOCH = 20000

    def __init__(self, nc, stack):
        self.nc = nc
        self.stack = stack
        self.engs = {"pe": nc.tensor, "dve": nc.vector, "act": nc.scalar, "pool": nc.gpsimd, "sp": nc.sync}
        self.csem = {}
        self.ccnt = {}
        self.waited = {e: {} for e in self.engs}
        self.nsem = 0
        self.allsems = []
        self.sem_owner = {}
        for e in ("pe", "dve", "act", "pool"):
            self._new_csem(e)
        self.slots = {}
        self.slot_i = {}
        for q, k in (("sp", 16), ("pool", 8), ("act", 4)):
            self.slots[q] = [[self._sem(f"d{q}{i}"), 0] for i in range(k)]
            self.slot_i[q] = 0

    def _same_eng(self, e, ev):
        return self.sem_owner.get(id(ev[0])) == e

    def _sem(self, name):
        self.nsem += 1
        s = self.stack.enter_context(self.nc.semaphore(f"{name}_{self.nsem}"))
        self.allsems.append(s)
        return s

    def _new_csem(self, e):
        self.csem[e] = self._sem("c" + e)
        self.sem_owner[id(self.csem[e])] = e
        self.ccnt[e] = 0

    def _wait(self, e, ev):
        sem, val = ev
        k = id(sem)
        if self.waited[e].get(k, 0) >= val:
            return
        self.engs[e].wait_ge(sem, val)
        self.waited[e][k] = val

    def _deps(self, e, reads, writes):
        for b in reads:
            if b.multi:
                for ev in list(b.ws.values()):
                    self._wait(e, ev)
                continue
            if b.w is not None:
                if not (e == "pe" and self._same_eng(e, b.w)):
                    self._wait(e, b.w)
        for b in writes:
            if b.w is not None and not b.multi:
                if not self._same_eng(e, b.w):
                    self._wait(e, b.w)
            for ev in list(b.r.values()):
                if self._same_eng(e, ev):
                    continue
                self._wait(e, ev)

    def _mark(self, ev, reads, writes):
        for b in reads:
            k = id(ev[0])
            if k not in b.r or b.r[k][1] < ev[1]:
                b.r[k] = ev
        for b in writes:
            if b.multi:
                k = id(ev[0])
                if k not in b.ws or b.ws[k][1] < ev[1]:
                    b.ws[k] = ev
                continue
            b.w = ev
            b.r = {}

    def op(self, e, fn, reads=(), writes=()):
        self._deps(e, reads, writes)
        inst = fn(self.engs[e])
        self.ccnt[e] += 1
        inst.then_inc(self.csem[e], 1)
        ev = (self.csem[e], self.ccnt[e])
        if self.ccnt[e] >= self.EPOCH:
            self._new_csem(e)
        self._mark(ev, reads, writes)
        return ev

    def dma(self, e, fn, reads=(), writes=()):
        self._deps(e, reads, writes)
        sl = self.slots[e]
        slot = sl[self.slot_i[e] % len(sl)]
        self.slot_i[e] += 1
        sem, uses = slot
        if uses > 0:
            self._wait(e, (sem, 16 * uses))
        inst = fn(self.engs[e])
        inst.then_inc(sem, 16)
        slot[1] = uses + 1
        ev = (sem, 16 * (uses + 1))
        self._mark(ev, reads, writes)
        return ev

    def barrier(self):
        evs = []
        for e in ("pe", "dve", "act", "pool"):
            if self.ccnt[e] > 0:
                evs.append((self.csem[e], self.ccnt[e]))
        for q in self.slots:
            for sem, uses in self.slots[q]:
                if uses > 0:
                    evs.append((sem, 16 * uses))
        for e in self.engs:
            for ev in evs:
                self._wait(e, ev)


def cap(full_ap, off, dims):
    return bass.AP(full_ap.tensor, full_ap.offset + off, [list(full_ap.ap[0])] + [list(d) for d in dims])


def dram_ap(full_ap, off, dims):
    return bass.AP(full_ap.tensor, full_ap.offset + off, [list(d) for d in dims])


def build_nc(stop_after=99, debug=False):
    nc = bass.Bass("TRN2", target_bir_lowering=False)
    okind = "ExternalOutput" if debug else "Internal"

    def din(name, shape, dt=F32):
        return nc.dram_tensor(name, list(shape), dt, kind="ExternalInput").ap()

    x_all = din("x_all", [NT, D])
    cols_d = din("cols", [128, NCOL])
    w_ada = din("w_ada", [D, 6 * D])
    wih = din("wih", [32, 128, 2048])
    gw_d = din("gw", [32, 128, 128])
    w_out = din("w_out", [D, D])
    lnp = din("lnp", [4, D])
    wqh = din("wqh", [16, 128, 2048])
    skt_d = din("skt", [128, 16 * 128])
    uth = din("uth", [128, 128, 2048])
    pv = din("pv", [16384, D])
    ident_d = din("ident", [128, 128])
    sel_d = din("sel", [64, 4096])
    iota_d = din("iota", [128, 128])

    y_all = nc.dram_tensor("y_all", [NT, D], F32, kind="ExternalOutput").ap()
    ns_d = nc.dram_tensor("ns", [4, 1024], F32, kind="ExternalOutput").ap()

    Zs = nc.dram_tensor("Zs", [1024, NT], F32, kind=okind).ap()
    CATC = nc.dram_tensor("CATC", [1024, NT], BF16, kind=okind).ap()
    bCATC = Buf(multi=True)
    RXs = nc.dram_tensor("RXs", [1024, NT], F32, kind=okind).ap()
    GRs = nc.dram_tensor("GRs", [1024, NT], F32, kind=okind).ap()
    CATR = nc.dram_tensor("CATR", [1024, NT], BF16, kind=okind).ap()
    X1s = nc.dram_tensor("X1s", [NT, D], F32, kind=okind).ap()
    U2T = nc.dram_tensor("U2T", [D, NT], BF16, kind=okind).ap()
    Gs = nc.dram_tensor("Gs", [128, 128, NT], BF16, kind="Internal").ap()
    UTB = nc.dram_tensor("UTB", [128, 128, 2048], BF16, kind="Internal").ap()
    PVB = nc.dram_tensor("PVB", [16384, D], BF16, kind="Internal").ap()
    bUTB = Buf(multi=True)
    bPVB = Buf(multi=True)
    ERs = nc.dram_tensor("ERs", [64, 1024], F32, kind="Internal").ap()
    bERs = Buf()
    bZ, bRX, bGR, bCATR, bX1, bU2T, bG, bY, bNS = (Buf(multi=True) for _ in range(9))

    with ExitStack() as top:
        top.enter_context(nc.allow_non_contiguous_dma(reason="small column/state transfers"))
        S = Sched(nc, top)

        uniq = [0]

        def sbt(stack, name, shape, dt=F32):
            uniq[0] += 1
            return stack.enter_context(nc.sbuf_tensor(f"s{uniq[0]}_{name}", list(shape), dt))

        cols = sbt(top, "cols", [128, NCOL])
        b_cols = Buf()
        ident = sbt(top, "ident", [128, 128])
        b_ident = Buf()
        modT = sbt(top, "modT", [128, 96, 2])
        b_modT = Buf()
        cdec = sbt(top, "cdec", [128, 32])
        b_cdec = Buf()
        NSt = sbt(top, "NSt", [128, 32])
        b_NSt = Buf()
        EC = sbt(top, "EC", [128, 1024])
        b_EC = Buf()
        ER = sbt(top, "ER", [64, 1024])
        b_ER = Buf()
        ERBS = {"t": [], "b": []}

        def make_erb(stack, n):
            ERBS["t"] = [sbt(stack, f"ERB{i}", [128, 1024]) for i in range(n)]
            ERBS["b"] = [Buf() for _ in range(n)]
        sel_ctr = [0]
        psb = [top.enter_context(nc.psum_tensor(f"ps{i}", [128, 512], F32)) for i in range(8)]
        b_ps = [Buf() for _ in range(8)]

        def col(name, j0=0, n=1):
            o, w = COLS[name]
            return cols[:, o + j0:o + j0 + n]

        S.dma("sp", lambda e: e.dma_start(out=cols[:], in_=cols_d[:, :]), writes=[b_cols])
        S.dma("sp", lambda e: e.dma_start(out=ident[:], in_=ident_d[:, :]), writes=[b_ident])

        with ExitStack() as ph:
            scT = sbt(ph, "scT", [128, 32])
            b_scT = Buf()
            WA = [sbt(ph, f"WA{i}", [128, 16, 512]) for i in range(2)]
            b_WA = [Buf() for _ in range(2)]
            tmp0 = sbt(ph, "tmp0", [128, 1024])
            b_tmp0 = Buf()
            tmp1 = sbt(ph, "tmp1", [128, 1024])
            b_tmp1 = Buf()
            tmpi = sbt(ph, "tmpi", [128, 1024], I32)
            b_tmpi = Buf()
            iot = sbt(ph, "iot", [128, 128])
            b_iot = Buf()
            pidx = sbt(ph, "pidx", [128, 4])
            b_pidx = Buf()

            S.op("act", lambda e: e.activation(out=scT[:], in_=col("cvT", 0, 32), func=AF.Silu),
                 reads=[b_cols], writes=[b_scT])
            w_ada_v = w_ada.rearrange("(kk p) n -> p kk n", p=128)
            Rrow = [sbt(ph, f"Rrow{i}", [2, 512]) for i in range(2)]
            b_Rrow = [Buf() for _ in range(2)]
            for nb in range(24):
                wa = WA[nb % 2]
                bwa = b_WA[nb % 2]
                S.dma("sp", lambda e, wa=wa, nb=nb: e.dma_start(out=wa[:], in_=w_ada_v[:, :, nb * 512:(nb + 1) * 512]),
                      writes=[bwa])
                pb = nb % 2
                for kk in range(16):
                    S.op("pe", lambda e, wa=wa, kk=kk, pb=pb: e.matmul(
                        psb[pb][0:2, :], lhsT=scT[:, kk * 2:kk * 2 + 2], rhs=wa[:, kk, :],
                        start=(kk == 0), stop=(kk == 15)), reads=[bwa, b_scT], writes=[b_ps[pb]])
                rr, brr = Rrow[nb % 2], b_Rrow[nb % 2]
                S.op("act", lambda e, rr=rr, pb=pb: e.activation(out=rr[:], in_=psb[pb][0:2, :], func=AF.Copy),
                     reads=[b_ps[pb]], writes=[brr])
                pt = 2 + (nb % 2)
                for nn in range(4):
                    S.op("pe", lambda e, rr=rr, nn=nn, pt=pt: e.transpose(
                        out=psb[pt][:, nn * 2:nn * 2 + 2], in_=rr[0:2, nn * 128:(nn + 1) * 128],
                        identity=ident[0:2, 0:2]), reads=[brr, b_ident], writes=[b_ps[pt]])
                for nn in range(4):
                    nch = nb * 4 + nn
                    S.op("dve", lambda e, nch=nch, nn=nn, pt=pt: e.tensor_scalar(
                        out=modT[:, nch, :], in0=psb[pt][:, nn * 2:nn * 2 + 2], scalar1=col("badaT", nch),
                        scalar2=None, op0=ALU.add), reads=[b_ps[pt], b_cols], writes=[b_modT])
            for base in (16, 64):
                S.op("dve", lambda e, base=base: e.tensor_scalar(
                    out=modT[:, base:base + 16, :], in0=modT[:, base:base + 16, :], scalar1=1.0, scalar2=None,
                    op0=ALU.add), reads=[b_modT], writes=[b_modT])
            S.op("act", lambda e: e.activation(out=tmp0[:, 0:16], in_=col("lam", 0, 16), func=AF.Exp, scale=-1.0),
                 reads=[b_cols], writes=[b_tmp0])
            S.op("act", lambda e: e.activation(out=tmp0[:, 16:32], in_=tmp0[:, 0:16], func=AF.Ln, bias=1.0),
                 reads=[b_tmp0], writes=[b_tmp0])
            S.op("dve", lambda e: e.tensor_scalar(out=cdec[:, 0:16], in0=tmp0[:, 16:32], scalar1=-8.0, scalar2=None,
                                                  op0=ALU.mult), reads=[b_tmp0], writes=[b_cdec])
            S.op("dve", lambda e: e.tensor_scalar(out=cdec[:, 16:32], in0=tmp0[:, 16:32], scalar1=-16.0, scalar2=None,
                                                  op0=ALU.mult), reads=[b_tmp0], writes=[b_cdec])
            S.dma("sp", lambda e: e.dma_start(out=iot[:], in_=iota_d[:, :]), writes=[b_iot])
            S.op("pe", lambda e: e.transpose(out=psb[4][:, 0:128], in_=iot[:], identity=ident[:]),
                 reads=[b_iot, b_ident], writes=[b_ps[4]])
            S.op("dve", lambda e: e.tensor_copy(out=pidx[:, 0:1], in_=psb[4][:, 0:1]), reads=[b_ps[4]], writes=[b_pidx])
            S.op("dve", lambda e: e.tensor_scalar(out=pidx[:, 1:2], in0=pidx[:, 0:1], scalar1=64.0, scalar2=-64.0,
                                                  op0=ALU.is_ge, op1=ALU.mult), reads=[b_pidx], writes=[b_pidx])
            S.op("dve", lambda e: e.tensor_tensor(out=pidx[:, 2:3], in0=pidx[:, 0:1], in1=pidx[:, 1:2], op=ALU.add),
                 reads=[b_pidx], writes=[b_pidx])
            for h in range(4):
                S.op("act", lambda e, h=h: e.activation(
                    out=tmp0[:, 512 + h * 128:512 + (h + 1) * 128], in_=iot[:], func=AF.Exp,
                    scale=-float(np.log(10000.0) / 512.0), bias=-float(np.log(10000.0) / 512.0) * 128.0 * h),
                    reads=[b_iot], writes=[b_tmp0])
            S.op("dve", lambda e: e.tensor_scalar(out=tmp0[:, 512:1024], in0=tmp0[:, 512:1024],
                                                  scalar1=float(1.0 / (2.0 * np.pi)), scalar2=None, op0=ALU.mult),
                 reads=[b_tmp0], writes=[b_tmp0])

            def sincos(dst, bdst, npart, idxcol):
                for part, shift in ((0, 0.0), (1, 0.25)):
                    S.op("dve", lambda e, shift=shift: e.tensor_scalar(
                        out=tmp1[:npart, 0:512], in0=tmp0[:npart, 512:1024], scalar1=idxcol, scalar2=shift,
                        op0=ALU.mult, op1=ALU.add), reads=[b_tmp0, b_pidx], writes=[b_tmp1])
                    S.op("dve", lambda e: e.tensor_copy(out=tmpi[:npart, 0:512], in_=tmp1[:npart, 0:512]),
                         reads=[b_tmp1], writes=[b_tmpi])
                    S.op("dve", lambda e: e.tensor_copy(out=tmp1[:npart, 512:1024], in_=tmpi[:npart, 0:512]),
                         reads=[b_tmpi], writes=[b_tmp1])
                    S.op("dve", lambda e: e.tensor_tensor(out=tmp1[:npart, 0:512], in0=tmp1[:npart, 0:512],
                                                          in1=tmp1[:npart, 512:1024], op=ALU.subtract),
                         reads=[b_tmp1], writes=[b_tmp1])
                    S.op("dve", lambda e: e.tensor_scalar(out=tmp1[:npart, 512:1024], in0=tmp1[:npart, 0:512],
                                                          scalar1=0.5, scalar2=None, op0=ALU.is_gt),
                         reads=[b_tmp1], writes=[b_tmp1])
                    S.op("dve", lambda e: e.tensor_tensor(out=tmp1[:npart, 0:512], in0=tmp1[:npart, 0:512],
                                                          in1=tmp1[:npart, 512:1024], op=ALU.subtract),
                         reads=[b_tmp1], writes=[b_tmp1])
                    S.op("dve", lambda e: e.tensor_scalar(out=tmp1[:npart, 512:1024], in0=tmp1[:npart, 0:512],
                                                          scalar1=-0.5, scalar2=None, op0=ALU.is_lt),
                         reads=[b_tmp1], writes=[b_tmp1])
                    S.op("dve", lambda e: e.tensor_tensor(out=tmp1[:npart, 0:512], in0=tmp1[:npart, 0:512],
                                                          in1=tmp1[:npart, 512:1024], op=ALU.add),
                         reads=[b_tmp1], writes=[b_tmp1])
                    S.op("dve", lambda e: e.tensor_scalar(out=tmp1[:npart, 0:512], in0=tmp1[:npart, 0:512],
                                                          scalar1=0.49999, scalar2=-0.49999, op0=ALU.min, op1=ALU.max),
                         reads=[b_tmp1], writes=[b_tmp1])
                    S.op("act", lambda e, part=part: e.activation(
                        out=dst[:npart, part * 512:(part + 1) * 512], in_=tmp1[:npart, 0:512], func=AF.Sin,
                        scale=float(2.0 * np.pi)), reads=[b_tmp1], writes=[bdst])

            sincos(EC, b_EC, 128, pidx[:, 2:3])
            sincos(ER, b_ER, 64, pidx[:64, 0:1])
            S.dma("sp", lambda e: e.dma_start(out=ERs[:, :], in_=ER[:]), reads=[b_ER], writes=[bERs])
            S.op("pool", lambda e: e.memset(NSt[:], 0.0), writes=[b_NSt])
            S.barrier()

        def bcast_mod(dst, bdst, chunk0, r, tmpt, btmp):
            for q in range(4):
                pb = 4 + (q % 2)
                for j in range(4):
                    kk = q * 4 + j
                    S.op("dve", lambda e, kk=kk: e.tensor_copy(
                        out=tmpt[:], in_=cap(modT[:], (chunk0 + kk) * 2 + r, [[0, 128]])),
                        reads=[b_modT], writes=[btmp])
                    S.op("pe", lambda e, j=j, pb=pb: e.matmul(psb[pb][:, j * 128:(j + 1) * 128], lhsT=tmpt[:],
                                                             rhs=ident[:], start=True, stop=True),
                         reads=[btmp, b_ident], writes=[b_ps[pb]])
                S.op("act", lambda e, q=q, pb=pb: e.activation(out=dst[:, q * 512:(q + 1) * 512], in_=psb[pb][:],
                                                              func=AF.Copy), reads=[b_ps[pb]], writes=[bdst])

        def ln_stats(xap, bx, st, bst):
            for q in range(4):
                S.op("dve", lambda e, q=q: e.bn_stats(out=st[:, 8 + q * 6:8 + (q + 1) * 6], in_=xap[:, q * 512:(q + 1) * 512]),
                     reads=[bx], writes=[bst])
            S.op("dve", lambda e: e.bn_aggr(out=st[:, 0:2], in_=st[:, 8:32]), reads=[bst], writes=[bst])
            S.op("act", lambda e: e.activation(out=st[:, 2:3], in_=st[:, 1:2], func=AF.Sqrt, bias=EPS, scale=1.0),
                 reads=[bst], writes=[bst])
            S.op("dve", lambda e: e.reciprocal(out=st[:, 2:3], in_=st[:, 2:3]), reads=[bst], writes=[bst])

        def load_x_dma(xt, bxt, tok0, mrow, tile_in_seq):
            S.dma("sp", lambda e: e.dma_start(out=xt[:], in_=x_all[tok0:tok0 + 128, :]), writes=[bxt])
            if mrow != 1:
                return None
            k_ = sel_ctr[0] % len(ERBS["t"])
            sel_ctr[0] += 1
            erb, berb = ERBS["t"][k_], ERBS["b"][k_]
            for hh in range(2):
                S.dma("sp", lambda e, hh=hh: e.dma_start(
                    out=erb[hh * 64:(hh + 1) * 64, :],
                    in_=dram_ap(ERs, (2 * tile_in_seq + hh) * 1024, [[0, 64], [1, 1024]])),
                    reads=[bERs], writes=[berb])
            return (erb, berb)

        def load_x_add(xt, bxt, eh, ec_eng="pool"):
            if eh is None:
                return
            erb, berb = eh
            S.op("dve", lambda e: e.tensor_tensor(out=xt[:, 0:1024], in0=xt[:, 0:1024], in1=erb[:],
                                                  op=ALU.add), reads=[berb, bxt], writes=[bxt])
            S.op(ec_eng, lambda e: e.tensor_tensor(out=xt[:, 1024:2048], in0=xt[:, 1024:2048], in1=EC[:],
                                                   op=ALU.add), reads=[b_EC, bxt], writes=[bxt])

        def load_x_tile(xt, bxt, tok0, mrow, tile_in_seq, ec_eng="pool"):
            eh = load_x_dma(xt, bxt, tok0, mrow, tile_in_seq)
            load_x_add(xt, bxt, eh, ec_eng)

        def transpose_mod(xh, bxh, dstT, bdst, c0, sc_chunk, sh_chunk, r, tb0=4):
            for q in range(4):
                pb = tb0 + (q % 2)
                for j in range(4):
                    kk = q * 4 + j
                    S.op("pe", lambda e, kk=kk, j=j, pb=pb: e.transpose(
                        out=psb[pb][:, j * 128:(j + 1) * 128], in_=xh[:, kk * 128:(kk + 1) * 128], identity=ident[:]),
                        reads=[bxh, b_ident], writes=[b_ps[pb]])
                for j in range(4):
                    kk = q * 4 + j
                    S.op("dve", lambda e, kk=kk, j=j, pb=pb: e.tensor_scalar(
                        out=dstT[:, kk, c0:c0 + 128], in0=psb[pb][:, j * 128:(j + 1) * 128],
                        scalar1=modT[:, sc_chunk + kk, r:r + 1], scalar2=modT[:, sh_chunk + kk, r:r + 1],
                        op0=ALU.mult, op1=ALU.add), reads=[b_ps[pb], b_modT], writes=[bdst])

        blocks = []
        for (g0, L, mrow) in SEQS:
            T = 512 if L >= 512 else L
            for s in range(0, L, T):
                blocks.append((g0, L, mrow, s, T))

        blocks1 = []
        for (g0, L, mrow) in SEQS:
            T = 1024 if L >= 1024 else L
            for s_ in range(0, L, T):
                blocks1.append((g0, L, mrow, s_, T))

        def uvconv_gen(ph):
            stg = [sbt(ph, f"stg{i}", [128, D], BF16) for i in range(2)]
            b_stg = [Buf() for _ in range(2)]
            k = 0
            for i in range(128):
                for which in range(2):
                    st_, bst_ = stg[k % 2], b_stg[k % 2]
                    k += 1
                    if which == 0:
                        S.dma("pool", lambda e: e.dma_start(out=st_[:], in_=uth[i, :, :]), writes=[bst_])
                        S.dma("sp", lambda e: e.dma_start(out=UTB[i, :, :], in_=st_[:]), reads=[bst_], writes=[bUTB])
                    else:
                        S.dma("pool", lambda e: e.dma_start(out=st_[:], in_=pv[i * 128:(i + 1) * 128, :]),
                              writes=[bst_])
                        S.dma("sp", lambda e: e.dma_start(out=PVB[i * 128:(i + 1) * 128, :], in_=st_[:]),
                              reads=[bst_], writes=[bPVB])
                    yield

        def p2b1_gen(ph):
            zb = [sbt(ph, f"zb{i}", [128, 512 + 32]) for i in range(4)]
            b_zb = [Buf() for _ in range(4)]
            ct = sbt(ph, "ct", [128, 8, 512])
            b_ctl = [Buf() for _ in range(8)]
            sq = [sbt(ph, f"sq{i}", [128, 512]) for i in range(2)]
            b_sq = [Buf() for _ in range(2)]
            mu = sbt(ph, "mu", [128, 512])
            b_mu = Buf()
            rs = sbt(ph, "rs", [128, 512])
            b_rs = Buf()
            catc = [sbt(ph, f"catc{i}", [128, 8, 512], BF16) for i in range(1)]
            b_catc = [Buf() for _ in range(1)]
            ones = sbt(ph, "ones", [128, 128])
            b_ones = Buf()
            S.op("pool", lambda e: e.memset(ones[:], 1.0), writes=[b_ones])
            yield "alloc"
            zi = 0
            for bi, (g0, L, mrow, s, T) in enumerate(blocks):
                lo = max(s - 15, 0)
                hi = min(s + T + 15, L)
                doff = lo - (s - 15)
                while zready.get(g0, 0) < hi:
                    yield "wait"
                for cp in range(0, 8, 2):
                    zs_ = []
                    for cc in (cp, cp + 1):
                        z, bz = zb[zi % 4], b_zb[zi % 4]
                        zi += 1
                        zs_.append((z, bz))
                        if lo > s - 15 or hi < s + T + 15:
                            S.op("dve", lambda e, z=z: e.memset(z[:], 0.0), writes=[bz])
                        S.dma("sp", lambda e, z=z, cc=cc, lo=lo, hi=hi, doff=doff, g0=g0: e.dma_start(
                            out=z[:, doff:doff + hi - lo], in_=Zs[cc * 128:(cc + 1) * 128, g0 + lo:g0 + hi]),
                            reads=[bZ], writes=[bz])
                    for k in range(31):
                        for q_, cc in enumerate((cp, cp + 1)):
                            z, bz = zs_[q_]
                            if k == 0:
                                S.op("dve", lambda e, z=z, cc=cc, T=T: e.tensor_scalar(
                                    out=ct[:, cc, 0:T], in0=z[:, 0:T], scalar1=col("conv_w", cc * 31),
                                    scalar2=col("conv_b", cc), op0=ALU.mult, op1=ALU.add), reads=[bz, b_cols],
                                    writes=[b_ctl[cc]])
                            else:
                                S.op("dve", lambda e, z=z, cc=cc, T=T, k=k: e.scalar_tensor_tensor(
                                    out=ct[:, cc, 0:T], in0=z[:, k:k + T], scalar=col("conv_w", cc * 31 + k),
                                    in1=ct[:, cc, 0:T], op0=ALU.mult, op1=ALU.add),
                                    reads=[bz, b_cols, b_ctl[cc]], writes=[b_ctl[cc]])
                            yield
                    for cc in (cp, cp + 1):
                        S.op("act", lambda e, cc=cc, T=T: e.activation(out=sq[cc % 2][:, 0:T], in_=ct[:, cc, 0:T],
                                                                       func=AF.Square), reads=[b_ctl[cc]],
                             writes=[b_sq[cc % 2]])
                        S.op("pe", lambda e, cc=cc, T=T: e.matmul(psb[6][:, 0:T], lhsT=ones[:], rhs=ct[:, cc, 0:T],
                                                                  start=True, stop=True),
                             reads=[b_ones, b_ctl[cc]], writes=[b_ps[6]])
                        if cc == 0:
                            S.op("dve", lambda e, T=T: e.tensor_copy(out=mu[:, 0:T], in_=psb[6][:, 0:T]),
                                 reads=[b_ps[6]], writes=[b_mu])
                        else:
                            S.op("dve", lambda e, T=T: e.tensor_tensor(out=mu[:, 0:T], in0=psb[6][:, 0:T],
                                                                       in1=mu[:, 0:T], op=ALU.add),
                                 reads=[b_ps[6], b_mu], writes=[b_mu])
                        S.op("pe", lambda e, cc=cc, T=T: e.matmul(psb[7][:, 0:T], lhsT=ones[:],
                                                                  rhs=sq[cc % 2][:, 0:T], start=True, stop=True),
                             reads=[b_ones, b_sq[cc % 2]], writes=[b_ps[7]])
                        if cc == 0:
                            S.op("dve", lambda e, T=T: e.tensor_copy(out=rs[:, 0:T], in_=psb[7][:, 0:T]),
                                 reads=[b_ps[7]], writes=[b_rs])
                        else:
                            S.op("dve", lambda e, T=T: e.tensor_tensor(out=rs[:, 0:T], in0=psb[7][:, 0:T],
                                                                       in1=rs[:, 0:T], op=ALU.add),
                                 reads=[b_ps[7], b_rs], writes=[b_rs])
                S.op("dve", lambda e, T=T: e.tensor_scalar(out=mu[:, 0:T], in0=mu[:, 0:T], scalar1=1.0 / 1024.0,
                                                           scalar2=None, op0=ALU.mult), reads=[b_mu],
                     writes=[b_mu])
                S.op("dve", lambda e, T=T: e.tensor_tensor(out=sq[0][:, 0:T], in0=mu[:, 0:T], in1=mu[:, 0:T],
                                                           op=ALU.mult), reads=[b_mu], writes=[b_sq[0]])
                S.op("dve", lambda e, T=T: e.scalar_tensor_tensor(
                    out=rs[:, 0:T], in0=rs[:, 0:T], scalar=1.0 / 1024.0, in1=sq[0][:, 0:T], op0=ALU.mult,
                    op1=ALU.subtract), reads=[b_rs, b_sq[0]], writes=[b_rs])
                S.op("act", lambda e, T=T: e.activation(out=rs[:, 0:T], in_=rs[:, 0:T], func=AF.Sqrt, bias=EPS,
                                                        scale=1.0), reads=[b_rs], writes=[b_rs])
                S.op("dve", lambda e, T=T: e.reciprocal(out=rs[:, 0:T], in_=rs[:, 0:T]), reads=[b_rs],
                     writes=[b_rs])
                cco, bcco = catc[0], b_catc[0]
                for cc in range(8):
                    S.op("dve", lambda e, cc=cc, T=T: e.tensor_tensor(
                        out=ct[:, cc, 0:T], in0=ct[:, cc, 0:T], in1=mu[:, 0:T], op=ALU.subtract),
                        reads=[b_ctl[cc], b_mu], writes=[b_ctl[cc]])
                    S.op("dve", lambda e, cc=cc, T=T: e.tensor_tensor(
                        out=ct[:, cc, 0:T], in0=ct[:, cc, 0:T], in1=rs[:, 0:T], op=ALU.mult),
                        reads=[b_ctl[cc], b_rs], writes=[b_ctl[cc]])
                    S.op("act", lambda e, cc=cc, T=T, cco=cco: e.activation(
                        out=cco[:, cc, 0:T], in_=ct[:, cc, 0:T], func=AF.Silu, scale=col("cln_g", cc),
                        bias=col("cln_b", cc)), reads=[b_ctl[cc], b_cols], writes=[bcco])
                S.dma("sp", lambda e, g0=g0, s=s, T=T, cco=cco: e.dma_start(
                    out=CATC[:, g0 + s:g0 + s + T].rearrange("(c p) t -> p c t", p=128), in_=cco[:, :, 0:T]),
                    reads=[bcco], writes=[bCATC])

        zready = {}
        cst = ExitStack()
        cgen0 = [None]
        if stop_after >= 3:
            cgen0[0] = p2b1_gen(cst)
            assert next(cgen0[0]) == "alloc"

        if stop_after >= 1:
            with ExitStack() as ph:
                xts = [sbt(ph, f"xt{i}", [128, D]) for i in range(3)]
                b_xts = [Buf() for _ in range(3)]
                sts = [sbt(ph, f"st{i}", [128, 32]) for i in range(2)]
                b_sts = [Buf() for _ in range(2)]
                uTs = [sbt(ph, f"uT{i}", [128, 16, 1024], BF16) for i in range(2)]
                b_uTs = [Buf() for _ in range(2)]
                Wb = [sbt(ph, f"Wb{i}", [128, 16, 128], BF16) for i in range(4)]
                b_Wb = [Buf() for _ in range(4)]
                sg = [sbt(ph, f"sg{i}", [128, 512]) for i in range(3)]
                b_sg = [Buf() for _ in range(3)]
                zo = [sbt(ph, f"zo{i}", [128, 512], BF16) for i in range(2)]
                b_zo = [Buf() for _ in range(2)]
                ot = [sbt(ph, f"ot{i}", [128, 512]) for i in range(4)]
                b_ot = [Buf() for _ in range(4)]
                xi = 0
                wi = 0
                oi = 0
                gi = 0
                xi_ = [0]
                lnq = []
                make_erb(ph, 2)

                def ln_tile(bi, ti):
                    g0, L, mrow, s, T = blocks1[bi]
                    uT, b_uT = uTs[bi % 2], b_uTs[bi % 2]
                    xt, bxt = xts[xi_[0] % 3], b_xts[xi_[0] % 3]
                    st, bst = sts[xi_[0] % 2], b_sts[xi_[0] % 2]
                    xi_[0] += 1
                    tok0 = g0 + s + ti * 128
                    load_x_tile(xt, bxt, tok0, mrow, (s + ti * 128) // 128, ec_eng="dve")
                    ln_stats(xt, bxt, st, bst)
                    S.op("dve", lambda e, xt=xt, st=st: e.tensor_scalar(
                        out=xt[:], in0=xt[:], scalar1=st[:, 0:1], scalar2=st[:, 2:3], op0=ALU.subtract,
                        op1=ALU.mult), reads=[bxt, bst], writes=[bxt])
                    lnq.append((xt, bxt, uT, b_uT, ti * 128, mrow))

                def ln_b():
                    if lnq:
                        xt, bxt, uT, b_uT, c0, mrow = lnq.pop(0)
                        transpose_mod(xt, bxt, uT, b_uT, c0, 16, 0, mrow, tb0=6)

                order = []
                for cc in range(8):
                    order += [cc, 8 + cc]
                order += list(range(16, 32))
                for bi, (g0, L, mrow, s, T) in enumerate(blocks1):
                    uT, b_uT = uTs[bi % 2], b_uTs[bi % 2]
                    if bi == 0:
                        for ti in range(T // 128):
                            ln_tile(0, ti)
                            ln_b()
                    while lnq:
                        ln_b()
                    nxt_tiles = list(range(blocks1[bi + 1][4] // 128)) if bi + 1 < len(blocks1) else []
                    pend_val = None
                    nh = max(T // 512, 1)
                    Th = min(T, 512)
                    for idx, n in enumerate(order):
                        if idx == 16:
                            zready[g0] = s + T
                        if cgen0[0] is not None and idx >= 1:
                            for _ in range(6 if T >= 1024 else 2):
                                if next(cgen0[0], None) == "wait":
                                    break
                        if idx % 4 == 1:
                            ln_b()
                            if nxt_tiles:
                                ln_tile(bi + 1, nxt_tiles.pop(0))
                        wb, bwb = Wb[wi % 4], b_Wb[wi % 4]
                        wi += 1
                        S.dma("pool", lambda e, wb=wb, n=n: e.dma_start(
                            out=wb[:].rearrange("p a b -> p (a b)"), in_=wih[n, :, :]), writes=[bwb])
                        pbs = [((wi % 3) * 2 + hh) for hh in range(nh)]
                        for kk in range(16):
                            for hh in range(nh):
                                S.op("pe", lambda e, wb=wb, kk=kk, hh=hh: e.matmul(
                                    psb[pbs[hh]][:, 0:Th], lhsT=wb[:, kk, :], rhs=uT[:, kk, hh * 512:hh * 512 + Th],
                                    start=(kk == 0), stop=(kk == 15)), reads=[bwb, b_uT], writes=[b_ps[pbs[hh]]])
                        if n < 8:
                            pend_val = pbs
                            continue
                        for hh in range(nh):
                            pb = pbs[hh]
                            tsl = slice(g0 + s + hh * 512, g0 + s + hh * 512 + Th)
                            if n < 16:
                                cc = n - 8
                                sgt, bsg = sg[gi % 3], b_sg[gi % 3]
                                gi += 1
                                o, bo = ot[oi % 4], b_ot[oi % 4]
                                oi += 1
                                S.op("act", lambda e, sgt=sgt, pb=pb: e.activation(
                                    out=sgt[:, 0:Th], in_=psb[pb][:, 0:Th], func=AF.Sigmoid),
                                    reads=[b_ps[pb]], writes=[bsg])
                                pv_ = pend_val[hh]
                                S.op("dve", lambda e, o=o, sgt=sgt, pv_=pv_: e.tensor_tensor(
                                    out=o[:, 0:Th], in0=psb[pv_][:, 0:Th], in1=sgt[:, 0:Th], op=ALU.mult),
                                    reads=[b_ps[pv_], bsg], writes=[bo])
                                S.dma("sp", lambda e, o=o, cc=cc, tsl=tsl: e.dma_start(
                                    out=Zs[cc * 128:(cc + 1) * 128, tsl], in_=o[:, 0:Th]), reads=[bo], writes=[bZ])
                            elif n < 24:
                                cc = n - 16
                                o, bo = ot[oi % 4], b_ot[oi % 4]
                                oi += 1
                                S.op("act", lambda e, o=o, pb=pb: e.activation(
                                    out=o[:, 0:Th], in_=psb[pb][:, 0:Th], func=AF.Gelu_apprx_tanh),
                                    reads=[b_ps[pb]], writes=[bo])
                                S.dma("sp", lambda e, o=o, cc=cc, tsl=tsl: e.dma_start(
                                    out=GRs[cc * 128:(cc + 1) * 128, tsl], in_=o[:, 0:Th]), reads=[bo], writes=[bGR])
                            else:
                                cc = n - 24
                                o, bo = ot[oi % 4], b_ot[oi % 4]
                                oi += 1
                                S.op("dve", lambda e, o=o, pb=pb: e.tensor_copy(out=o[:, 0:Th], in_=psb[pb][:, 0:Th]),
                                     reads=[b_ps[pb]], writes=[bo])
                                S.dma("sp", lambda e, o=o, cc=cc, tsl=tsl: e.dma_start(
                                    out=RXs[cc * 128:(cc + 1) * 128, tsl], in_=o[:, 0:Th]), reads=[bo], writes=[bRX])
                    while nxt_tiles:
                        ln_b()
                        ln_tile(bi + 1, nxt_tiles.pop(0))
                S.barrier()

        if stop_after >= 2:
            with ExitStack() as ph:
                LM = 4096
                rxp = sbt(ph, "rxp", [128, LM + 4])
                b_rxp = Buf()
                xr = sbt(ph, "xr", [128, LM])
                b_xr = Buf()
                xrb = sbt(ph, "xrb", [128, LM], BF16)
                b_xrb = Buf()
                At = sbt(ph, "At", [128, LM])
                b_A = Buf()
                Bt = sbt(ph, "Bt", [128, LM])
                b_B = Buf()
                Tm = sbt(ph, "Tm", [128, LM])
                b_T = Buf()
                HF = sbt(ph, "HF", [128, LM])
                b_HF = Buf()
                GRt = sbt(ph, "GRt", [128, LM])
                b_GRt = Buf()
                yrb = sbt(ph, "yrb", [128, LM], BF16)
                b_yrb = Buf()
                GWb = sbt(ph, "GWb", [128, 32, 128], BF16)
                b_GWb = Buf()
                b_GWf = Buf()
                zcol = sbt(ph, "zcol", [128, 1])
                b_zcol = Buf()
                S.op("pool", lambda e: e.memset(zcol[:], 0.0), writes=[b_zcol])
                inner_ = ExitStack()
                GWf = sbt(inner_, "GWf", [128, 32, 128])
                S.dma("sp", lambda e: e.dma_start(out=GWf[:], in_=gw_d.rearrange("g p m -> p g m")), writes=[b_GWf])
                S.op("act", lambda e: e.activation(out=GWb[:], in_=GWf[:], func=AF.Copy), reads=[b_GWf],
                     writes=[b_GWb])
                S.barrier()
                inner_.close()
                for (g0_, L_, m_) in SEQS:
                    zready[g0_] = L_
                cgen = cgen0[0] if cgen0[0] is not None else iter(())
                ugen = uvconv_gen(ph) if stop_after >= 5 else iter(())
                debt = [0.0]
                npull = [0]

                def cstep():
                    npull[0] += 1
                    if npull[0] % 6 == 0:
                        next(ugen, None)
                    return next(cgen, None)

                def pull(us):
                    debt[0] += us
                    while debt[0] >= 0.23:
                        debt[0] -= 0.23
                        cstep()
                pbi = 0
                for ch in range(8):
                    for si, (g0, L, mrow) in enumerate(SEQS):
                        TB = 512 if L >= 512 else L
                        S.op("pool", lambda e: e.memset(rxp[:, 0:2], 0.0), writes=[b_rxp])
                        S.op("pool", lambda e, L=L: e.memset(rxp[:, 2 + L:3 + L], 0.0), writes=[b_rxp])
                        S.dma("sp", lambda e, ch=ch, g0=g0, L=L: e.dma_start(
                            out=rxp[:, 2:2 + L], in_=RXs[ch * 128:(ch + 1) * 128, g0:g0 + L]), reads=[bRX],
                            writes=[b_rxp])
                        S.dma("sp", lambda e, ch=ch, g0=g0, L=L: e.dma_start(
                            out=GRt[:, 0:L], in_=GRs[ch * 128:(ch + 1) * 128, g0:g0 + L]), reads=[bGR],
                            writes=[b_GRt])
                        S.op("dve", lambda e, ch=ch, L=L: e.tensor_scalar(
                            out=xr[:, 0:L], in0=rxp[:, 0:L], scalar1=col("w4", ch * 4 + 0), scalar2=col("b4", ch),
                            op0=ALU.mult, op1=ALU.add), reads=[b_rxp, b_cols], writes=[b_xr])
                        for k in range(1, 4):
                            S.op("dve", lambda e, ch=ch, L=L, k=k: e.scalar_tensor_tensor(
                                out=xr[:, 0:L], in0=rxp[:, k:k + L], scalar=col("w4", ch * 4 + k), in1=xr[:, 0:L],
                                op0=ALU.mult, op1=ALU.add), reads=[b_rxp, b_cols, b_xr], writes=[b_xr])
                        S.op("act", lambda e, L=L: e.activation(out=xrb[:, 0:L], in_=xr[:, 0:L], func=AF.Copy),
                             reads=[b_xr], writes=[b_xrb])
                        pull(1.0 * L / 1100.0)
                        for d in range(2):
                            for tb in range(L // TB):
                                cs = slice(tb * TB, (tb + 1) * TB)
                                p1 = pbi % 4
                                p2 = (pbi + 1) % 4
                                pbi += 2
                                S.op("pe", lambda e, d=d, ch=ch, cs=cs, p1=p1, TB=TB: e.matmul(
                                    psb[p1][:, 0:TB], lhsT=GWb[:, d * 16 + ch, :], rhs=xrb[:, cs], start=True,
                                    stop=True), reads=[b_GWb, b_xrb], writes=[b_ps[p1]])
                                S.op("pe", lambda e, d=d, ch=ch, cs=cs, p2=p2, TB=TB: e.matmul(
                                    psb[p2][:, 0:TB], lhsT=GWb[:, d * 16 + 8 + ch, :], rhs=xrb[:, cs], start=True,
                                    stop=True), reads=[b_GWb, b_xrb], writes=[b_ps[p2]])
                                S.op("act", lambda e, d=d, ch=ch, cs=cs, p1=p1, TB=TB: e.activation(
                                    out=At[:, cs], in_=psb[p1][:, 0:TB], func=AF.Sigmoid, bias=col("ba", d * 8 + ch)),
                                    reads=[b_ps[p1], b_cols], writes=[b_A])
                                S.op("act", lambda e, d=d, ch=ch, cs=cs, p2=p2, TB=TB: e.activation(
                                    out=Bt[:, cs], in_=psb[p2][:, 0:TB], func=AF.Sigmoid, bias=col("bx", d * 8 + ch)),
                                    reads=[b_ps[p2], b_cols], writes=[b_B])
                                pull(2.0 * TB / 1100.0)
                            S.op("act", lambda e, d=d, ch=ch, L=L: e.activation(
                                out=Tm[:, 0:L], in_=At[:, 0:L], func=AF.Exp, scale=cdec[:, 16 + d * 8 + ch:17 + d * 8 + ch]),
                                reads=[b_A, b_cdec], writes=[b_T])
                            S.op("act", lambda e, L=L: e.activation(out=Tm[:, 0:L], in_=Tm[:, 0:L], func=AF.Sqrt,
                                                                    scale=-1.0, bias=1.0), reads=[b_T], writes=[b_T])
                            S.op("act", lambda e, d=d, ch=ch, L=L: e.activation(
                                out=At[:, 0:L], in_=At[:, 0:L], func=AF.Exp, scale=cdec[:, d * 8 + ch:d * 8 + ch + 1]),
                                reads=[b_A, b_cdec], writes=[b_A])
                            pull(3.0 * L / 1100.0)
                            S.op("pool", lambda e, L=L: e.tensor_tensor(out=Bt[:, 0:L], in0=Bt[:, 0:L], in1=xr[:, 0:L],
                                                                        op=ALU.mult), reads=[b_B, b_xr], writes=[b_B])
                            S.op("dve", lambda e, L=L: e.tensor_tensor(out=Bt[:, 0:L], in0=Bt[:, 0:L], in1=Tm[:, 0:L],
                                                                       op=ALU.mult), reads=[b_B, b_T], writes=[b_B])
                            if mrow == 1:
                                h0ap = col("h0", d * 8 + ch)
                                h0r = [b_cols]
                            else:
                                h0ap = zcol[:, 0:1]
                                h0r = [b_zcol]
                            if d == 0:
                                S.op("dve", lambda e, L=L, h0ap=h0ap: e.tensor_tensor_scan(
                                    out=HF[:, 0:L], data0=At[:, 0:L], data1=Bt[:, 0:L], initial=h0ap, op0=ALU.mult,
                                    op1=ALU.add), reads=[b_A, b_B] + h0r, writes=[b_HF])
                            else:
                                S.op("dve", lambda e, L=L, h0ap=h0ap: e.tensor_tensor_scan(
                                    out=cap(Tm[:], L - 1, [[-1, L]]), data0=cap(At[:], L - 1, [[-1, L]]),
                                    data1=cap(Bt[:], L - 1, [[-1, L]]), initial=h0ap, op0=ALU.mult, op1=ALU.add),
                                    reads=[b_A, b_B] + h0r, writes=[b_T])
                        if mrow == 0:
                            sidx = si - 1
                            S.op("act", lambda e, sidx=sidx, ch=ch, L=L: e.activation(
                                out=NSt[:, (sidx * 2 + 0) * 8 + ch:(sidx * 2 + 0) * 8 + ch + 1], in_=HF[:, L - 1:L],
                                func=AF.Copy), reads=[b_HF], writes=[b_NSt])
                            S.op("act", lambda e, sidx=sidx, ch=ch: e.activation(
                                out=NSt[:, (sidx * 2 + 1) * 8 + ch:(sidx * 2 + 1) * 8 + ch + 1], in_=Tm[:, 0:1],
                                func=AF.Copy), reads=[b_T], writes=[b_NSt])
                        S.op("dve", lambda e, L=L: e.tensor_tensor(out=HF[:, 0:L], in0=HF[:, 0:L], in1=Tm[:, 0:L],
                                                                   op=ALU.add), reads=[b_HF, b_T], writes=[b_HF])
                        S.op("pool", lambda e, L=L: e.tensor_tensor(out=yrb[:, 0:L], in0=HF[:, 0:L], in1=GRt[:, 0:L],
                                                                    op=ALU.mult), reads=[b_HF, b_GRt], writes=[b_yrb])
                        S.dma("sp", lambda e, ch=ch, g0=g0, L=L: e.dma_start(
                            out=CATR[ch * 128:(ch + 1) * 128, g0:g0 + L], in_=yrb[:, 0:L]), reads=[b_yrb],
                            writes=[bCATR])
                for q in range(4):
                    S.dma("sp", lambda e, q=q: e.dma_start(
                        out=ns_d[q, :].rearrange("(c p) -> p c", p=128), in_=NSt[:, q * 8:(q + 1) * 8]),
                        reads=[b_NSt], writes=[bNS])
                while True:
                    npull[0] += 1
                    if npull[0] % 6 == 0:
                        next(ugen, None)
                    try:
                        next(cgen)
                    except StopIteration:
                        break
                for _ in ugen:
                    pass
                S.barrier()

        cst.close()

        if stop_after >= 3 and os.environ.get("K_SKIP2B2", "0") != "1":
            with ExitStack() as ph:
                WO = sbt(ph, "WO", [128, 16, D], BF16)
                b_WO = Buf()
                catT = [sbt(ph, f"catT{i}", [128, 16, 512], BF16) for i in range(2)]
                b_cat = [Buf() for _ in range(2)]
                g1b = [sbt(ph, f"g1b{r}", [128, D]) for r in range(2)]
                b_g1b = [Buf() for _ in range(2)]
                lg = sbt(ph, "lg", [128, D])
                lb = sbt(ph, "lb", [128, D])
                b_lg = Buf()
                b_lb = Buf()
                tmpt = sbt(ph, "tmpt", [128, 128])
                b_tmpt = Buf()
                xts = [sbt(ph, f"xt{i}", [128, D]) for i in range(3)]
                b_xts = [Buf() for _ in range(3)]
                make_erb(ph, 3)
                ehs = {}
                x1t = [sbt(ph, f"x1t{i}", [128, D]) for i in range(2)]
                b_x1t = [Buf() for _ in range(2)]
                sts = [sbt(ph, f"st{i}", [128, 32]) for i in range(2)]
                b_sts = [Buf() for _ in range(2)]
                u2s = [sbt(ph, f"u2s{i}", [128, 16, 128], BF16) for i in range(2)]
                b_u2s = [Buf() for _ in range(2)]
                for kk in range(16):
                    S.dma("pool", lambda e, kk=kk: e.dma_start(out=WO[:, kk, :], in_=w_out[kk * 128:(kk + 1) * 128, :]),
                          writes=[b_WO])
                for r in range(2):
                    bcast_mod(g1b[r], b_g1b[r], 32, r, tmpt, b_tmpt)
                S.dma("sp", lambda e: e.dma_start(out=lg[:], in_=dram_ap(lnp, 0, [[0, 128], [1, D]])), writes=[b_lg])
                S.dma("sp", lambda e: e.dma_start(out=lb[:], in_=dram_ap(lnp, D, [[0, 128], [1, D]])), writes=[b_lb])
                tiles2 = []
                for bi, (g0, L, mrow, s, T) in enumerate(blocks):
                    for ti in range(T // 128):
                        tiles2.append((bi, ti))

                def bufs(idx):
                    k = idx % 2
                    k3 = idx % 3
                    return (xts[k3], b_xts[k3], x1t[k], b_x1t[k], sts[k], b_sts[k], u2s[k], b_u2s[k])

                def stA(idx):
                    bi, ti = tiles2[idx]
                    g0, L, mrow, s, T = blocks[bi]
                    cT, bcT = catT[bi % 2], b_cat[bi % 2]
                    xt, bxt, x1, bx1, st, bst, u2, b_u2 = bufs(idx)
                    if ti == 0:
                        S.dma("sp", lambda e: e.dma_start(
                            out=cT[:, 0:8, 0:T], in_=CATC[:, g0 + s:g0 + s + T].rearrange("(c p) t -> p c t", p=128)),
                            reads=[bCATC], writes=[bcT])
                        S.dma("sp", lambda e: e.dma_start(
                            out=cT[:, 8:16, 0:T], in_=CATR[:, g0 + s:g0 + s + T].rearrange("(c p) t -> p c t", p=128)),
                            reads=[bCATR], writes=[bcT])
                    load_x_add(xt, bxt, ehs.pop(idx))
                    for nb in range(4):
                        pb = nb
                        for kk in range(16):
                            S.op("pe", lambda e, kk=kk, nb=nb, pb=pb: e.matmul(
                                psb[pb][:], lhsT=cT[:, kk, ti * 128:(ti + 1) * 128],
                                rhs=WO[:, kk, nb * 512:(nb + 1) * 512], start=(kk == 0), stop=(kk == 15)),
                                reads=[bcT, b_WO], writes=[b_ps[pb]])

                def stA0(idx):
                    bi, ti = tiles2[idx]
                    g0, L, mrow, s, T = blocks[bi]
                    xt, bxt = bufs(idx)[0], bufs(idx)[1]
                    tok0 = g0 + s + ti * 128
                    ehs[idx] = load_x_dma(xt, bxt, tok0, mrow, (s + ti * 128) // 128)

                def stC(idx):
                    bi, ti = tiles2[idx]
                    g0, L, mrow, s, T = blocks[bi]
                    xt, bxt, x1, bx1, st, bst, u2, b_u2 = bufs(idx)
                    for nb in range(4):
                        cs = slice(nb * 512, (nb + 1) * 512)
                        S.op("dve", lambda e, nb=nb, cs=cs: e.tensor_tensor(
                            out=x1[:, cs], in0=psb[nb][:], in1=g1b[mrow][:, cs], op=ALU.mult),
                            reads=[b_ps[nb], b_g1b[mrow]], writes=[bx1])
                    S.op("dve", lambda e: e.scalar_tensor_tensor(
                        out=x1[:], in0=xt[:], scalar=ALPHA, in1=x1[:], op0=ALU.mult, op1=ALU.add),
                        reads=[bxt, bx1], writes=[bx1])

                def stB(idx):
                    bi, ti = tiles2[idx]
                    g0, L, mrow, s, T = blocks[bi]
                    xt, bxt, x1, bx1, st, bst, u2, b_u2 = bufs(idx)
                    tok0 = g0 + s + ti * 128
                    ln_stats(x1, bx1, st, bst)
                    S.op("dve", lambda e: e.tensor_scalar(
                        out=x1[:], in0=x1[:], scalar1=st[:, 0:1], scalar2=st[:, 2:3], op0=ALU.subtract,
                        op1=ALU.mult), reads=[bx1, bst], writes=[bx1])
                    S.op("dve", lambda e: e.tensor_tensor(out=x1[:], in0=x1[:], in1=lg[:], op=ALU.mult),
                         reads=[bx1, b_lg], writes=[bx1])
                    S.op("dve", lambda e: e.tensor_tensor(out=x1[:], in0=x1[:], in1=lb[:], op=ALU.add),
                         reads=[bx1, b_lb], writes=[bx1])
                    S.dma("sp", lambda e: e.dma_start(out=X1s[tok0:tok0 + 128, :], in_=x1[:]),
                          reads=[bx1], writes=[bX1])
                    ln_stats(x1, bx1, st, bst)
                    S.op("dve", lambda e: e.tensor_scalar(
                        out=xt[:], in0=x1[:], scalar1=st[:, 0:1], scalar2=st[:, 2:3], op0=ALU.subtract,
                        op1=ALU.mult), reads=[bx1, bst], writes=[bxt])
                    transpose_mod(xt, bxt, u2, b_u2, 0, 64, 48, mrow)
                    S.dma("sp", lambda e: e.dma_start(
                        out=U2T[:, tok0:tok0 + 128].rearrange("(k p) t -> p k t", p=128), in_=u2[:]),
                        reads=[b_u2], writes=[bU2T])

                N2 = len(tiles2)
                stA0(0)
                for n in range(-1, N2):
                    if n + 2 < N2:
                        stA0(n + 2)
                    if n + 1 < N2:
                        stA(n + 1)
                    if n >= 0:
                        stB(n)
                    if n + 1 < N2:
                        stC(n + 1)
                S.barrier()

        if stop_after >= 4:
            with ExitStack() as ph:
                u2 = sbt(ph, "u2p", [128, 16, 512], BF16)
                b_u2 = Buf()
                qTs = [sbt(ph, f"qT{i}", [128, 16, 512], BF16) for i in range(2)]
                b_qTs = [Buf() for _ in range(2)]
                Wq = [sbt(ph, f"Wq{i}", [128, 16, 128], BF16) for i in range(3)]
                b_Wq = [Buf() for _ in range(3)]
                SKT = sbt(ph, "SKT", [128, 16, 128], BF16)
                b_SKT = Buf()
                Sxs = [sbt(ph, f"Sx{i}", [128, 16, 128]) for i in range(2)]
                b_Sxs = [Buf() for _ in range(2)]
                S2 = sbt(ph, "S2", [128, 16, 128])
                b_S2 = Buf()
                sv = sbt(ph, "sv", [128, 16, 16])
                b_sv = Buf()
                sv2 = sbt(ph, "sv2", [128, 16, 8])
                b_sv2 = Buf()
                siu = sbt(ph, "siu", [128, 16, 16], U32)
                b_siu = Buf()
                sif = sbt(ph, "sif", [128, 16, 16])
                b_sif = Buf()
                cand = sbt(ph, "cand", [128, 8, 256])
                b_cand = Buf()
                cand2 = sbt(ph, "cand2", [128, 8, 256])
                b_cand2 = Buf()
                tops = sbt(ph, "tops", [128, 8, 16])
                b_tops = Buf()
                tops2 = sbt(ph, "tops2", [128, 8, 8])
                b_tops2 = Buf()
                posu = sbt(ph, "posu", [128, 8, 16], U32)
                b_posu = Buf()
                posf = sbt(ph, "posf", [128, 128])
                b_posf = Buf()
                thr = sbt(ph, "thr", [128, 16])
                b_thr = Buf()
                kf = sbt(ph, "kf", [128, 2, 128])
                b_kf = Buf()
                eq = sbt(ph, "eq", [128, 128, 16])
                b_eq = Buf()
                ijg = sbt(ph, "ijg", [128, 3, 128])
                b_ijg = Buf()
                zz = sbt(ph, "zz", [128, 16])
                b_zz = Buf()
                negIJG = sbt(ph, "negIJG", [128, 3, 128])
                b_negJ = Buf()
                ijgTs = [sbt(ph, f"ijgT{i}", [128, 3, 128]) for i in range(2)]
                b_ijgTs = [Buf() for _ in range(2)]
                iot = sbt(ph, "iotp", [128, 128])
                b_iot = Buf()
                OHI = [sbt(ph, f"OHI{i}", [128, 32, 128], BF16) for i in range(2)]
                b_OHI = [Buf() for _ in range(2)]
                OHJ = [sbt(ph, f"OHJ{i}", [128, 32, 128], BF16) for i in range(2)]
                b_OHJ = [Buf() for _ in range(2)]
                b_OHI_a = [Buf() for _ in range(2)]
                b_OHJ_a = [Buf() for _ in range(2)]
                ohc = 0
                Gt = sbt(ph, "Gt", [128, 128, 128], BF16)
                b_Gt = Buf()
                S.dma("sp", lambda e: e.dma_start(out=iot[:], in_=iota_d[:, :]), writes=[b_iot])
                iotb = sbt(ph, "iotb", [128, 128], BF16)
                b_iotb = Buf()
                S.op("act", lambda e: e.activation(out=iotb[:], in_=iot[:], func=AF.Copy), reads=[b_iot],
                     writes=[b_iotb])
                S.op("dve", lambda e: e.tensor_scalar(out=thr[:], in0=iot[:, 0:16], scalar1=1.0, scalar2=16.0,
                                                      op0=ALU.add, op1=ALU.mult), reads=[b_iot], writes=[b_thr])
                S.dma("pool", lambda e: e.dma_start(out=SKT[:].rearrange("p a b -> p (a b)"), in_=skt_d[:, :]),
                      writes=[b_SKT])
                ohc_ = [0]
                gp_ = [0]
                tcnt = [0]

                def phaseB(tok0, ijgT, b_ijgT, gen=None):
                    S.op("act", lambda e: e.activation(out=negIJG[:].rearrange("p a b -> p (a b)"),
                                                       in_=ijgT[:].rearrange("p a b -> p (a b)"), func=AF.Copy,
                                                       scale=-1.0), reads=[b_ijgT], writes=[b_negJ])
                    for qt in range(4):
                        c0 = qt * 32
                        ohi, bohi = OHI[ohc_[0] % 2], b_OHI[ohc_[0] % 2]
                        ohj, bohj = OHJ[ohc_[0] % 2], b_OHJ[ohc_[0] % 2]
                        bohi_a, bohj_a = b_OHI_a[ohc_[0] % 2], b_OHJ_a[ohc_[0] % 2]
                        ohc_[0] += 1
                        NACT = 8
                        for cl_ in range(32 - NACT, 32):
                            S.op("act", lambda e, c0=c0, ohi=ohi, cl_=cl_: e.activation(
                                out=ohi[:, cl_, :], in_=iot[:], func=AF.Abs,
                                bias=negIJG[:, 0, c0 + cl_:c0 + cl_ + 1], scale=1.0),
                                reads=[b_iot, b_negJ], writes=[bohi_a])
                            S.op("act", lambda e, c0=c0, ohi=ohi, cl_=cl_: e.activation(
                                out=ohi[:, cl_, :], in_=ohi[:, cl_, :], func=AF.Relu, scale=-1.0, bias=1.0),
                                reads=[bohi_a], writes=[bohi_a])
                            S.op("act", lambda e, c0=c0, ohj=ohj, cl_=cl_: e.activation(
                                out=ohj[:, cl_, :], in_=iot[:], func=AF.Abs,
                                bias=negIJG[:, 1, c0 + cl_:c0 + cl_ + 1], scale=1.0),
                                reads=[b_iot, b_negJ], writes=[bohj_a])
                            S.op("act", lambda e, c0=c0, ohj=ohj, cl_=cl_: e.activation(
                                out=ohj[:, cl_, :], in_=ohj[:, cl_, :], func=AF.Relu,
                                scale=negIJG[:, 2, c0 + cl_:c0 + cl_ + 1], bias=ijgT[:, 2, c0 + cl_:c0 + cl_ + 1]),
                                reads=[bohj_a, b_negJ, b_ijgT], writes=[bohj_a])
                        for cl_ in range(32 - NACT):
                            S.op("dve", lambda e, c0=c0, ohi=ohi, cl_=cl_: e.tensor_scalar(
                                out=ohi[:, cl_, :], in0=iotb[:], scalar1=ijgT[:, 0, c0 + cl_:c0 + cl_ + 1],
                                scalar2=None, op0=ALU.is_equal), reads=[b_iotb, b_ijgT], writes=[bohi])
                            S.op("dve", lambda e, c0=c0, ohj=ohj, cl_=cl_: e.tensor_scalar(
                                out=ohj[:, cl_, :], in0=iotb[:], scalar1=ijgT[:, 1, c0 + cl_:c0 + cl_ + 1],
                                scalar2=ijgT[:, 2, c0 + cl_:c0 + cl_ + 1], op0=ALU.is_equal, op1=ALU.mult),
                                reads=[b_iotb, b_ijgT], writes=[bohj])
                        for c4 in range(8):
                            pb = 5 + (gp_[0] % 3)
                            gp_[0] += 1
                            for j in range(4):
                                cl = c4 * 4 + j
                                S.op("pe", lambda e, cl=cl, j=j, pb=pb, ohi=ohi, ohj=ohj: e.matmul(
                                    psb[pb][:, j * 128:(j + 1) * 128], lhsT=ohj[:, cl, :], rhs=ohi[:, cl, :],
                                    start=True, stop=True),
                                    reads=([bohi_a, bohj_a] if cl >= 24 else [bohi, bohj]), writes=[b_ps[pb]])
                            cg = c0 + c4 * 4
                            S.op("act", lambda e, cg=cg, pb=pb: e.activation(
                                out=cap(Gt[:], cg, [[128, 128], [1, 4]]),
                                in_=cap(psb[pb][:], 0, [[1, 128], [128, 4]]), func=AF.Copy),
                                reads=[b_ps[pb]], writes=[b_Gt])
                        if gen is not None:
                            next(gen, None)
                    for q in range(8):
                        S.dma("sp", lambda e, q=q, tok0=tok0: e.dma_start(
                            out=Gs[q * 16:(q + 1) * 16, :, tok0:tok0 + 128].rearrange("i j c -> j i c"),
                            in_=Gt[:, q * 16:(q + 1) * 16, :]), reads=[b_Gt], writes=[bG])

                pendingB = [None]
                wqi = 0
                gp = 0
                tiles3 = [(b0_, t_) for b0_ in range(0, NT, 512) for t_ in range(4)]
                wqi_ = [0]

                def qproj(bI, hps):
                    blk0 = bI * 512
                    qT, b_qT = qTs[bI % 2], b_qTs[bI % 2]
                    if hps[0] == 0:
                        S.dma("sp", lambda e: e.dma_start(
                            out=u2[:], in_=U2T[:, blk0:blk0 + 512].rearrange("(k p) t -> p k t", p=128)),
                            reads=[bU2T], writes=[b_u2])
                    for hp in hps:
                        wq, bwq = Wq[wqi_[0] % 3], b_Wq[wqi_[0] % 3]
                        wqi_[0] += 1
                        S.dma("pool", lambda e, wq=wq, hp=hp: e.dma_start(
                            out=wq[:].rearrange("p a b -> p (a b)"), in_=wqh[hp, :, :]), writes=[bwq])
                        pb = hp % 2
                        for kk in range(16):
                            S.op("pe", lambda e, wq=wq, kk=kk, pb=pb: e.matmul(
                                psb[pb][:], lhsT=wq[:, kk, :], rhs=u2[:, kk, :], start=(kk == 0), stop=(kk == 15)),
                                reads=[bwq, b_u2], writes=[b_ps[pb]])
                        S.op("act", lambda e, hp=hp, pb=pb: e.activation(out=qT[:, hp, :], in_=psb[pb][:],
                                                                         func=AF.Copy), reads=[b_ps[pb]],
                             writes=[b_qT])

                def A1(n):
                    blk0, ti = tiles3[n]
                    Sx, b_Sx = Sxs[n % 2], b_Sxs[n % 2]
                    qT, b_qT = qTs[(n // 4) % 2], b_qTs[(n // 4) % 2]
                    if n == 0:
                        qproj(0, list(range(16)))
                    cs = slice(ti * 128, (ti + 1) * 128)
                    for q in range(4):
                        pb = 2 + (q % 2)
                        for j in range(4):
                            hp = q * 4 + j
                            S.op("pe", lambda e, hp=hp, j=j, pb=pb, cs=cs: e.matmul(
                                psb[pb][:, j * 128:(j + 1) * 128], lhsT=qT[:, hp, cs], rhs=SKT[:, hp, :],
                                start=True, stop=True), reads=[b_qT, b_SKT], writes=[b_ps[pb]])
                        S.op("act", lambda e, q=q, pb=pb: e.activation(
                            out=Sx[:, q * 4:(q + 1) * 4, :].rearrange("p a b -> p (a b)"), in_=psb[pb][:],
                            func=AF.Copy), reads=[b_ps[pb]], writes=[b_Sx])

                def A2(n):
                    blk0, ti = tiles3[n]
                    Sx, b_Sx = Sxs[n % 2], b_Sxs[n % 2]
                    ijgT, b_ijgT = ijgTs[n % 2], b_ijgTs[n % 2]
                    for hp in range(16):
                        S.op("dve", lambda e, hp=hp: e.max(out=sv[:, hp, 0:8], in_=Sx[:, hp, :]), reads=[b_Sx],
                             writes=[b_sv])
                    for hp in range(16):
                        S.op("dve", lambda e, hp=hp: e.max_index(out=siu[:, hp, 0:8], in_max=sv[:, hp, 0:8],
                                                                 in_values=Sx[:, hp, :]), reads=[b_Sx, b_sv],
                             writes=[b_siu])
                    for hp in range(16):
                        S.op("dve", lambda e, hp=hp: e.match_replace(
                            out=S2[:, hp, :], in_to_replace=sv[:, hp, 0:8], in_values=Sx[:, hp, :],
                            imm_value=-1e30), reads=[b_Sx, b_sv], writes=[b_S2])
                    yield
                    for hp in range(16):
                        S.op("dve", lambda e, hp=hp: e.max(out=sv2[:, hp, 0:8], in_=S2[:, hp, :]), reads=[b_S2],
                             writes=[b_sv2])
                    for hp in range(16):
                        S.op("dve", lambda e, hp=hp: e.max_index(out=siu[:, hp, 8:16], in_max=sv2[:, hp, 0:8],
                                                                 in_values=S2[:, hp, :]), reads=[b_S2, b_sv2],
                             writes=[b_siu])
                    S.op("dve", lambda e: e.tensor_copy(out=sv[:, :, 8:16], in_=sv2[:, :, 0:8]), reads=[b_sv2],
                         writes=[b_sv])
                    S.op("dve", lambda e: e.tensor_copy(out=sif[:], in_=siu[:]), reads=[b_siu], writes=[b_sif])
                    S.op("dve", lambda e: e.tensor_tensor(
                        out=cap(cand[:], 0, [[256, 8], [16, 16], [1, 16]]),
                        in0=cap(sv[:], 0, [[32, 8], [1, 16], [0, 16]]),
                        in1=cap(sv[:], 16, [[32, 8], [0, 16], [1, 16]]), op=ALU.add), reads=[b_sv],
                        writes=[b_cand])
                    yield
                    for h in range(8):
                        S.op("dve", lambda e, h=h: e.max(out=tops[:, h, 0:8], in_=cand[:, h, :]), reads=[b_cand],
                             writes=[b_tops])
                    for h in range(8):
                        S.op("dve", lambda e, h=h: e.max_index(out=posu[:, h, 0:8], in_max=tops[:, h, 0:8],
                                                               in_values=cand[:, h, :]), reads=[b_cand, b_tops],
                             writes=[b_posu])
                    for h in range(8):
                        S.op("dve", lambda e, h=h: e.match_replace(
                            out=cand2[:, h, :], in_to_replace=tops[:, h, 0:8], in_values=cand[:, h, :],
                            imm_value=-1e30), reads=[b_cand, b_tops], writes=[b_cand2])
                    for h in range(8):
                        S.op("dve", lambda e, h=h: e.max(out=tops2[:, h, 0:8], in_=cand2[:, h, :]),
                             reads=[b_cand2], writes=[b_tops2])
                    for h in range(8):
                        S.op("dve", lambda e, h=h: e.max_index(out=posu[:, h, 8:16], in_max=tops2[:, h, 0:8],
                                                               in_values=cand2[:, h, :]),
                             reads=[b_cand2, b_tops2], writes=[b_posu])
                    S.op("dve", lambda e: e.tensor_copy(out=tops[:, :, 8:16], in_=tops2[:, :, 0:8]),
                         reads=[b_tops2], writes=[b_tops])
                    yield
                    S.op("dve", lambda e: e.tensor_copy(out=posf[:], in_=posu[:].rearrange("p a b -> p (a b)")),
                         reads=[b_posu], writes=[b_posf])
                    S.op("dve", lambda e: e.tensor_tensor(
                        out=eq[:], in0=cap(posf[:], 0, [[1, 128], [0, 16]]),
                        in1=cap(thr[:], 0, [[0, 128], [1, 16]]), op=ALU.is_ge), reads=[b_posf, b_thr],
                        writes=[b_eq])
                    S.op("dve", lambda e: e.tensor_reduce(out=kf[:, 0, :], in_=eq[:], axis=AX.X, op=ALU.add),
                         reads=[b_eq], writes=[b_kf])
                    S.op("dve", lambda e: e.scalar_tensor_tensor(
                        out=kf[:, 1, :], in0=kf[:, 0, :], scalar=-16.0, in1=posf[:], op0=ALU.mult, op1=ALU.add),
                        reads=[b_kf, b_posf], writes=[b_kf])
                    for w in range(2):
                        S.op("dve", lambda e, w=w: e.tensor_tensor(
                            out=eq[:], in0=cap(kf[:], w * 128, [[1, 128], [0, 16]]),
                            in1=cap(iot[:], 0, [[0, 128], [1, 16]]), op=ALU.is_equal), reads=[b_kf, b_iot],
                            writes=[b_eq])
                        S.op("dve", lambda e, w=w: e.tensor_tensor(
                            out=cap(eq[:], 0, [[256, 8], [16, 16], [1, 16]]),
                            in0=cap(eq[:], 0, [[256, 8], [16, 16], [1, 16]]),
                            in1=cap(sif[:], w * 16, [[32, 8], [0, 16], [1, 16]]), op=ALU.mult),
                            reads=[b_eq, b_sif], writes=[b_eq])
                        S.op("dve", lambda e, w=w: e.tensor_reduce(out=ijg[:, w, :], in_=eq[:], axis=AX.X,
                                                                   op=ALU.add), reads=[b_eq], writes=[b_ijg])
                    S.op("dve", lambda e: e.tensor_tensor(
                        out=cap(cand2[:], 0, [[16, 8], [1, 16]]), in0=cap(tops[:], 0, [[16, 8], [1, 16]]),
                        in1=cap(tops[:], 0, [[16, 8], [0, 16]]), op=ALU.subtract), reads=[b_tops],
                        writes=[b_cand2])
                    S.op("act", lambda e: e.activation(out=cand2[:, 0, 128:256], in_=cand2[:, 0, 0:128],
                                                       func=AF.Exp), reads=[b_cand2], writes=[b_cand2])
                    S.op("dve", lambda e: e.tensor_reduce(
                        out=zz[:, 0:8], in_=cap(cand2[:], 128, [[16, 8], [1, 16]]), axis=AX.X, op=ALU.add),
                        reads=[b_cand2], writes=[b_zz])
                    S.op("dve", lambda e: e.reciprocal(out=zz[:, 8:16], in_=zz[:, 0:8]), reads=[b_zz],
                         writes=[b_zz])
                    S.op("dve", lambda e: e.tensor_tensor(
                        out=cap(ijg[:], 256, [[16, 8], [1, 16]]), in0=cap(cand2[:], 128, [[16, 8], [1, 16]]),
                        in1=cap(zz[:], 8, [[1, 8], [0, 16]]), op=ALU.mult), reads=[b_cand2, b_zz],
                        writes=[b_ijg])
                    for w in range(3):
                        S.op("pe", lambda e, w=w: e.transpose(out=psb[4][:, w * 128:(w + 1) * 128],
                                                              in_=ijg[:, w, :], identity=ident[:]),
                             reads=[b_ijg, b_ident], writes=[b_ps[4]])
                    S.op("act", lambda e: e.activation(out=ijgT[:].rearrange("p a b -> p (a b)"),
                                                       in_=psb[4][:, 0:384], func=AF.Copy), reads=[b_ps[4]],
                         writes=[b_ijgT])

                NT3 = len(tiles3)
                for n in range(-1, NT3 + 1):
                    if 0 <= n + 1 < NT3:
                        A1(n + 1)
                    gen = A2(n) if 0 <= n < NT3 else None
                    if 0 <= n - 1 < NT3:
                        phaseB(tiles3[n - 1][0] + tiles3[n - 1][1] * 128, ijgTs[(n - 1) % 2], b_ijgTs[(n - 1) % 2], gen)
                    if gen is not None:
                        for _ in gen:
                            pass
                    m_ = n + 1
                    if 0 <= m_ and (m_ // 4 + 1) * 4 < NT3:
                        qproj(m_ // 4 + 1, [4 * (m_ % 4) + x_ for x_ in range(4)])
                S.barrier()

        if stop_after >= 5:
            with ExitStack() as ph:
                TG = 512
                EG = 4
                PF = 3
                u2s4 = [sbt(ph, f"u2q{i}", [128, 16, TG], BF16) for i in range(2)]
                b_u2s4 = [Buf() for _ in range(2)]
                ACC = sbt(ph, "ACC", [128, TG // 128, D])
                b_ACC = [Buf() for _ in range(TG // 128)]
                NU = 5
                Ub = [sbt(ph, f"Ub{i}", [128, 16, 128], BF16) for i in range(NU)]
                b_Ub = [Buf() for _ in range(NU)]
                NV = 8
                Vb = [sbt(ph, f"Vb{i}", [128, D], BF16) for i in range(NV)]
                b_Vb = [Buf() for _ in range(NV)]
                Gb = [sbt(ph, f"Gb{i}", [128, TG], BF16) for i in range(NU)]
                b_Gb = [Buf() for _ in range(NU)]
                Ab = [sbt(ph, f"Ab{i}", [128, TG], BF16) for i in range(2)]
                b_Ab = [Buf() for _ in range(2)]
                Wt = [sbt(ph, f"Wt{i}", [128, EG, TG], BF16) for i in range(2)]
                b_Wt = [Buf() for _ in range(2)]
                g2b = [sbt(ph, f"g2b{r}", [128, D]) for r in range(2)]
                b_g2b = [Buf() for _ in range(2)]
                lg = sbt(ph, "lg2", [128, D])
                lb = sbt(ph, "lb2", [128, D])
                b_lg = Buf()
                b_lb = Buf()
                tmpt = sbt(ph, "tmpt2", [128, 128])
                b_tmpt = Buf()
                x1t = [sbt(ph, f"x1q{i}", [128, D]) for i in range(4)]
                b_x1t = [Buf() for _ in range(4)]
                sts = [sbt(ph, f"stq{i}", [128, 32]) for i in range(4)]
                b_sts = [Buf() for _ in range(4)]
                for r in range(2):
                    bcast_mod(g2b[r], b_g2b[r], 80, r, tmpt, b_tmpt)
                S.dma("sp", lambda e: e.dma_start(out=lg[:], in_=dram_ap(lnp, 2 * D, [[0, 128], [1, D]])),
                      writes=[b_lg])
                S.dma("sp", lambda e: e.dma_start(out=lb[:], in_=dram_ap(lnp, 3 * D, [[0, 128], [1, D]])),
                      writes=[b_lb])
                ei_ = [0]
                pend_epi = []

                def epilogue_tile(g0, t, mrow):
                    tok0 = g0 + t * 128
                    x1, bx1 = x1t[ei_[0] % 4], b_x1t[ei_[0] % 4]
                    st, bst = sts[ei_[0] % 4], b_sts[ei_[0] % 4]
                    ei_[0] += 1
                    S.dma("sp", lambda e: e.dma_start(out=x1[:], in_=X1s[tok0:tok0 + 128, :]), reads=[bX1],
                          writes=[bx1])
                    S.op("dve", lambda e: e.tensor_tensor(out=ACC[:, t, :], in0=ACC[:, t, :], in1=g2b[mrow][:],
                                                          op=ALU.mult), reads=[b_ACC[t], b_g2b[mrow]],
                         writes=[b_ACC[t]])
                    S.op("dve", lambda e: e.scalar_tensor_tensor(
                        out=x1[:], in0=x1[:], scalar=ALPHA, in1=ACC[:, t, :], op0=ALU.mult, op1=ALU.add),
                        reads=[bx1, b_ACC[t]], writes=[bx1])
                    yield
                    for q in range(4):
                        S.op("dve", lambda e, q=q: e.bn_stats(out=st[:, 8 + q * 6:8 + (q + 1) * 6],
                                                              in_=x1[:, q * 512:(q + 1) * 512]),
                             reads=[bx1], writes=[bst])
                        yield
                    S.op("dve", lambda e: e.bn_aggr(out=st[:, 0:2], in_=st[:, 8:32]), reads=[bst], writes=[bst])
                    S.op("act", lambda e: e.activation(out=st[:, 2:3], in_=st[:, 1:2], func=AF.Sqrt, bias=EPS, scale=1.0),
                         reads=[bst], writes=[bst])
                    S.op("dve", lambda e: e.reciprocal(out=st[:, 2:3], in_=st[:, 2:3]), reads=[bst], writes=[bst])
                    yield
                    S.op("dve", lambda e: e.tensor_scalar(
                        out=x1[:], in0=x1[:], scalar1=st[:, 0:1], scalar2=st[:, 2:3], op0=ALU.subtract,
                        op1=ALU.mult), reads=[bx1, bst], writes=[bx1])
                    yield
                    S.op("dve", lambda e: e.tensor_tensor(out=x1[:], in0=x1[:], in1=lg[:], op=ALU.mult),
                         reads=[bx1, b_lg], writes=[bx1])
                    yield
                    S.op("dve", lambda e: e.tensor_tensor(out=x1[:], in0=x1[:], in1=lb[:], op=ALU.add),
                         reads=[bx1, b_lb], writes=[bx1])
                    S.dma("sp", lambda e: e.dma_start(out=y_all[tok0:tok0 + 128, :], in_=x1[:]), reads=[bx1],
                          writes=[bY])

                epi_run = []
                ob_ = [0]

                def epi_step():
                    while epi_run:
                        try:
                            next(epi_run[0])
                            return
                        except StopIteration:
                            epi_run.pop(0)

                groups4 = list(range(0, NT, TG))
                NG = len(groups4)
                ntile = TG // 128
                total4 = NG * 128

                def load_u2(gi):
                    g0 = groups4[gi]
                    u2_, bu2_ = u2s4[gi % 2], b_u2s4[gi % 2]
                    S.dma("sp", lambda e: e.dma_start(
                        out=u2_[:], in_=U2T[:, g0:g0 + TG].rearrange("(k p) t -> p k t", p=128)), reads=[bU2T],
                        writes=[bu2_])

                def vstage(gj, grp):
                    wt, bwt = Wt[grp % 2], b_Wt[grp % 2]
                    for t in range(ntile):
                        for nb in range(4):
                            pb = 2 + (ob_[0] % 6)
                            ob_[0] += 1
                            for e_ in range(EG):
                                fj_ = gj * 128 + grp * EG + e_
                                vb, bvb = Vb[fj_ % NV], b_Vb[fj_ % NV]
                                S.op("pe", lambda e, e_=e_, vb=vb: e.matmul(
                                    psb[pb][:], lhsT=wt[:, e_, t * 128:(t + 1) * 128],
                                    rhs=vb[:, nb * 512:(nb + 1) * 512], start=(e_ == 0), stop=(e_ == EG - 1)),
                                    reads=[bwt, bvb], writes=[b_ps[pb]])
                            if grp == 0:
                                S.op("dve", lambda e: e.tensor_copy(
                                    out=ACC[:, t, nb * 512:(nb + 1) * 512], in_=psb[pb][:]),
                                    reads=[b_ps[pb]], writes=[b_ACC[t]])
                            else:
                                S.op("dve", lambda e: e.tensor_tensor(
                                    out=ACC[:, t, nb * 512:(nb + 1) * 512], in0=psb[pb][:],
                                    in1=ACC[:, t, nb * 512:(nb + 1) * 512], op=ALU.add),
                                    reads=[b_ps[pb], b_ACC[t]], writes=[b_ACC[t]])

                load_u2(0)
                for f in range(total4 + PF):
                    if f < total4:
                        gi, i = divmod(f, 128)
                        g0 = groups4[gi]
                        if i == 100 and gi + 1 < NG:
                            load_u2(gi + 1)
                        ub, bub = Ub[f % NU], b_Ub[f % NU]
                        vb, bvb = Vb[f % NV], b_Vb[f % NV]
                        gb, bgb = Gb[f % NU], b_Gb[f % NU]
                        S.dma("sp", lambda e: e.dma_start(
                            out=ub[:].rearrange("p a b -> p (a b)"), in_=UTB[i, :, :]), reads=[bUTB], writes=[bub])
                        S.dma("sp", lambda e: e.dma_start(
                            out=vb[:], in_=PVB[i * 128:(i + 1) * 128, :]), reads=[bPVB], writes=[bvb])
                        S.dma("sp", lambda e: e.dma_start(
                            out=gb[:], in_=Gs[i, :, g0:g0 + TG]), reads=[bG], writes=[bgb])
                    fj = f - PF
                    if fj < 0:
                        continue
                    gj, j = divmod(fj, 128)
                    u2, b_u2 = u2s4[gj % 2], b_u2s4[gj % 2]
                    if pend_epi and j < 4:
                        g_ = epilogue_tile(*pend_epi.pop(0))
                        next(g_)
                        epi_run.append(g_)
                        if j == 3:
                            while pend_epi:
                                g_ = epilogue_tile(*pend_epi.pop(0))
                                next(g_)
                                epi_run.append(g_)
                    elif j % 2 == 0:
                        epi_step()
                    ub, bub = Ub[fj % NU], b_Ub[fj % NU]
                    gb, bgb = Gb[fj % NU], b_Gb[fj % NU]
                    pa = fj % 2
                    for kk in range(16):
                        S.op("pe", lambda e, kk=kk: e.matmul(
                            psb[pa][:], lhsT=ub[:, kk, :], rhs=u2[:, kk, :], start=(kk == 0), stop=(kk == 15)),
                            reads=[bub, b_u2], writes=[b_ps[pa]])
                    ab, bab = Ab[fj % 2], b_Ab[fj % 2]
                    S.op("act", lambda e: e.activation(out=ab[:], in_=psb[pa][:], func=AF.Gelu_apprx_tanh),
                         reads=[b_ps[pa]], writes=[bab])
                    grp = j // EG
                    wt, bwt = Wt[grp % 2], b_Wt[grp % 2]
                    S.op("dve", lambda e: e.tensor_tensor(
                        out=wt[:, j % EG, :], in0=ab[:], in1=gb[:], op=ALU.mult), reads=[bab, bgb], writes=[bwt])
                    if j >= 1 and (j - 1) % EG == EG - 1:
                        vstage(gj, (j - 1) // EG)
                    if j == 127:
                        vstage(gj, 127 // EG)
                        g0j = groups4[gj]
                        pend_epi = [(g0j, t, 1 if g0j < 4096 else 0) for t in range(ntile)]
                for (eg0, et, emrow) in pend_epi:
                    g_ = epilogue_tile(eg0, et, emrow)
                    next(g_)
                    epi_run.append(g_)
                while epi_run:
                    epi_step()
                S.barrier()
        S.barrier()
    return nc


def _host_layout(inp, core):
    f = np.float32
    g = {}
    x_all = np.concatenate([inp["x_sample"][core], inp["x_prompt"][2 * core], inp["x_prompt"][2 * core + 1]], axis=0)
    g["x_all"] = np.ascontiguousarray(x_all, dtype=f)
    cols = np.zeros((128, NCOL), f)

    def put(name, arr):
        o, w = COLS[name]
        assert arr.shape == (128, w), (name, arr.shape)
        cols[:, o:o + w] = arr

    cv = np.stack([inp["c_ctx"], inp["c"][core]], axis=0)
    put("cvT", cv.reshape(2, 16, 128).transpose(2, 1, 0).reshape(128, 32))
    put("badaT", inp["b_ada"][0].reshape(96, 128).T)
    put("conv_w", inp["conv_w"][0].reshape(31, 8, 128).transpose(2, 1, 0).reshape(128, 248))
    put("conv_b", inp["conv_b"][0].reshape(8, 128).T)
    put("cln_g", inp["conv_ln_g"][0].reshape(8, 128).T)
    put("cln_b", inp["conv_ln_b"][0].reshape(8, 128).T)
    put("w4", inp["lru_conv_w"][0].reshape(4, 8, 128).transpose(2, 1, 0).reshape(128, 32))
    put("b4", inp["lru_conv_b"][0].reshape(8, 128).T)
    put("ba", inp["lru_ba"][0].reshape(2, 8, 128).transpose(2, 0, 1).reshape(128, 16))
    put("bx", inp["lru_bx"][0].reshape(2, 8, 128).transpose(2, 0, 1).reshape(128, 16))
    put("lam", inp["lru_lam"][0].reshape(2, 8, 128).transpose(2, 0, 1).reshape(128, 16))
    put("h0", inp["state_rglru"][core, 0].reshape(2, 8, 128).transpose(2, 0, 1).reshape(128, 16))
    g["cols"] = cols
    return g


_SHARED = {}


def _shared_layout(inp):
    f = np.float32
    s = {}
    s["w_ada"] = np.ascontiguousarray(inp["w_ada"][0], dtype=f)
    w_in = inp["w_in"][0]
    s["wih"] = np.ascontiguousarray(w_in.reshape(16, 128, 32, 128).transpose(2, 1, 0, 3).reshape(32, 128, 2048))
    gw = np.zeros((2, 2, 8, 128, 128), f)
    for d in range(2):
        for gi, nm in enumerate(("lru_wa", "lru_wx")):
            w = inp[nm][0, d]
            for ch in range(8):
                gw[d, gi, ch, 0:64, 0:64] = w[2 * ch]
                gw[d, gi, ch, 64:128, 64:128] = w[2 * ch + 1]
    s["gw"] = gw.reshape(32, 128, 128)
    s["w_out"] = np.ascontiguousarray(inp["w_out"][0], dtype=f)
    s["lnp"] = np.ascontiguousarray(np.stack([inp["ln1_g"][0], inp["ln1_b"][0], inp["ln2_g"][0], inp["ln2_b"][0]]))
    wq = inp["w_query"][0]
    s["wqh"] = np.ascontiguousarray(wq.reshape(16, 128, 16, 128).transpose(2, 1, 0, 3).reshape(16, 128, 2048))
    sk = inp["sub_keys"][0].reshape(16, 128, 128)
    s["skt"] = np.ascontiguousarray(sk.transpose(2, 0, 1).reshape(128, 2048))
    U = inp["peer_u"][0]
    s["uth"] = np.ascontiguousarray(U.reshape(128, 128, 16, 128).transpose(0, 3, 2, 1).reshape(128, 128, 2048))
    s["pv"] = np.ascontiguousarray(inp["peer_v"][0], dtype=f)
    s["ident"] = np.eye(128, dtype=f)
    sel = np.zeros((64, 4096), f)
    for ti in range(32):
        for m in range(128):
            sel[2 * ti + (m >> 6), ti * 128 + m] = 1.0
    s["sel"] = sel
    s["iota"] = np.tile(np.arange(128, dtype=f)[None, :], (128, 1))
    return s


def kernel(**inputs):
    inp = {k: np.asarray(v) for k, v in inputs.items()}
    stop_after = int(os.environ.get("K_STOP", "99"))
    debug = os.environ.get("K_DEBUG", "0") == "1"
    nc = build_nc(stop_after=stop_after, debug=debug)
    shared = _shared_layout(inp)
    in_maps = []
    for core in range(NCORES):
        m = dict(shared)
        m.update(_host_layout(inp, core))
        in_maps.append(m)
    res = run_bass_kernel_spmd(nc, in_maps, core_ids=list(range(NCORES)))
    if debug:
        kernel.last = res
    y_p = np.zeros((16, 256, D), np.float32)
    y_s = np.zeros((8, 4096, D), np.float32)
    ns = np.zeros((16, 1, 2, 1024), np.float32)
    for core in range(NCORES):
        r = res.results[core]
        ya = np.asarray(r["y_all"])
        y_s[core] = ya[0:4096]
        y_p[2 * core] = ya[4096:4352]
        y_p[2 * core + 1] = ya[4352:4608]
        nsr = np.asarray(r["ns"]).reshape(2, 2, 1024)
        ns[2 * core, 0] = nsr[0]
        ns[2 * core + 1, 0] = nsr[1]
    return (y_p, y_s, ns)
```

```python
import os
from contextlib import ExitStack
import numpy as np
import concourse.bass as bass
import concourse.mybir as mybir
from concourse.bass_utils import run_bass_kernel_spmd

F32 = mybir.dt.float32
BF16 = mybir.dt.bfloat16
U32 = mybir.dt.uint32
I32 = mybir.dt.int32
AF = mybir.ActivationFunctionType
ALU = mybir.AluOpType
AX = mybir.AxisListType

D = 2048
NT = 4608
SEQS = [(0, 4096, 1), (4096, 256, 0), (4352, 256, 0)]
ALPHA = 2.0 ** 0.25
EPS = 1e-6
NCORES = 8

COLS = {}
_off = 0
for _n, _w in [("cvT", 32), ("badaT", 96), ("conv_w", 8 * 31), ("conv_b", 8), ("cln_g", 8), ("cln_b", 8),
               ("w4", 32), ("b4", 8), ("ba", 16), ("bx", 16), ("lam", 16), ("h0", 16)]:
    COLS[_n] = (_off, _w)
    _off += _w
NCOL = _off


class Buf:
    __slots__ = ("w", "r", "multi", "ws")

    def __init__(self, multi=False):
        self.w = None
        self.r = {}
        self.multi = multi
        self.ws = {}


class Sched:
    EPOCH = 20000

    def __init__(self, nc, stack):
        self.nc = nc
        self.stack = stack
        self.engs = {"pe": nc.tensor, "dve": nc.vector, "act": nc.scalar, "pool": nc.gpsimd, "sp": nc.sync}
        self.csem = {}
        self.ccnt = {}
        self.waited = {e: {} for e in self.engs}
        self.nsem = 0
        self.allsems = []
        self.sem_owner = {}
        for e in ("pe", "dve", "act", "pool"):
            self._new_csem(e)
        self.slots = {}
        self.slot_i = {}
        for q, k in (("sp", 16), ("pool", 8), ("act", 4)):
            self.slots[q] = [[self._sem(f"d{q}{i}"), 0] for i in range(k)]
            self.slot_i[q] = 0

    def _same_eng(self, e, ev):
        return self.sem_owner.get(id(ev[0])) == e

    def _sem(self, name):
        self.nsem += 1
        s = self.stack.enter_context(self.nc.semaphore(f"{name}_{self.nsem}"))
        self.allsems.append(s)
        return s

    def _new_csem(self, e):
        self.csem[e] = self._sem("c" + e)
        self.sem_owner[id(self.csem[e])] = e
        self.ccnt[e] = 0

    def _wait(self, e, ev):
        sem, val = ev
        k = id(sem)
        if self.waited[e].get(k, 0) >= val:
            return
        self.engs[e].wait_ge(sem, val)
        self.waited[e][k] = val

    def _deps(self, e, reads, writes):
        for b in reads:
            if b.multi:
                for ev in list(b.ws.values()):
                    self._wait(e, ev)
                continue
            if b.w is not None:
                if not (e == "pe" and self._same_eng(e, b.w)):
                    self._wait(e, b.w)
        for b in writes:
            if b.w is not None and not b.multi:
                if not self._same_eng(e, b.w):
                    self._wait(e, b.w)
            for ev in list(b.r.values()):
                if self._same_eng(e, ev):
                    continue
                self._wait(e, ev)

    def _mark(self, ev, reads, writes):
        for b in reads:
            k = id(ev[0])
            if k not in b.r or b.r[k][1] < ev[1]:
                b.r[k] = ev
        for b in writes:
            if b.multi:
                k = id(ev[0])
                if k not in b.ws or b.ws[k][1] < ev[1]:
                    b.ws[k] = ev
                continue
            b.w = ev
            b.r = {}

    def op(self, e, fn, reads=(), writes=()):
        self._deps(e, reads, writes)
        inst = fn(self.engs[e])
        self.ccnt[e] += 1
        inst.then_inc(self.csem[e], 1)
        ev = (self.csem[e], self.ccnt[e])
        if self.ccnt[e] >= self.EPOCH:
            self._new_csem(e)
        self._mark(ev, reads, writes)
        return ev

    def dma(self, e, fn, reads=(), writes=()):
        self._deps(e, reads, writes)
        sl = self.slots[e]
        slot = sl[self.slot_i[e] % len(sl)]
        self.slot_i[e] += 1
        sem, uses = slot
        if uses > 0:
            self._wait(e, (sem, 16 * uses))
        inst = fn(self.engs[e])
        inst.then_inc(sem, 16)
        slot[1] = uses + 1
        ev = (sem, 16 * (uses + 1))
        self._mark(ev, reads, writes)
        return ev

    def barrier(self):
        evs = []
        for e in ("pe", "dve", "act", "pool"):
            if self.ccnt[e] > 0:
                evs.append((self.csem[e], self.ccnt[e]))
        for q in self.slots:
            for sem, uses in self.slots[q]:
                if uses > 0:
                    evs.append((sem, 16 * uses))
        for e in self.engs:
            for ev in evs:
                self._wait(e, ev)


def cap(full_ap, off, dims):
    return bass.AP(full_ap.tensor, full_ap.offset + off, [list(full_ap.ap[0])] + [list(d) for d in dims])


def dram_ap(full_ap, off, dims):
    return bass.AP(full_ap.tensor, full_ap.offset + off, [list(d) for d in dims])


def build_nc(stop_after=99, debug=False):
    nc = bass.Bass("TRN2", target_bir_lowering=False)
    okind = "ExternalOutput" if debug else "Internal"

    def din(name, shape, dt=F32):
        return nc.dram_tensor(name, list(shape), dt, kind="ExternalInput").ap()

    x_all = din("x_all", [NT, D])
    cols_d = din("cols", [128, NCOL])
    w_ada = din("w_ada", [D, 6 * D])
    wih = din("wih", [32, 128, 2048])
    gw_d = din("gw", [32, 128, 128])
    w_out = din("w_out", [D, D])
    lnp = din("lnp", [4, D])
    wqh = din("wqh", [16, 128, 2048])
    skt_d = din("skt", [128, 16 * 128])
    uth = din("uth", [128, 128, 2048])
    pv = din("pv", [16384, D])
    ident_d = din("ident", [128, 128])
    sel_d = din("sel", [64, 4096])
    iota_d = din("iota", [128, 128])

    y_all = nc.dram_tensor("y_all", [NT, D], F32, kind="ExternalOutput").ap()
    ns_d = nc.dram_tensor("ns", [4, 1024], F32, kind="ExternalOutput").ap()

    Zs = nc.dram_tensor("Zs", [1024, NT], F32, kind=okind).ap()
    CATC = nc.dram_tensor("CATC", [1024, NT], BF16, kind=okind).ap()
    bCATC = Buf(multi=True)
    RXs = nc.dram_tensor("RXs", [1024, NT], F32, kind=okind).ap()
    GRs = nc.dram_tensor("GRs", [1024, NT], F32, kind=okind).ap()
    CATR = nc.dram_tensor("CATR", [1024, NT], BF16, kind=okind).ap()
    X1s = nc.dram_tensor("X1s", [NT, D], F32, kind=okind).ap()
    U2T = nc.dram_tensor("U2T", [D, NT], BF16, kind=okind).ap()
    Gs = nc.dram_tensor("Gs", [128, 128, NT], BF16, kind="Internal").ap()
    UTB = nc.dram_tensor("UTB", [128, 128, 2048], BF16, kind="Internal").ap()
    PVB = nc.dram_tensor("PVB", [16384, D], BF16, kind="Internal").ap()
    bUTB = Buf(multi=True)
    bPVB = Buf(multi=True)
    ERs = nc.dram_tensor("ERs", [64, 1024], F32, kind="Internal").ap()
    bERs = Buf()
    bZ, bRX, bGR, bCATR, bX1, bU2T, bG, bY, bNS = (Buf(multi=True) for _ in range(9))

    with ExitStack() as top:
        top.enter_context(nc.allow_non_contiguous_dma(reason="small column/state transfers"))
        S = Sched(nc, top)

        uniq = [0]

        def sbt(stack, name, shape, dt=F32):
            uniq[0] += 1
            return stack.enter_context(nc.sbuf_tensor(f"s{uniq[0]}_{name}", list(shape), dt))

        cols = sbt(top, "cols", [128, NCOL])
        b_cols = Buf()
        ident = sbt(top, "ident", [128, 128])
        b_ident = Buf()
        modT = sbt(top, "modT", [128, 96, 2])
        b_modT = Buf()
        cdec = sbt(top, "cdec", [128, 32])
        b_cdec = Buf()
        NSt = sbt(top, "NSt", [128, 32])
        b_NSt = Buf()
        EC = sbt(top, "EC", [128, 1024])
        b_EC = Buf()
        ER = sbt(top, "ER", [64, 1024])
        b_ER = Buf()
        ERBS = {"t": [], "b": []}

        def make_erb(stack, n):
            ERBS["t"] = [sbt(stack, f"ERB{i}", [128, 1024]) for i in range(n)]
            ERBS["b"] = [Buf() for _ in range(n)]
        sel_ctr = [0]
        psb = [top.enter_context(nc.psum_tensor(f"ps{i}", [128, 512], F32)) for i in range(8)]
        b_ps = [Buf() for _ in range(8)]

        def col(name, j0=0, n=1):
            o, w = COLS[name]
            return cols[:, o + j0:o + j0 + n]

        S.dma("sp", lambda e: e.dma_start(out=cols[:], in_=cols_d[:, :]), writes=[b_cols])
        S.dma("sp", lambda e: e.dma_start(out=ident[:], in_=ident_d[:, :]), writes=[b_ident])

        with ExitStack() as ph:
            scT = sbt(ph, "scT", [128, 32])
            b_scT = Buf()
            WA = [sbt(ph, f"WA{i}", [128, 16, 512]) for i in range(2)]
            b_WA = [Buf() for _ in range(2)]
            tmp0 = sbt(ph, "tmp0", [128, 1024])
            b_tmp0 = Buf()
            tmp1 = sbt(ph, "tmp1", [128, 1024])
            b_tmp1 = Buf()
            tmpi = sbt(ph, "tmpi", [128, 1024], I32)
            b_tmpi = Buf()
            iot = sbt(ph, "iot", [128, 128])
            b_iot = Buf()
            pidx = sbt(ph, "pidx", [128, 4])
            b_pidx = Buf()

            S.op("act", lambda e: e.activation(out=scT[:], in_=col("cvT", 0, 32), func=AF.Silu),
                 reads=[b_cols], writes=[b_scT])
            w_ada_v = w_ada.rearrange("(kk p) n -> p kk n", p=128)
            Rrow = [sbt(ph, f"Rrow{i}", [2, 512]) for i in range(2)]
            b_Rrow = [Buf() for _ in range(2)]
            for nb in range(24):
                wa = WA[nb % 2]
                bwa = b_WA[nb % 2]
                S.dma("sp", lambda e, wa=wa, nb=nb: e.dma_start(out=wa[:], in_=w_ada_v[:, :, nb * 512:(nb + 1) * 512]),
                      writes=[bwa])
                pb = nb % 2
                for kk in range(16):
                    S.op("pe", lambda e, wa=wa, kk=kk, pb=pb: e.matmul(
                        psb[pb][0:2, :], lhsT=scT[:, kk * 2:kk * 2 + 2], rhs=wa[:, kk, :],
                        start=(kk == 0), stop=(kk == 15)), reads=[bwa, b_scT], writes=[b_ps[pb]])
                rr, brr = Rrow[nb % 2], b_Rrow[nb % 2]
                S.op("act", lambda e, rr=rr, pb=pb: e.activation(out=rr[:], in_=psb[pb][0:2, :], func=AF.Copy),
                     reads=[b_ps[pb]], writes=[brr])
                pt = 2 + (nb % 2)
                for nn in range(4):
                    S.op("pe", lambda e, rr=rr, nn=nn, pt=pt: e.transpose(
                        out=psb[pt][:, nn * 2:nn * 2 + 2], in_=rr[0:2, nn * 128:(nn + 1) * 128],
                        identity=ident[0:2, 0:2]), reads=[brr, b_ident], writes=[b_ps[pt]])
                for nn in range(4):
                    nch = nb * 4 + nn
                    S.op("dve", lambda e, nch=nch, nn=nn, pt=pt: e.tensor_scalar(
                        out=modT[:, nch, :], in0=psb[pt][:, nn * 2:nn * 2 + 2], scalar1=col("badaT", nch),
                        scalar2=None, op0=ALU.add), reads=[b_ps[pt], b_cols], writes=[b_modT])
            for base in (16, 64):
                S.op("dve", lambda e, base=base: e.tensor_scalar(
                    out=modT[:, base:base + 16, :], in0=modT[:, base:base + 16, :], scalar1=1.0, scalar2=None,
                    op0=ALU.add), reads=[b_modT], writes=[b_modT])
            S.op("act", lambda e: e.activation(out=tmp0[:, 0:16], in_=col("lam", 0, 16), func=AF.Exp, scale=-1.0),
                 reads=[b_cols], writes=[b_tmp0])
            S.op("act", lambda e: e.activation(out=tmp0[:, 16:32], in_=tmp0[:, 0:16], func=AF.Ln, bias=1.0),
                 reads=[b_tmp0], writes=[b_tmp0])
            S.op("dve", lambda e: e.tensor_scalar(out=cdec[:, 0:16], in0=tmp0[:, 16:32], scalar1=-8.0, scalar2=None,
                                                  op0=ALU.mult), reads=[b_tmp0], writes=[b_cdec])
            S.op("dve", lambda e: e.tensor_scalar(out=cdec[:, 16:32], in0=tmp0[:, 16:32], scalar1=-16.0, scalar2=None,
                                                  op0=ALU.mult), reads=[b_tmp0], writes=[b_cdec])
            S.dma("sp", lambda e: e.dma_start(out=iot[:], in_=iota_d[:, :]), writes=[b_iot])
            S.op("pe", lambda e: e.transpose(out=psb[4][:, 0:128], in_=iot[:], identity=ident[:]),
                 reads=[b_iot, b_ident], writes=[b_ps[4]])
            S.op("dve", lambda e: e.tensor_copy(out=pidx[:, 0:1], in_=psb[4][:, 0:1]), reads=[b_ps[4]], writes=[b_pidx])
            S.op("dve", lambda e: e.tensor_scalar(out=pidx[:, 1:2], in0=pidx[:, 0:1], scalar1=64.0, scalar2=-64.0,
                                                  op0=ALU.is_ge, op1=ALU.mult), reads=[b_pidx], writes=[b_pidx])
            S.op("dve", lambda e: e.tensor_tensor(out=pidx[:, 2:3], in0=pidx[:, 0:1], in1=pidx[:, 1:2], op=ALU.add),
                 reads=[b_pidx], writes=[b_pidx])
            for h in range(4):
                S.op("act", lambda e, h=h: e.activation(
                    out=tmp0[:, 512 + h * 128:512 + (h + 1) * 128], in_=iot[:], func=AF.Exp,
                    scale=-float(np.log(10000.0) / 512.0), bias=-float(np.log(10000.0) / 512.0) * 128.0 * h),
                    reads=[b_iot], writes=[b_tmp0])
            S.op("dve", lambda e: e.tensor_scalar(out=tmp0[:, 512:1024], in0=tmp0[:, 512:1024],
                                                  scalar1=float(1.0 / (2.0 * np.pi)), scalar2=None, op0=ALU.mult),
                 reads=[b_tmp0], writes=[b_tmp0])

            def sincos(dst, bdst, npart, idxcol):
                for part, shift in ((0, 0.0), (1, 0.25)):
                    S.op("dve", lambda e, shift=shift: e.tensor_scalar(
                        out=tmp1[:npart, 0:512], in0=tmp0[:npart, 512:1024], scalar1=idxcol, scalar2=shift,
                        op0=ALU.mult, op1=ALU.add), reads=[b_tmp0, b_pidx], writes=[b_tmp1])
                    S.op("dve", lambda e: e.tensor_copy(out=tmpi[:npart, 0:512], in_=tmp1[:npart, 0:512]),
                         reads=[b_tmp1], writes=[b_tmpi])
                    S.op("dve", lambda e: e.tensor_copy(out=tmp1[:npart, 512:1024], in_=tmpi[:npart, 0:512]),
                         reads=[b_tmpi], writes=[b_tmp1])
                    S.op("dve", lambda e: e.tensor_tensor(out=tmp1[:npart, 0:512], in0=tmp1[:npart, 0:512],
                                                          in1=tmp1[:npart, 512:1024], op=ALU.subtract),
                         reads=[b_tmp1], writes=[b_tmp1])
                    S.op("dve", lambda e: e.tensor_scalar(out=tmp1[:npart, 512:1024], in0=tmp1[:npart, 0:512],
                                                          scalar1=0.5, scalar2=None, op0=ALU.is_gt),
                         reads=[b_tmp1], writes=[b_tmp1])
                    S.op("dve", lambda e: e.tensor_tensor(out=tmp1[:npart, 0:512], in0=tmp1[:npart, 0:512],
                                                          in1=tmp1[:npart, 512:1024], op=ALU.subtract),
                         reads=[b_tmp1], writes=[b_tmp1])
                    S.op("dve", lambda e: e.tensor_scalar(out=tmp1[:npart, 512:1024], in0=tmp1[:npart, 0:512],
                                                          scalar1=-0.5, scalar2=None, op0=ALU.is_lt),
                         reads=[b_tmp1], writes=[b_tmp1])
                    S.op("dve", lambda e: e.tensor_tensor(out=tmp1[:npart, 0:512], in0=tmp1[:npart, 0:512],
                                                          in1=tmp1[:npart, 512:1024], op=ALU.add),
                         reads=[b_tmp1], writes=[b_tmp1])
                    S.op("dve", lambda e: e.tensor_scalar(out=tmp1[:npart, 0:512], in0=tmp1[:npart, 0:512],
                                                          scalar1=0.49999, scalar2=-0.49999, op0=ALU.min, op1=ALU.max),
                         reads=[b_tmp1], writes=[b_tmp1])
                    S.op("act", lambda e, part=part: e.activation(
                        out=dst[:npart, part * 512:(part + 1) * 512], in_=tmp1[:npart, 0:512], func=AF.Sin,
                        scale=float(2.0 * np.pi)), reads=[b_tmp1], writes=[bdst])

            sincos(EC, b_EC, 128, pidx[:, 2:3])
            sincos(ER, b_ER, 64, pidx[:64, 0:1])
            S.dma("sp", lambda e: e.dma_start(out=ERs[:, :], in_=ER[:]), reads=[b_ER], writes=[bERs])
            S.op("pool", lambda e: e.memset(NSt[:], 0.0), writes=[b_NSt])
            S.barrier()

        def bcast_mod(dst, bdst, chunk0, r, tmpt, btmp):
            for q in range(4):
                pb = 4 + (q % 2)
                for j in range(4):
                    kk = q * 4 + j
                    S.op("dve", lambda e, kk=kk: e.tensor_copy(
                        out=tmpt[:], in_=cap(modT[:], (chunk0 + kk) * 2 + r, [[0, 128]])),
                        reads=[b_modT], writes=[btmp])
                    S.op("pe", lambda e, j=j, pb=pb: e.matmul(psb[pb][:, j * 128:(j + 1) * 128], lhsT=tmpt[:],
                                                             rhs=ident[:], start=True, stop=True),
                         reads=[btmp, b_ident], writes=[b_ps[pb]])
                S.op("act", lambda e, q=q, pb=pb: e.activation(out=dst[:, q * 512:(q + 1) * 512], in_=psb[pb][:],
                                                              func=AF.Copy), reads=[b_ps[pb]], writes=[bdst])

        def ln_stats(xap, bx, st, bst):
            for q in range(4):
                S.op("dve", lambda e, q=q: e.bn_stats(out=st[:, 8 + q * 6:8 + (q + 1) * 6], in_=xap[:, q * 512:(q + 1) * 512]),
                     reads=[bx], writes=[bst])
            S.op("dve", lambda e: e.bn_aggr(out=st[:, 0:2], in_=st[:, 8:32]), reads=[bst], writes=[bst])
            S.op("act", lambda e: e.activation(out=st[:, 2:3], in_=st[:, 1:2], func=AF.Sqrt, bias=EPS, scale=1.0),
                 reads=[bst], writes=[bst])
            S.op("dve", lambda e: e.reciprocal(out=st[:, 2:3], in_=st[:, 2:3]), reads=[bst], writes=[bst])

        def load_x_dma(xt, bxt, tok0, mrow, tile_in_seq):
            S.dma("sp", lambda e: e.dma_start(out=xt[:], in_=x_all[tok0:tok0 + 128, :]), writes=[bxt])
            if mrow != 1:
                return None
            k_ = sel_ctr[0] % len(ERBS["t"])
            sel_ctr[0] += 1
            erb, berb = ERBS["t"][k_], ERBS["b"][k_]
            for hh in range(2):
                S.dma("sp", lambda e, hh=hh: e.dma_start(
                    out=erb[hh * 64:(hh + 1) * 64, :],
                    in_=dram_ap(ERs, (2 * tile_in_seq + hh) * 1024, [[0, 64], [1, 1024]])),
                    reads=[bERs], writes=[berb])
            return (erb, berb)

        def load_x_add(xt, bxt, eh, ec_eng="pool"):
            if eh is None:
                return
            erb, berb = eh
            S.op("dve", lambda e: e.tensor_tensor(out=xt[:, 0:1024], in0=xt[:, 0:1024], in1=erb[:],
                                                  op=ALU.add), reads=[berb, bxt], writes=[bxt])
            S.op(ec_eng, lambda e: e.tensor_tensor(out=xt[:, 1024:2048], in0=xt[:, 1024:2048], in1=EC[:],
                                                   op=ALU.add), reads=[b_EC, bxt], writes=[bxt])

        def load_x_tile(xt, bxt, tok0, mrow, tile_in_seq, ec_eng="pool"):
            eh = load_x_dma(xt, bxt, tok0, mrow, tile_in_seq)
            load_x_add(xt, bxt, eh, ec_eng)

        def transpose_mod(xh, bxh, dstT, bdst, c0, sc_chunk, sh_chunk, r, tb0=4):
            for q in range(4):
                pb = tb0 + (q % 2)
                for j in range(4):
                    kk = q * 4 + j
                    S.op("pe", lambda e, kk=kk, j=j, pb=pb: e.transpose(
                        out=psb[pb][:, j * 128:(j + 1) * 128], in_=xh[:, kk * 128:(kk + 1) * 128], identity=ident[:]),
                        reads=[bxh, b_ident], writes=[b_ps[pb]])
                for j in range(4):
                    kk = q * 4 + j
                    S.op("dve", lambda e, kk=kk, j=j, pb=pb: e.tensor_scalar(
                        out=dstT[:, kk, c0:c0 + 128], in0=psb[pb][:, j * 128:(j + 1) * 128],
                        scalar1=modT[:, sc_chunk + kk, r:r + 1], scalar2=modT[:, sh_chunk + kk, r:r + 1],
                        op0=ALU.mult, op1=ALU.add), reads=[b_ps[pb], b_modT], writes=[bdst])

        blocks = []
        for (g0, L, mrow) in SEQS:
            T = 512 if L >= 512 else L
            for s in range(0, L, T):
                blocks.append((g0, L, mrow, s, T))

        blocks1 = []
        for (g0, L, mrow) in SEQS:
            T = 1024 if L >= 1024 else L
            for s_ in range(0, L, T):
                blocks1.append((g0, L, mrow, s_, T))

        def uvconv_gen(ph):
            stg = [sbt(ph, f"stg{i}", [128, D], BF16) for i in range(2)]
            b_stg = [Buf() for _ in range(2)]
            k = 0
            for i in range(128):
                for which in range(2):
                    st_, bst_ = stg[k % 2], b_stg[k % 2]
                    k += 1
                    if which == 0:
                        S.dma("pool", lambda e: e.dma_start(out=st_[:], in_=uth[i, :, :]), writes=[bst_])
                        S.dma("sp", lambda e: e.dma_start(out=UTB[i, :, :], in_=st_[:]), reads=[bst_], writes=[bUTB])
                    else:
                        S.dma("pool", lambda e: e.dma_start(out=st_[:], in_=pv[i * 128:(i + 1) * 128, :]),
                              writes=[bst_])
                        S.dma("sp", lambda e: e.dma_start(out=PVB[i * 128:(i + 1) * 128, :], in_=st_[:]),
                              reads=[bst_], writes=[bPVB])
                    yield

        def p2b1_gen(ph):
            zb = [sbt(ph, f"zb{i}", [128, 512 + 32]) for i in range(4)]
            b_zb = [Buf() for _ in range(4)]
            ct = sbt(ph, "ct", [128, 8, 512])
            b_ctl = [Buf() for _ in range(8)]
            sq = [sbt(ph, f"sq{i}", [128, 512]) for i in range(2)]
            b_sq = [Buf() for _ in range(2)]
            mu = sbt(ph, "mu", [128, 512])
            b_mu = Buf()
            rs = sbt(ph, "rs", [128, 512])
            b_rs = Buf()
            catc = [sbt(ph, f"catc{i}", [128, 8, 512], BF16) for i in range(1)]
            b_catc = [Buf() for _ in range(1)]
            ones = sbt(ph, "ones", [128, 128])
            b_ones = Buf()
            S.op("pool", lambda e: e.memset(ones[:], 1.0), writes=[b_ones])
            yield "alloc"
            zi = 0
            for bi, (g0, L, mrow, s, T) in enumerate(blocks):
                lo = max(s - 15, 0)
                hi = min(s + T + 15, L)
                doff = lo - (s - 15)
                while zready.get(g0, 0) < hi:
                    yield "wait"
                for cp in range(0, 8, 2):
                    zs_ = []
                    for cc in (cp, cp + 1):
                        z, bz = zb[zi % 4], b_zb[zi % 4]
                        zi += 1
                        zs_.append((z, bz))
                        if lo > s - 15 or hi < s + T + 15:
                            S.op("dve", lambda e, z=z: e.memset(z[:], 0.0), writes=[bz])
                        S.dma("sp", lambda e, z=z, cc=cc, lo=lo, hi=hi, doff=doff, g0=g0: e.dma_start(
                            out=z[:, doff:doff + hi - lo], in_=Zs[cc * 128:(cc + 1) * 128, g0 + lo:g0 + hi]),
                            reads=[bZ], writes=[bz])
                    for k in range(31):
                        for q_, cc in enumerate((cp, cp + 1)):
                            z, bz = zs_[q_]
                            if k == 0:
                                S.op("dve", lambda e, z=z, cc=cc, T=T: e.tensor_scalar(
                                    out=ct[:, cc, 0:T], in0=z[:, 0:T], scalar1=col("conv_w", cc * 31),
                                    scalar2=col("conv_b", cc), op0=ALU.mult, op1=ALU.add), reads=[bz, b_cols],
                                    writes=[b_ctl[cc]])
                            else:
                                S.op("dve", lambda e, z=z, cc=cc, T=T, k=k: e.scalar_tensor_tensor(
                                    out=ct[:, cc, 0:T], in0=z[:, k:k + T], scalar=col("conv_w", cc * 31 + k),
                                    in1=ct[:, cc, 0:T], op0=ALU.mult, op1=ALU.add),
                                    reads=[bz, b_cols, b_ctl[cc]], writes=[b_ctl[cc]])
                            yield
                    for cc in (cp, cp + 1):
                        S.op("act", lambda e, cc=cc, T=T: e.activation(out=sq[cc % 2][:, 0:T], in_=ct[:, cc, 0:T],
                                                                       func=AF.Square), reads=[b_ctl[cc]],
                             writes=[b_sq[cc % 2]])
                        S.op("pe", lambda e, cc=cc, T=T: e.matmul(psb[6][:, 0:T], lhsT=ones[:], rhs=ct[:, cc, 0:T],
                                                                  start=True, stop=True),
                             reads=[b_ones, b_ctl[cc]], writes=[b_ps[6]])
                        if cc == 0:
                            S.op("dve", lambda e, T=T: e.tensor_copy(out=mu[:, 0:T], in_=psb[6][:, 0:T]),
                                 reads=[b_ps[6]], writes=[b_mu])
                        else:
                            S.op("dve", lambda e, T=T: e.tensor_tensor(out=mu[:, 0:T], in0=psb[6][:, 0:T],
                                                                       in1=mu[:, 0:T], op=ALU.add),
                                 reads=[b_ps[6], b_mu], writes=[b_mu])
                        S.op("pe", lambda e, cc=cc, T=T: e.matmul(psb[7][:, 0:T], lhsT=ones[:],
                                                                  rhs=sq[cc % 2][:, 0:T], start=True, stop=True),
                             reads=[b_ones, b_sq[cc % 2]], writes=[b_ps[7]])
                        if cc == 0:
                            S.op("dve", lambda e, T=T: e.tensor_copy(out=rs[:, 0:T], in_=psb[7][:, 0:T]),
                                 reads=[b_ps[7]], writes=[b_rs])
                        else:
                            S.op("dve", lambda e, T=T: e.tensor_tensor(out=rs[:, 0:T], in0=psb[7][:, 0:T],
                                                                       in1=rs[:, 0:T], op=ALU.add),
                                 reads=[b_ps[7], b_rs], writes=[b_rs])
                S.op("dve", lambda e, T=T: e.tensor_scalar(out=mu[:, 0:T], in0=mu[:, 0:T], scalar1=1.0 / 1024.0,
                                                           scalar2=None, op0=ALU.mult), reads=[b_mu],
                     writes=[b_mu])
                S.op("dve", lambda e, T=T: e.tensor_tensor(out=sq[0][:, 0:T], in0=mu[:, 0:T], in1=mu[:, 0:T],
                                                           op=ALU.mult), reads=[b_mu], writes=[b_sq[0]])
                S.op("dve", lambda e, T=T: e.scalar_tensor_tensor(
                    out=rs[:, 0:T], in0=rs[:, 0:T], scalar=1.0 / 1024.0, in1=sq[0][:, 0:T], op0=ALU.mult,
                    op1=ALU.subtract), reads=[b_rs, b_sq[0]], writes=[b_rs])
                S.op("act", lambda e, T=T: e.activation(out=rs[:, 0:T], in_=rs[:, 0:T], func=AF.Sqrt, bias=EPS,
                                                        scale=1.0), reads=[b_rs], writes=[b_rs])
                S.op("dve", lambda e, T=T: e.reciprocal(out=rs[:, 0:T], in_=rs[:, 0:T]), reads=[b_rs],
                     writes=[b_rs])
                cco, bcco = catc[0], b_catc[0]
                for cc in range(8):
                    S.op("dve", lambda e, cc=cc, T=T: e.tensor_tensor(
                        out=ct[:, cc, 0:T], in0=ct[:, cc, 0:T], in1=mu[:, 0:T], op=ALU.subtract),
                        reads=[b_ctl[cc], b_mu], writes=[b_ctl[cc]])
                    S.op("dve", lambda e, cc=cc, T=T: e.tensor_tensor(
                        out=ct[:, cc, 0:T], in0=ct[:, cc, 0:T], in1=rs[:, 0:T], op=ALU.mult),
                        reads=[b_ctl[cc], b_rs], writes=[b_ctl[cc]])
                    S.op("act", lambda e, cc=cc, T=T, cco=cco: e.activation(
                        out=cco[:, cc, 0:T], in_=ct[:, cc, 0:T], func=AF.Silu, scale=col("cln_g", cc),
                        bias=col("cln_b", cc)), reads=[b_ctl[cc], b_cols], writes=[bcco])
                S.dma("sp", lambda e, g0=g0, s=s, T=T, cco=cco: e.dma_start(
                    out=CATC[:, g0 + s:g0 + s + T].rearrange("(c p) t -> p c t", p=128), in_=cco[:, :, 0:T]),
                    reads=[bcco], writes=[bCATC])

        zready = {}
        cst = ExitStack()
        cgen0 = [None]
        if stop_after >= 3:
            cgen0[0] = p2b1_gen(cst)
            assert next(cgen0[0]) == "alloc"

        if stop_after >= 1:
            with ExitStack() as ph:
                xts = [sbt(ph, f"xt{i}", [128, D]) for i in range(3)]
                b_xts = [Buf() for _ in range(3)]
                sts = [sbt(ph, f"st{i}", [128, 32]) for i in range(2)]
                b_sts = [Buf() for _ in range(2)]
                uTs = [sbt(ph, f"uT{i}", [128, 16, 1024], BF16) for i in range(2)]
                b_uTs = [Buf() for _ in range(2)]
                Wb = [sbt(ph, f"Wb{i}", [128, 16, 128], BF16) for i in range(4)]
                b_Wb = [Buf() for _ in range(4)]
                sg = [sbt(ph, f"sg{i}", [128, 512]) for i in range(3)]
                b_sg = [Buf() for _ in range(3)]
                zo = [sbt(ph, f"zo{i}", [128, 512], BF16) for i in range(2)]
                b_zo = [Buf() for _ in range(2)]
                ot = [sbt(ph, f"ot{i}", [128, 512]) for i in range(4)]
                b_ot = [Buf() for _ in range(4)]
                xi = 0
                wi = 0
                oi = 0
                gi = 0
                xi_ = [0]
                lnq = []
                make_erb(ph, 2)

                def ln_tile(bi, ti):
                    g0, L, mrow, s, T = blocks1[bi]
                    uT, b_uT = uTs[bi % 2], b_uTs[bi % 2]
                    xt, bxt = xts[xi_[0] % 3], b_xts[xi_[0] % 3]
                    st, bst = sts[xi_[0] % 2], b_sts[xi_[0] % 2]
                    xi_[0] += 1
                    tok0 = g0 + s + ti * 128
                    load_x_tile(xt, bxt, tok0, mrow, (s + ti * 128) // 128, ec_eng="dve")
                    ln_stats(xt, bxt, st, bst)
                    S.op("dve", lambda e, xt=xt, st=st: e.tensor_scalar(
                        out=xt[:], in0=xt[:], scalar1=st[:, 0:1], scalar2=st[:, 2:3], op0=ALU.subtract,
                        op1=ALU.mult), reads=[bxt, bst], writes=[bxt])
                    lnq.append((xt, bxt, uT, b_uT, ti * 128, mrow))

                def ln_b():
                    if lnq:
                        xt, bxt, uT, b_uT, c0, mrow = lnq.pop(0)
                        transpose_mod(xt, bxt, uT, b_uT, c0, 16, 0, mrow, tb0=6)

                order = []
                for cc in range(8):
                    order += [cc, 8 + cc]
                order += list(range(16, 32))
                for bi, (g0, L, mrow, s, T) in enumerate(blocks1):
                    uT, b_uT = uTs[bi % 2], b_uTs[bi % 2]
                    if bi == 0:
                        for ti in range(T // 128):
                            ln_tile(0, ti)
                            ln_b()
                    while lnq:
                        ln_b()
                    nxt_tiles = list(range(blocks1[bi + 1][4] // 128)) if bi + 1 < len(blocks1) else []
                    pend_val = None
                    nh = max(T // 512, 1)
                    Th = min(T, 512)
                    for idx, n in enumerate(order):
                        if idx == 16:
                            zready[g0] = s + T
                        if cgen0[0] is not None and idx >= 1:
                            for _ in range(7 if T >= 1024 else 2):
                                if next(cgen0[0], None) == "wait":
                                    break
                        if idx % 4 == 1:
                            ln_b()
                            if nxt_tiles:
                                ln_tile(bi + 1, nxt_tiles.pop(0))
                        wb, bwb = Wb[wi % 4], b_Wb[wi % 4]
                        wi += 1
                        S.dma("pool", lambda e, wb=wb, n=n: e.dma_start(
                            out=wb[:].rearrange("p a b -> p (a b)"), in_=wih[n, :, :]), writes=[bwb])
                        pbs = [((wi % 3) * 2 + hh) for hh in range(nh)]
                        for kk in range(16):
                            for hh in range(nh):
                                S.op("pe", lambda e, wb=wb, kk=kk, hh=hh: e.matmul(
                                    psb[pbs[hh]][:, 0:Th], lhsT=wb[:, kk, :], rhs=uT[:, kk, hh * 512:hh * 512 + Th],
                                    start=(kk == 0), stop=(kk == 15)), reads=[bwb, b_uT], writes=[b_ps[pbs[hh]]])
                        if n < 8:
                            pend_val = pbs
                            continue
                        for hh in range(nh):
                            pb = pbs[hh]
                            tsl = slice(g0 + s + hh * 512, g0 + s + hh * 512 + Th)
                            if n < 16:
                                cc = n - 8
                                sgt, bsg = sg[gi % 3], b_sg[gi % 3]
                                gi += 1
                                o, bo = ot[oi % 4], b_ot[oi % 4]
                                oi += 1
                                S.op("act", lambda e, sgt=sgt, pb=pb: e.activation(
                                    out=sgt[:, 0:Th], in_=psb[pb][:, 0:Th], func=AF.Sigmoid),
                                    reads=[b_ps[pb]], writes=[bsg])
                                pv_ = pend_val[hh]
                                S.op("dve", lambda e, o=o, sgt=sgt, pv_=pv_: e.tensor_tensor(
                                    out=o[:, 0:Th], in0=psb[pv_][:, 0:Th], in1=sgt[:, 0:Th], op=ALU.mult),
                                    reads=[b_ps[pv_], bsg], writes=[bo])
                                S.dma("sp", lambda e, o=o, cc=cc, tsl=tsl: e.dma_start(
                                    out=Zs[cc * 128:(cc + 1) * 128, tsl], in_=o[:, 0:Th]), reads=[bo], writes=[bZ])
                            elif n < 24:
                                cc = n - 16
                                o, bo = ot[oi % 4], b_ot[oi % 4]
                                oi += 1
                                S.op("act", lambda e, o=o, pb=pb: e.activation(
                                    out=o[:, 0:Th], in_=psb[pb][:, 0:Th], func=AF.Gelu_apprx_tanh),
                                    reads=[b_ps[pb]], writes=[bo])
                                S.dma("sp", lambda e, o=o, cc=cc, tsl=tsl: e.dma_start(
                                    out=GRs[cc * 128:(cc + 1) * 128, tsl], in_=o[:, 0:Th]), reads=[bo], writes=[bGR])
                            else:
                                cc = n - 24
                                o, bo = ot[oi % 4], b_ot[oi % 4]
                                oi += 1
                                S.op("dve", lambda e, o=o, pb=pb: e.tensor_copy(out=o[:, 0:Th], in_=psb[pb][:, 0:Th]),
                                     reads=[b_ps[pb]], writes=[bo])
                                S.dma("sp", lambda e, o=o, cc=cc, tsl=tsl: e.dma_start(
                                    out=RXs[cc * 128:(cc + 1) * 128, tsl], in_=o[:, 0:Th]), reads=[bo], writes=[bRX])
                    while nxt_tiles:
                        ln_b()
                        ln_tile(bi + 1, nxt_tiles.pop(0))
                S.barrier()

        if stop_after >= 2:
            with ExitStack() as ph:
                LM = 4096
                rxp = sbt(ph, "rxp", [128, LM + 4])
                b_rxp = Buf()
                xr = sbt(ph, "xr", [128, LM])
                b_xr = Buf()
                xrb = sbt(ph, "xrb", [128, LM], BF16)
                b_xrb = Buf()
                At = sbt(ph, "At", [128, LM])
                b_A = Buf()
                Bt = sbt(ph, "Bt", [128, LM])
                b_B = Buf()
                Tm = sbt(ph, "Tm", [128, LM])
                b_T = Buf()
                HF = sbt(ph, "HF", [128, LM])
                b_HF = Buf()
                GRt = sbt(ph, "GRt", [128, LM])
                b_GRt = Buf()
                yrb = sbt(ph, "yrb", [128, LM], BF16)
                b_yrb = Buf()
                GWb = sbt(ph, "GWb", [128, 32, 128], BF16)
                b_GWb = Buf()
                b_GWf = Buf()
                zcol = sbt(ph, "zcol", [128, 1])
                b_zcol = Buf()
                S.op("pool", lambda e: e.memset(zcol[:], 0.0), writes=[b_zcol])
                inner_ = ExitStack()
                GWf = sbt(inner_, "GWf", [128, 32, 128])
                S.dma("sp", lambda e: e.dma_start(out=GWf[:], in_=gw_d.rearrange("g p m -> p g m")), writes=[b_GWf])
                S.op("act", lambda e: e.activation(out=GWb[:], in_=GWf[:], func=AF.Copy), reads=[b_GWf],
                     writes=[b_GWb])
                S.barrier()
                inner_.close()
                for (g0_, L_, m_) in SEQS:
                    zready[g0_] = L_
                cgen = cgen0[0] if cgen0[0] is not None else iter(())
                ugen = uvconv_gen(ph) if stop_after >= 5 else iter(())
                debt = [0.0]
                npull = [0]

                def cstep():
                    npull[0] += 1
                    if npull[0] % 6 == 0:
                        next(ugen, None)
                    return next(cgen, None)

                def pull(us):
                    debt[0] += us
                    while debt[0] >= 0.3:
                        debt[0] -= 0.3
                        cstep()
                pbi = 0
                for ch in range(8):
                    for si, (g0, L, mrow) in enumerate(SEQS):
                        TB = 512 if L >= 512 else L
                        S.op("pool", lambda e: e.memset(rxp[:, 0:2], 0.0), writes=[b_rxp])
                        S.op("pool", lambda e, L=L: e.memset(rxp[:, 2 + L:3 + L], 0.0), writes=[b_rxp])
                        S.dma("sp", lambda e, ch=ch, g0=g0, L=L: e.dma_start(
                            out=rxp[:, 2:2 + L], in_=RXs[ch * 128:(ch + 1) * 128, g0:g0 + L]), reads=[bRX],
                            writes=[b_rxp])
                        S.dma("sp", lambda e, ch=ch, g0=g0, L=L: e.dma_start(
                            out=GRt[:, 0:L], in_=GRs[ch * 128:(ch + 1) * 128, g0:g0 + L]), reads=[bGR],
                            writes=[b_GRt])
                        S.op("dve", lambda e, ch=ch, L=L: e.tensor_scalar(
                            out=xr[:, 0:L], in0=rxp[:, 0:L], scalar1=col("w4", ch * 4 + 0), scalar2=col("b4", ch),
                            op0=ALU.mult, op1=ALU.add), reads=[b_rxp, b_cols], writes=[b_xr])
                        for k in range(1, 4):
                            S.op("dve", lambda e, ch=ch, L=L, k=k: e.scalar_tensor_tensor(
                                out=xr[:, 0:L], in0=rxp[:, k:k + L], scalar=col("w4", ch * 4 + k), in1=xr[:, 0:L],
                                op0=ALU.mult, op1=ALU.add), reads=[b_rxp, b_cols, b_xr], writes=[b_xr])
                        S.op("act", lambda e, L=L: e.activation(out=xrb[:, 0:L], in_=xr[:, 0:L], func=AF.Copy),
                             reads=[b_xr], writes=[b_xrb])
                        pull(1.0 * L / 1100.0)
                        for d in range(2):
                            for tb in range(L // TB):
                                cs = slice(tb * TB, (tb + 1) * TB)
                                p1 = pbi % 4
                                p2 = (pbi + 1) % 4
                                pbi += 2
                                S.op("pe", lambda e, d=d, ch=ch, cs=cs, p1=p1, TB=TB: e.matmul(
                                    psb[p1][:, 0:TB], lhsT=GWb[:, d * 16 + ch, :], rhs=xrb[:, cs], start=True,
                                    stop=True), reads=[b_GWb, b_xrb], writes=[b_ps[p1]])
                                S.op("pe", lambda e, d=d, ch=ch, cs=cs, p2=p2, TB=TB: e.matmul(
                                    psb[p2][:, 0:TB], lhsT=GWb[:, d * 16 + 8 + ch, :], rhs=xrb[:, cs], start=True,
                                    stop=True), reads=[b_GWb, b_xrb], writes=[b_ps[p2]])
                                S.op("act", lambda e, d=d, ch=ch, cs=cs, p1=p1, TB=TB: e.activation(
                                    out=At[:, cs], in_=psb[p1][:, 0:TB], func=AF.Sigmoid, bias=col("ba", d * 8 + ch)),
                                    reads=[b_ps[p1], b_cols], writes=[b_A])
                                S.op("act", lambda e, d=d, ch=ch, cs=cs, p2=p2, TB=TB: e.activation(
                                    out=Bt[:, cs], in_=psb[p2][:, 0:TB], func=AF.Sigmoid, bias=col("bx", d * 8 + ch)),
                                    reads=[b_ps[p2], b_cols], writes=[b_B])
                                pull(2.0 * TB / 1100.0)
                            S.op("act", lambda e, d=d, ch=ch, L=L: e.activation(
                                out=Tm[:, 0:L], in_=At[:, 0:L], func=AF.Exp, scale=cdec[:, 16 + d * 8 + ch:17 + d * 8 + ch]),
                                reads=[b_A, b_cdec], writes=[b_T])
                            S.op("act", lambda e, L=L: e.activation(out=Tm[:, 0:L], in_=Tm[:, 0:L], func=AF.Sqrt,
                                                                    scale=-1.0, bias=1.0), reads=[b_T], writes=[b_T])
                            S.op("act", lambda e, d=d, ch=ch, L=L: e.activation(
                                out=At[:, 0:L], in_=At[:, 0:L], func=AF.Exp, scale=cdec[:, d * 8 + ch:d * 8 + ch + 1]),
                                reads=[b_A, b_cdec], writes=[b_A])
                            pull(3.0 * L / 1100.0)
                            S.op("pool", lambda e, L=L: e.tensor_tensor(out=Bt[:, 0:L], in0=Bt[:, 0:L], in1=xr[:, 0:L],
                                                                        op=ALU.mult), reads=[b_B, b_xr], writes=[b_B])
                            S.op("dve", lambda e, L=L: e.tensor_tensor(out=Bt[:, 0:L], in0=Bt[:, 0:L], in1=Tm[:, 0:L],
                                                                       op=ALU.mult), reads=[b_B, b_T], writes=[b_B])
                            if mrow == 1:
                                h0ap = col("h0", d * 8 + ch)
                                h0r = [b_cols]
                            else:
                                h0ap = zcol[:, 0:1]
                                h0r = [b_zcol]
                            if d == 0:
                                S.op("dve", lambda e, L=L, h0ap=h0ap: e.tensor_tensor_scan(
                                    out=HF[:, 0:L], data0=At[:, 0:L], data1=Bt[:, 0:L], initial=h0ap, op0=ALU.mult,
                                    op1=ALU.add), reads=[b_A, b_B] + h0r, writes=[b_HF])
                            else:
                                S.op("dve", lambda e, L=L, h0ap=h0ap: e.tensor_tensor_scan(
                                    out=cap(Tm[:], L - 1, [[-1, L]]), data0=cap(At[:], L - 1, [[-1, L]]),
                                    data1=cap(Bt[:], L - 1, [[-1, L]]), initial=h0ap, op0=ALU.mult, op1=ALU.add),
                                    reads=[b_A, b_B] + h0r, writes=[b_T])
                        if mrow == 0:
                            sidx = si - 1
                            S.op("act", lambda e, sidx=sidx, ch=ch, L=L: e.activation(
                                out=NSt[:, (sidx * 2 + 0) * 8 + ch:(sidx * 2 + 0) * 8 + ch + 1], in_=HF[:, L - 1:L],
                                func=AF.Copy), reads=[b_HF], writes=[b_NSt])
                            S.op("act", lambda e, sidx=sidx, ch=ch: e.activation(
                                out=NSt[:, (sidx * 2 + 1) * 8 + ch:(sidx * 2 + 1) * 8 + ch + 1], in_=Tm[:, 0:1],
                                func=AF.Copy), reads=[b_T], writes=[b_NSt])
                        S.op("dve", lambda e, L=L: e.tensor_tensor(out=HF[:, 0:L], in0=HF[:, 0:L], in1=Tm[:, 0:L],
                                                                   op=ALU.add), reads=[b_HF, b_T], writes=[b_HF])
                        S.op("pool", lambda e, L=L: e.tensor_tensor(out=yrb[:, 0:L], in0=HF[:, 0:L], in1=GRt[:, 0:L],
                                                                    op=ALU.mult), reads=[b_HF, b_GRt], writes=[b_yrb])
                        S.dma("sp", lambda e, ch=ch, g0=g0, L=L: e.dma_start(
                            out=CATR[ch * 128:(ch + 1) * 128, g0:g0 + L], in_=yrb[:, 0:L]), reads=[b_yrb],
                            writes=[bCATR])
                for q in range(4):
                    S.dma("sp", lambda e, q=q: e.dma_start(
                        out=ns_d[q, :].rearrange("(c p) -> p c", p=128), in_=NSt[:, q * 8:(q + 1) * 8]),
                        reads=[b_NSt], writes=[bNS])
                while True:
                    npull[0] += 1
                    if npull[0] % 6 == 0:
                        next(ugen, None)
                    try:
                        next(cgen)
                    except StopIteration:
                        break
                for _ in ugen:
                    pass
                S.barrier()

        cst.close()

        if stop_after >= 3 and os.environ.get("K_SKIP2B2", "0") != "1":
            with ExitStack() as ph:
                WO = sbt(ph, "WO", [128, 16, D], BF16)
                b_WO = Buf()
                catT = [sbt(ph, f"catT{i}", [128, 16, 512], BF16) for i in range(2)]
                b_cat = [Buf() for _ in range(2)]
                g1b = [sbt(ph, f"g1b{r}", [128, D]) for r in range(2)]
                b_g1b = [Buf() for _ in range(2)]
                lg = sbt(ph, "lg", [128, D])
                lb = sbt(ph, "lb", [128, D])
                b_lg = Buf()
                b_lb = Buf()
                tmpt = sbt(ph, "tmpt", [128, 128])
                b_tmpt = Buf()
                xts = [sbt(ph, f"xt{i}", [128, D]) for i in range(3)]
                b_xts = [Buf() for _ in range(3)]
                make_erb(ph, 3)
                ehs = {}
                x1t = [sbt(ph, f"x1t{i}", [128, D]) for i in range(2)]
                b_x1t = [Buf() for _ in range(2)]
                sts = [sbt(ph, f"st{i}", [128, 32]) for i in range(2)]
                b_sts = [Buf() for _ in range(2)]
                u2s = [sbt(ph, f"u2s{i}", [128, 16, 128], BF16) for i in range(2)]
                b_u2s = [Buf() for _ in range(2)]
                for kk in range(16):
                    S.dma("pool", lambda e, kk=kk: e.dma_start(out=WO[:, kk, :], in_=w_out[kk * 128:(kk + 1) * 128, :]),
                          writes=[b_WO])
                for r in range(2):
                    bcast_mod(g1b[r], b_g1b[r], 32, r, tmpt, b_tmpt)
                S.dma("sp", lambda e: e.dma_start(out=lg[:], in_=dram_ap(lnp, 0, [[0, 128], [1, D]])), writes=[b_lg])
                S.dma("sp", lambda e: e.dma_start(out=lb[:], in_=dram_ap(lnp, D, [[0, 128], [1, D]])), writes=[b_lb])
                tiles2 = []
                for bi, (g0, L, mrow, s, T) in enumerate(blocks):
                    for ti in range(T // 128):
                        tiles2.append((bi, ti))

                def bufs(idx):
                    k = idx % 2
                    k3 = idx % 3
                    return (xts[k3], b_xts[k3], x1t[k], b_x1t[k], sts[k], b_sts[k], u2s[k], b_u2s[k])

                def stA(idx):
                    bi, ti = tiles2[idx]
                    g0, L, mrow, s, T = blocks[bi]
                    cT, bcT = catT[bi % 2], b_cat[bi % 2]
                    xt, bxt, x1, bx1, st, bst, u2, b_u2 = bufs(idx)
                    if ti == 0:
                        S.dma("sp", lambda e: e.dma_start(
                            out=cT[:, 0:8, 0:T], in_=CATC[:, g0 + s:g0 + s + T].rearrange("(c p) t -> p c t", p=128)),
                            reads=[bCATC], writes=[bcT])
                        S.dma("sp", lambda e: e.dma_start(
                            out=cT[:, 8:16, 0:T], in_=CATR[:, g0 + s:g0 + s + T].rearrange("(c p) t -> p c t", p=128)),
                            reads=[bCATR], writes=[bcT])
                    load_x_add(xt, bxt, ehs.pop(idx))
                    for nb in range(4):
                        pb = nb
                        for kk in range(16):
                            S.op("pe", lambda e, kk=kk, nb=nb, pb=pb: e.matmul(
                                psb[pb][:], lhsT=cT[:, kk, ti * 128:(ti + 1) * 128],
                                rhs=WO[:, kk, nb * 512:(nb + 1) * 512], start=(kk == 0), stop=(kk == 15)),
                                reads=[bcT, b_WO], writes=[b_ps[pb]])

                def stA0(idx):
                    bi, ti = tiles2[idx]
                    g0, L, mrow, s, T = blocks[bi]
                    xt, bxt = bufs(idx)[0], bufs(idx)[1]
                    tok0 = g0 + s + ti * 128
                    ehs[idx] = load_x_dma(xt, bxt, tok0, mrow, (s + ti * 128) // 128)

                def stC(idx):
                    bi, ti = tiles2[idx]
                    g0, L, mrow, s, T = blocks[bi]
                    xt, bxt, x1, bx1, st, bst, u2, b_u2 = bufs(idx)
                    for nb in range(4):
                        cs = slice(nb * 512, (nb + 1) * 512)
                        S.op("dve", lambda e, nb=nb, cs=cs: e.tensor_tensor(
                            out=x1[:, cs], in0=psb[nb][:], in1=g1b[mrow][:, cs], op=ALU.mult),
                            reads=[b_ps[nb], b_g1b[mrow]], writes=[bx1])
                    S.op("dve", lambda e: e.scalar_tensor_tensor(
                        out=x1[:], in0=xt[:], scalar=ALPHA, in1=x1[:], op0=ALU.mult, op1=ALU.add),
                        reads=[bxt, bx1], writes=[bx1])

                def stB(idx):
                    bi, ti = tiles2[idx]
                    g0, L, mrow, s, T = blocks[bi]
                    xt, bxt, x1, bx1, st, bst, u2, b_u2 = bufs(idx)
                    tok0 = g0 + s + ti * 128
                    ln_stats(x1, bx1, st, bst)
                    S.op("dve", lambda e: e.tensor_scalar(
                        out=x1[:], in0=x1[:], scalar1=st[:, 0:1], scalar2=st[:, 2:3], op0=ALU.subtract,
                        op1=ALU.mult), reads=[bx1, bst], writes=[bx1])
                    S.op("dve", lambda e: e.tensor_tensor(out=x1[:], in0=x1[:], in1=lg[:], op=ALU.mult),
                         reads=[bx1, b_lg], writes=[bx1])
                    S.op("dve", lambda e: e.tensor_tensor(out=x1[:], in0=x1[:], in1=lb[:], op=ALU.add),
                         reads=[bx1, b_lb], writes=[bx1])
                    S.dma("sp", lambda e: e.dma_start(out=X1s[tok0:tok0 + 128, :], in_=x1[:]),
                          reads=[bx1], writes=[bX1])
                    ln_stats(x1, bx1, st, bst)
                    S.op("dve", lambda e: e.tensor_scalar(
                        out=xt[:], in0=x1[:], scalar1=st[:, 0:1], scalar2=st[:, 2:3], op0=ALU.subtract,
                        op1=ALU.mult), reads=[bx1, bst], writes=[bxt])
                    transpose_mod(xt, bxt, u2, b_u2, 0, 64, 48, mrow)
                    S.dma("sp", lambda e: e.dma_start(
                        out=U2T[:, tok0:tok0 + 128].rearrange("(k p) t -> p k t", p=128), in_=u2[:]),
                        reads=[b_u2], writes=[bU2T])

                N2 = len(tiles2)
                stA0(0)
                for n in range(-1, N2):
                    if n + 2 < N2:
                        stA0(n + 2)
                    if n + 1 < N2:
                        stA(n + 1)
                    if n >= 0:
                        stB(n)
                    if n + 1 < N2:
                        stC(n + 1)
                S.barrier()

        if stop_after >= 4:
            with ExitStack() as ph:
                u2 = sbt(ph, "u2p", [128, 16, 512], BF16)
                b_u2 = Buf()
                qTs = [sbt(ph, f"qT{i}", [128, 16, 512], BF16) for i in range(2)]
                b_qTs = [Buf() for _ in range(2)]
                Wq = [sbt(ph, f"Wq{i}", [128, 16, 128], BF16) for i in range(3)]
                b_Wq = [Buf() for _ in range(3)]
                SKT = sbt(ph, "SKT", [128, 16, 128], BF16)
                b_SKT = Buf()
                Sxs = [sbt(ph, f"Sx{i}", [128, 16, 128]) for i in range(2)]
                b_Sxs = [Buf() for _ in range(2)]
                S2 = sbt(ph, "S2", [128, 16, 128])
                b_S2 = Buf()
                sv = sbt(ph, "sv", [128, 16, 16])
                b_sv = Buf()
                sv2 = sbt(ph, "sv2", [128, 16, 8])
                b_sv2 = Buf()
                siu = sbt(ph, "siu", [128, 16, 16], U32)
                b_siu = Buf()
                sif = sbt(ph, "sif", [128, 16, 16])
                b_sif = Buf()
                cand = sbt(ph, "cand", [128, 8, 256])
                b_cand = Buf()
                cand2 = sbt(ph, "cand2", [128, 8, 256])
                b_cand2 = Buf()
                tops = sbt(ph, "tops", [128, 8, 16])
                b_tops = Buf()
                tops2 = sbt(ph, "tops2", [128, 8, 8])
                b_tops2 = Buf()
                posu = sbt(ph, "posu", [128, 8, 16], U32)
                b_posu = Buf()
                posf = sbt(ph, "posf", [128, 128])
                b_posf = Buf()
                thr = sbt(ph, "thr", [128, 16])
                b_thr = Buf()
                kf = sbt(ph, "kf", [128, 2, 128])
                b_kf = Buf()
                eq = sbt(ph, "eq", [128, 128, 16])
                b_eq = Buf()
                ijg = sbt(ph, "ijg", [128, 3, 128])
                b_ijg = Buf()
                zz = sbt(ph, "zz", [128, 16])
                b_zz = Buf()
                negIJG = sbt(ph, "negIJG", [128, 3, 128])
                b_negJ = Buf()
                ijgTs = [sbt(ph, f"ijgT{i}", [128, 3, 128]) for i in range(2)]
                b_ijgTs = [Buf() for _ in range(2)]
                iot = sbt(ph, "iotp", [128, 128])
                b_iot = Buf()
                OHI = [sbt(ph, f"OHI{i}", [128, 32, 128], BF16) for i in range(2)]
                b_OHI = [Buf() for _ in range(2)]
                OHJ = [sbt(ph, f"OHJ{i}", [128, 32, 128], BF16) for i in range(2)]
                b_OHJ = [Buf() for _ in range(2)]
                b_OHI_a = [Buf() for _ in range(2)]
                b_OHJ_a = [Buf() for _ in range(2)]
                ohc = 0
                Gt = sbt(ph, "Gt", [128, 128, 128], BF16)
                b_Gt = Buf()
                S.dma("sp", lambda e: e.dma_start(out=iot[:], in_=iota_d[:, :]), writes=[b_iot])
                iotb = sbt(ph, "iotb", [128, 128], BF16)
                b_iotb = Buf()
                S.op("act", lambda e: e.activation(out=iotb[:], in_=iot[:], func=AF.Copy), reads=[b_iot],
                     writes=[b_iotb])
                S.op("dve", lambda e: e.tensor_scalar(out=thr[:], in0=iot[:, 0:16], scalar1=1.0, scalar2=16.0,
                                                      op0=ALU.add, op1=ALU.mult), reads=[b_iot], writes=[b_thr])
                S.dma("pool", lambda e: e.dma_start(out=SKT[:].rearrange("p a b -> p (a b)"), in_=skt_d[:, :]),
                      writes=[b_SKT])
                ohc_ = [0]
                gp_ = [0]
                tcnt = [0]

                def phaseB(tok0, ijgT, b_ijgT, gen=None):
                    S.op("act", lambda e: e.activation(out=negIJG[:].rearrange("p a b -> p (a b)"),
                                                       in_=ijgT[:].rearrange("p a b -> p (a b)"), func=AF.Copy,
                                                       scale=-1.0), reads=[b_ijgT], writes=[b_negJ])
                    for qt in range(4):
                        c0 = qt * 32
                        ohi, bohi = OHI[ohc_[0] % 2], b_OHI[ohc_[0] % 2]
                        ohj, bohj = OHJ[ohc_[0] % 2], b_OHJ[ohc_[0] % 2]
                        bohi_a, bohj_a = b_OHI_a[ohc_[0] % 2], b_OHJ_a[ohc_[0] % 2]
                        ohc_[0] += 1
                        NACT = 8
                        for cl_ in range(32 - NACT, 32):
                            S.op("act", lambda e, c0=c0, ohi=ohi, cl_=cl_: e.activation(
                                out=ohi[:, cl_, :], in_=iot[:], func=AF.Abs,
                                bias=negIJG[:, 0, c0 + cl_:c0 + cl_ + 1], scale=1.0),
                                reads=[b_iot, b_negJ], writes=[bohi_a])
                            S.op("act", lambda e, c0=c0, ohi=ohi, cl_=cl_: e.activation(
                                out=ohi[:, cl_, :], in_=ohi[:, cl_, :], func=AF.Relu, scale=-1.0, bias=1.0),
                                reads=[bohi_a], writes=[bohi_a])
                            S.op("act", lambda e, c0=c0, ohj=ohj, cl_=cl_: e.activation(
                                out=ohj[:, cl_, :], in_=iot[:], func=AF.Abs,
                                bias=negIJG[:, 1, c0 + cl_:c0 + cl_ + 1], scale=1.0),
                                reads=[b_iot, b_negJ], writes=[bohj_a])
                            S.op("act", lambda e, c0=c0, ohj=ohj, cl_=cl_: e.activation(
                                out=ohj[:, cl_, :], in_=ohj[:, cl_, :], func=AF.Relu,
                                scale=negIJG[:, 2, c0 + cl_:c0 + cl_ + 1], bias=ijgT[:, 2, c0 + cl_:c0 + cl_ + 1]),
                                reads=[bohj_a, b_negJ, b_ijgT], writes=[bohj_a])
                        for cl_ in range(32 - NACT):
                            S.op("dve", lambda e, c0=c0, ohi=ohi, cl_=cl_: e.tensor_scalar(
                                out=ohi[:, cl_, :], in0=iotb[:], scalar1=ijgT[:, 0, c0 + cl_:c0 + cl_ + 1],
                                scalar2=None, op0=ALU.is_equal), reads=[b_iotb, b_ijgT], writes=[bohi])
                            S.op("dve", lambda e, c0=c0, ohj=ohj, cl_=cl_: e.tensor_scalar(
                                out=ohj[:, cl_, :], in0=iotb[:], scalar1=ijgT[:, 1, c0 + cl_:c0 + cl_ + 1],
                                scalar2=ijgT[:, 2, c0 + cl_:c0 + cl_ + 1], op0=ALU.is_equal, op1=ALU.mult),
                                reads=[b_iotb, b_ijgT], writes=[bohj])
                        for c4 in range(8):
                            pb = 5 + (gp_[0] % 3)
                            gp_[0] += 1
                            for j in range(4):
                                cl = c4 * 4 + j
                                S.op("pe", lambda e, cl=cl, j=j, pb=pb, ohi=ohi, ohj=ohj: e.matmul(
                                    psb[pb][:, j * 128:(j + 1) * 128], lhsT=ohj[:, cl, :], rhs=ohi[:, cl, :],
                                    start=True, stop=True),
                                    reads=([bohi_a, bohj_a] if cl >= 24 else [bohi, bohj]), writes=[b_ps[pb]])
                            cg = c0 + c4 * 4
                            S.op("act", lambda e, cg=cg, pb=pb: e.activation(
                                out=cap(Gt[:], cg, [[128, 128], [1, 4]]),
                                in_=cap(psb[pb][:], 0, [[1, 128], [128, 4]]), func=AF.Copy),
                                reads=[b_ps[pb]], writes=[b_Gt])
                        if gen is not None:
                            next(gen, None)
                    for q in range(8):
                        S.dma("sp", lambda e, q=q, tok0=tok0: e.dma_start(
                            out=Gs[q * 16:(q + 1) * 16, :, tok0:tok0 + 128].rearrange("i j c -> j i c"),
                            in_=Gt[:, q * 16:(q + 1) * 16, :]), reads=[b_Gt], writes=[bG])

                pendingB = [None]
                wqi = 0
                gp = 0
                tiles3 = [(b0_, t_) for b0_ in range(0, NT, 512) for t_ in range(4)]
                wqi_ = [0]

                def qproj(bI, hps):
                    blk0 = bI * 512
                    qT, b_qT = qTs[bI % 2], b_qTs[bI % 2]
                    if hps[0] == 0:
                        S.dma("sp", lambda e: e.dma_start(
                            out=u2[:], in_=U2T[:, blk0:blk0 + 512].rearrange("(k p) t -> p k t", p=128)),
                            reads=[bU2T], writes=[b_u2])
                    for hp in hps:
                        wq, bwq = Wq[wqi_[0] % 3], b_Wq[wqi_[0] % 3]
                        wqi_[0] += 1
                        S.dma("pool", lambda e, wq=wq, hp=hp: e.dma_start(
                            out=wq[:].rearrange("p a b -> p (a b)"), in_=wqh[hp, :, :]), writes=[bwq])
                        pb = hp % 2
                        for kk in range(16):
                            S.op("pe", lambda e, wq=wq, kk=kk, pb=pb: e.matmul(
                                psb[pb][:], lhsT=wq[:, kk, :], rhs=u2[:, kk, :], start=(kk == 0), stop=(kk == 15)),
                                reads=[bwq, b_u2], writes=[b_ps[pb]])
                        S.op("act", lambda e, hp=hp, pb=pb: e.activation(out=qT[:, hp, :], in_=psb[pb][:],
                                                                         func=AF.Copy), reads=[b_ps[pb]],
                             writes=[b_qT])

                def A1(n):
                    blk0, ti = tiles3[n]
                    Sx, b_Sx = Sxs[n % 2], b_Sxs[n % 2]
                    qT, b_qT = qTs[(n // 4) % 2], b_qTs[(n // 4) % 2]
                    if n == 0:
                        qproj(0, list(range(16)))
                    cs = slice(ti * 128, (ti + 1) * 128)
                    for q in range(4):
                        pb = 2 + (q % 2)
                        for j in range(4):
                            hp = q * 4 + j
                            S.op("pe", lambda e, hp=hp, j=j, pb=pb, cs=cs: e.matmul(
                                psb[pb][:, j * 128:(j + 1) * 128], lhsT=qT[:, hp, cs], rhs=SKT[:, hp, :],
                                start=True, stop=True), reads=[b_qT, b_SKT], writes=[b_ps[pb]])
                        S.op("act", lambda e, q=q, pb=pb: e.activation(
                            out=Sx[:, q * 4:(q + 1) * 4, :].rearrange("p a b -> p (a b)"), in_=psb[pb][:],
                            func=AF.Copy), reads=[b_ps[pb]], writes=[b_Sx])

                def A2(n):
                    blk0, ti = tiles3[n]
                    Sx, b_Sx = Sxs[n % 2], b_Sxs[n % 2]
                    ijgT, b_ijgT = ijgTs[n % 2], b_ijgTs[n % 2]
                    for hp in range(16):
                        S.op("dve", lambda e, hp=hp: e.max(out=sv[:, hp, 0:8], in_=Sx[:, hp, :]), reads=[b_Sx],
                             writes=[b_sv])
                    for hp in range(16):
                        S.op("dve", lambda e, hp=hp: e.max_index(out=siu[:, hp, 0:8], in_max=sv[:, hp, 0:8],
                                                                 in_values=Sx[:, hp, :]), reads=[b_Sx, b_sv],
                             writes=[b_siu])
                    for hp in range(16):
                        S.op("dve", lambda e, hp=hp: e.match_replace(
                            out=S2[:, hp, :], in_to_replace=sv[:, hp, 0:8], in_values=Sx[:, hp, :],
                            imm_value=-1e30), reads=[b_Sx, b_sv], writes=[b_S2])
                    yield
                    for hp in range(16):
                        S.op("dve", lambda e, hp=hp: e.max(out=sv2[:, hp, 0:8], in_=S2[:, hp, :]), reads=[b_S2],
                             writes=[b_sv2])
                    for hp in range(16):
                        S.op("dve", lambda e, hp=hp: e.max_index(out=siu[:, hp, 8:16], in_max=sv2[:, hp, 0:8],
                                                                 in_values=S2[:, hp, :]), reads=[b_S2, b_sv2],
                             writes=[b_siu])
                    S.op("dve", lambda e: e.tensor_copy(out=sv[:, :, 8:16], in_=sv2[:, :, 0:8]), reads=[b_sv2],
                         writes=[b_sv])
                    S.op("dve", lambda e: e.tensor_copy(out=sif[:], in_=siu[:]), reads=[b_siu], writes=[b_sif])
                    S.op("dve", lambda e: e.tensor_tensor(
                        out=cap(cand[:], 0, [[256, 8], [16, 16], [1, 16]]),
                        in0=cap(sv[:], 0, [[32, 8], [1, 16], [0, 16]]),
                        in1=cap(sv[:], 16, [[32, 8], [0, 16], [1, 16]]), op=ALU.add), reads=[b_sv],
                        writes=[b_cand])
                    yield
                    for h in range(8):
                        S.op("dve", lambda e, h=h: e.max(out=tops[:, h, 0:8], in_=cand[:, h, :]), reads=[b_cand],
                             writes=[b_tops])
                    for h in range(8):
                        S.op("dve", lambda e, h=h: e.max_index(out=posu[:, h, 0:8], in_max=tops[:, h, 0:8],
                                                               in_values=cand[:, h, :]), reads=[b_cand, b_tops],
                             writes=[b_posu])
                    for h in range(8):
                        S.op("dve", lambda e, h=h: e.match_replace(
                            out=cand2[:, h, :], in_to_replace=tops[:, h, 0:8], in_values=cand[:, h, :],
                            imm_value=-1e30), reads=[b_cand, b_tops], writes=[b_cand2])
                    for h in range(8):
                        S.op("dve", lambda e, h=h: e.max(out=tops2[:, h, 0:8], in_=cand2[:, h, :]),
                             reads=[b_cand2], writes=[b_tops2])
                    for h in range(8):
                        S.op("dve", lambda e, h=h: e.max_index(out=posu[:, h, 8:16], in_max=tops2[:, h, 0:8],
                                                               in_values=cand2[:, h, :]),
                             reads=[b_cand2, b_tops2], writes=[b_posu])
                    S.op("dve", lambda e: e.tensor_copy(out=tops[:, :, 8:16], in_=tops2[:, :, 0:8]),
                         reads=[b_tops2], writes=[b_tops])
                    yield
                    S.op("dve", lambda e: e.tensor_copy(out=posf[:], in_=posu[:].rearrange("p a b -> p (a b)")),
                         reads=[b_posu], writes=[b_posf])
                    S.op("dve", lambda e: e.tensor_tensor(
                        out=eq[:], in0=cap(posf[:], 0, [[1, 128], [0, 16]]),
                        in1=cap(thr[:], 0, [[0, 128], [1, 16]]), op=ALU.is_ge), reads=[b_posf, b_thr],
                        writes=[b_eq])
                    S.op("dve", lambda e: e.tensor_reduce(out=kf[:, 0, :], in_=eq[:], axis=AX.X, op=ALU.add),
                         reads=[b_eq], writes=[b_kf])
                    S.op("dve", lambda e: e.scalar_tensor_tensor(
                        out=kf[:, 1, :], in0=kf[:, 0, :], scalar=-16.0, in1=posf[:], op0=ALU.mult, op1=ALU.add),
                        reads=[b_kf, b_posf], writes=[b_kf])
                    for w in range(2):
                        S.op("dve", lambda e, w=w: e.tensor_tensor(
                            out=eq[:], in0=cap(kf[:], w * 128, [[1, 128], [0, 16]]),
                            in1=cap(iot[:], 0, [[0, 128], [1, 16]]), op=ALU.is_equal), reads=[b_kf, b_iot],
                            writes=[b_eq])
                        S.op("dve", lambda e, w=w: e.tensor_tensor(
                            out=cap(eq[:], 0, [[256, 8], [16, 16], [1, 16]]),
                            in0=cap(eq[:], 0, [[256, 8], [16, 16], [1, 16]]),
                            in1=cap(sif[:], w * 16, [[32, 8], [0, 16], [1, 16]]), op=ALU.mult),
                            reads=[b_eq, b_sif], writes=[b_eq])
                        S.op("dve", lambda e, w=w: e.tensor_reduce(out=ijg[:, w, :], in_=eq[:], axis=AX.X,
                                                                   op=ALU.add), reads=[b_eq], writes=[b_ijg])
                    S.op("dve", lambda e: e.tensor_tensor(
                        out=cap(cand2[:], 0, [[16, 8], [1, 16]]), in0=cap(tops[:], 0, [[16, 8], [1, 16]]),
                        in1=cap(tops[:], 0, [[16, 8], [0, 16]]), op=ALU.subtract), reads=[b_tops],
                        writes=[b_cand2])
                    S.op("act", lambda e: e.activation(out=cand2[:, 0, 128:256], in_=cand2[:, 0, 0:128],
                                                       func=AF.Exp), reads=[b_cand2], writes=[b_cand2])
                    S.op("dve", lambda e: e.tensor_reduce(
                        out=zz[:, 0:8], in_=cap(cand2[:], 128, [[16, 8], [1, 16]]), axis=AX.X, op=ALU.add),
                        reads=[b_cand2], writes=[b_zz])
                    S.op("dve", lambda e: e.reciprocal(out=zz[:, 8:16], in_=zz[:, 0:8]), reads=[b_zz],
                         writes=[b_zz])
                    S.op("dve", lambda e: e.tensor_tensor(
                        out=cap(ijg[:], 256, [[16, 8], [1, 16]]), in0=cap(cand2[:], 128, [[16, 8], [1, 16]]),
                        in1=cap(zz[:], 8, [[1, 8], [0, 16]]), op=ALU.mult), reads=[b_cand2, b_zz],
                        writes=[b_ijg])
                    for w in range(3):
                        S.op("pe", lambda e, w=w: e.transpose(out=psb[4][:, w * 128:(w + 1) * 128],
                                                              in_=ijg[:, w, :], identity=ident[:]),
                             reads=[b_ijg, b_ident], writes=[b_ps[4]])
                    S.op("act", lambda e: e.activation(out=ijgT[:].rearrange("p a b -> p (a b)"),
                                                       in_=psb[4][:, 0:384], func=AF.Copy), reads=[b_ps[4]],
                         writes=[b_ijgT])

                NT3 = len(tiles3)
                for n in range(-1, NT3 + 1):
                    if 0 <= n + 1 < NT3:
                        A1(n + 1)
                    gen = A2(n) if 0 <= n < NT3 else None
                    if 0 <= n - 1 < NT3:
                        phaseB(tiles3[n - 1][0] + tiles3[n - 1][1] * 128, ijgTs[(n - 1) % 2], b_ijgTs[(n - 1) % 2], gen)
                    if gen is not None:
                        for _ in gen:
                            pass
                    m_ = n + 1
                    if 0 <= m_ and (m_ // 4 + 1) * 4 < NT3:
                        qproj(m_ // 4 + 1, [4 * (m_ % 4) + x_ for x_ in range(4)])
                S.barrier()

        if stop_after >= 5:
            with ExitStack() as ph:
                TG = 512
                EG = 4
                PF = 3
                u2s4 = [sbt(ph, f"u2q{i}", [128, 16, TG], BF16) for i in range(2)]
                b_u2s4 = [Buf() for _ in range(2)]
                ACC = sbt(ph, "ACC", [128, TG // 128, D])
                b_ACC = [Buf() for _ in range(TG // 128)]
                NU = 5
                Ub = [sbt(ph, f"Ub{i}", [128, 16, 128], BF16) for i in range(NU)]
                b_Ub = [Buf() for _ in range(NU)]
                NV = 8
                Vb = [sbt(ph, f"Vb{i}", [128, D], BF16) for i in range(NV)]
                b_Vb = [Buf() for _ in range(NV)]
                Gb = [sbt(ph, f"Gb{i}", [128, TG], BF16) for i in range(NU)]
                b_Gb = [Buf() for _ in range(NU)]
                Ab = [sbt(ph, f"Ab{i}", [128, TG], BF16) for i in range(2)]
                b_Ab = [Buf() for _ in range(2)]
                Wt = [sbt(ph, f"Wt{i}", [128, EG, TG], BF16) for i in range(2)]
                b_Wt = [Buf() for _ in range(2)]
                g2b = [sbt(ph, f"g2b{r}", [128, D]) for r in range(2)]
                b_g2b = [Buf() for _ in range(2)]
                lg = sbt(ph, "lg2", [128, D])
                lb = sbt(ph, "lb2", [128, D])
                b_lg = Buf()
                b_lb = Buf()
                tmpt = sbt(ph, "tmpt2", [128, 128])
                b_tmpt = Buf()
                x1t = [sbt(ph, f"x1q{i}", [128, D]) for i in range(4)]
                b_x1t = [Buf() for _ in range(4)]
                sts = [sbt(ph, f"stq{i}", [128, 32]) for i in range(4)]
                b_sts = [Buf() for _ in range(4)]
                for r in range(2):
                    bcast_mod(g2b[r], b_g2b[r], 80, r, tmpt, b_tmpt)
                S.dma("sp", lambda e: e.dma_start(out=lg[:], in_=dram_ap(lnp, 2 * D, [[0, 128], [1, D]])),
                      writes=[b_lg])
                S.dma("sp", lambda e: e.dma_start(out=lb[:], in_=dram_ap(lnp, 3 * D, [[0, 128], [1, D]])),
                      writes=[b_lb])
                ei_ = [0]
                pend_epi = []

                def epilogue_tile(g0, t, mrow):
                    tok0 = g0 + t * 128
                    x1, bx1 = x1t[ei_[0] % 4], b_x1t[ei_[0] % 4]
                    st, bst = sts[ei_[0] % 4], b_sts[ei_[0] % 4]
                    ei_[0] += 1
                    S.dma("sp", lambda e: e.dma_start(out=x1[:], in_=X1s[tok0:tok0 + 128, :]), reads=[bX1],
                          writes=[bx1])
                    S.op("dve", lambda e: e.tensor_tensor(out=ACC[:, t, :], in0=ACC[:, t, :], in1=g2b[mrow][:],
                                                          op=ALU.mult), reads=[b_ACC[t], b_g2b[mrow]],
                         writes=[b_ACC[t]])
                    S.op("dve", lambda e: e.scalar_tensor_tensor(
                        out=x1[:], in0=x1[:], scalar=ALPHA, in1=ACC[:, t, :], op0=ALU.mult, op1=ALU.add),
                        reads=[bx1, b_ACC[t]], writes=[bx1])
                    yield
                    for q in range(4):
                        S.op("dve", lambda e, q=q: e.bn_stats(out=st[:, 8 + q * 6:8 + (q + 1) * 6],
                                                              in_=x1[:, q * 512:(q + 1) * 512]),
                             reads=[bx1], writes=[bst])
                        yield
                    S.op("dve", lambda e: e.bn_aggr(out=st[:, 0:2], in_=st[:, 8:32]), reads=[bst], writes=[bst])
                    S.op("act", lambda e: e.activation(out=st[:, 2:3], in_=st[:, 1:2], func=AF.Sqrt, bias=EPS, scale=1.0),
                         reads=[bst], writes=[bst])
                    S.op("dve", lambda e: e.reciprocal(out=st[:, 2:3], in_=st[:, 2:3]), reads=[bst], writes=[bst])
                    yield
                    S.op("dve", lambda e: e.tensor_scalar(
                        out=x1[:], in0=x1[:], scalar1=st[:, 0:1], scalar2=st[:, 2:3], op0=ALU.subtract,
                        op1=ALU.mult), reads=[bx1, bst], writes=[bx1])
                    yield
                    S.op("dve", lambda e: e.tensor_tensor(out=x1[:], in0=x1[:], in1=lg[:], op=ALU.mult),
                         reads=[bx1, b_lg], writes=[bx1])
                    yield
                    S.op("dve", lambda e: e.tensor_tensor(out=x1[:], in0=x1[:], in1=lb[:], op=ALU.add),
                         reads=[bx1, b_lb], writes=[bx1])
                    S.dma("sp", lambda e: e.dma_start(out=y_all[tok0:tok0 + 128, :], in_=x1[:]), reads=[bx1],
                          writes=[bY])

                epi_run = []
                ob_ = [0]

                def epi_step():
                    while epi_run:
                        try:
                            next(epi_run[0])
                            return
                        except StopIteration:
                            epi_run.pop(0)

                groups4 = list(range(0, NT, TG))
                NG = len(groups4)
                ntile = TG // 128
                total4 = NG * 128

                def load_u2(gi):
                    g0 = groups4[gi]
                    u2_, bu2_ = u2s4[gi % 2], b_u2s4[gi % 2]
                    S.dma("sp", lambda e: e.dma_start(
                        out=u2_[:], in_=U2T[:, g0:g0 + TG].rearrange("(k p) t -> p k t", p=128)), reads=[bU2T],
                        writes=[bu2_])

                def vstage(gj, grp):
                    wt, bwt = Wt[grp % 2], b_Wt[grp % 2]
                    for t in range(ntile):
                        for nb in range(4):
                            pb = 2 + (ob_[0] % 6)
                            ob_[0] += 1
                            for e_ in range(EG):
                                fj_ = gj * 128 + grp * EG + e_
                                vb, bvb = Vb[fj_ % NV], b_Vb[fj_ % NV]
                                S.op("pe", lambda e, e_=e_, vb=vb: e.matmul(
                                    psb[pb][:], lhsT=wt[:, e_, t * 128:(t + 1) * 128],
                                    rhs=vb[:, nb * 512:(nb + 1) * 512], start=(e_ == 0), stop=(e_ == EG - 1)),
                                    reads=[bwt, bvb], writes=[b_ps[pb]])
                            if grp == 0:
                                S.op("dve", lambda e: e.tensor_copy(
                                    out=ACC[:, t, nb * 512:(nb + 1) * 512], in_=psb[pb][:]),
                                    reads=[b_ps[pb]], writes=[b_ACC[t]])
                            else:
                                S.op("dve", lambda e: e.tensor_tensor(
                                    out=ACC[:, t, nb * 512:(nb + 1) * 512], in0=psb[pb][:],
                                    in1=ACC[:, t, nb * 512:(nb + 1) * 512], op=ALU.add),
                                    reads=[b_ps[pb], b_ACC[t]], writes=[b_ACC[t]])

                load_u2(0)
                for f in range(total4 + PF):
                    if f < total4:
                        gi, i = divmod(f, 128)
                        g0 = groups4[gi]
                        if i == 100 and gi + 1 < NG:
                            load_u2(gi + 1)
                        ub, bub = Ub[f % NU], b_Ub[f % NU]
                        vb, bvb = Vb[f % NV], b_Vb[f % NV]
                        gb, bgb = Gb[f % NU], b_Gb[f % NU]
                        S.dma("sp", lambda e: e.dma_start(
                            out=ub[:].rearrange("p a b -> p (a b)"), in_=UTB[i, :, :]), reads=[bUTB], writes=[bub])
                        S.dma("sp", lambda e: e.dma_start(
                            out=vb[:], in_=PVB[i * 128:(i + 1) * 128, :]), reads=[bPVB], writes=[bvb])
                        S.dma("sp", lambda e: e.dma_start(
                            out=gb[:], in_=Gs[i, :, g0:g0 + TG]), reads=[bG], writes=[bgb])
                    fj = f - PF
                    if fj < 0:
                        continue
                    gj, j = divmod(fj, 128)
                    u2, b_u2 = u2s4[gj % 2], b_u2s4[gj % 2]
                    if pend_epi and j < 4:
                        g_ = epilogue_tile(*pend_epi.pop(0))
                        next(g_)
                        epi_run.append(g_)
                        if j == 3:
                            while pend_epi:
                                g_ = epilogue_tile(*pend_epi.pop(0))
                                next(g_)
                                epi_run.append(g_)
                    elif j % 2 == 0:
                        epi_step()
                    ub, bub = Ub[fj % NU], b_Ub[fj % NU]
                    gb, bgb = Gb[fj % NU], b_Gb[fj % NU]
                    pa = fj % 2
                    for kk in range(16):
                        S.op("pe", lambda e, kk=kk: e.matmul(
                            psb[pa][:], lhsT=ub[:, kk, :], rhs=u2[:, kk, :], start=(kk == 0), stop=(kk == 15)),
                            reads=[bub, b_u2], writes=[b_ps[pa]])
                    ab, bab = Ab[fj % 2], b_Ab[fj % 2]
                    S.op("act", lambda e: e.activation(out=ab[:], in_=psb[pa][:], func=AF.Gelu_apprx_tanh),
                         reads=[b_ps[pa]], writes=[bab])
                    grp = j // EG
                    wt, bwt = Wt[grp % 2], b_Wt[grp % 2]
                    S.op("dve", lambda e: e.tensor_tensor(
                        out=wt[:, j % EG, :], in0=ab[:], in1=gb[:], op=ALU.mult), reads=[bab, bgb], writes=[bwt])
                    if j >= 1 and (j - 1) % EG == EG - 1:
                        vstage(gj, (j - 1) // EG)
                    if j == 127:
                        vstage(gj, 127 // EG)
                        g0j = groups4[gj]
                        pend_epi = [(g0j, t, 1 if g0j < 4096 else 0) for t in range(ntile)]
                for (eg0, et, emrow) in pend_epi:
                    g_ = epilogue_tile(eg0, et, emrow)
                    next(g_)
                    epi_run.append(g_)
                while epi_run:
                    epi_step()
                S.barrier()
        S.barrier()
    return nc


def _host_layout(inp, core):
    f = np.float32
    g = {}
    x_all = np.concatenate([inp["x_sample"][core], inp["x_prompt"][2 * core], inp["x_prompt"][2 * core + 1]], axis=0)
    g["x_all"] = np.ascontiguousarray(x_all, dtype=f)
    cols = np.zeros((128, NCOL), f)

    def put(name, arr):
        o, w = COLS[name]
        assert arr.shape == (128, w), (name, arr.shape)
        cols[:, o:o + w] = arr

    cv = np.stack([inp["c_ctx"], inp["c"][core]], axis=0)
    put("cvT", cv.reshape(2, 16, 128).transpose(2, 1, 0).reshape(128, 32))
    put("badaT", inp["b_ada"][0].reshape(96, 128).T)
    put("conv_w", inp["conv_w"][0].reshape(31, 8, 128).transpose(2, 1, 0).reshape(128, 248))
    put("conv_b", inp["conv_b"][0].reshape(8, 128).T)
    put("cln_g", inp["conv_ln_g"][0].reshape(8, 128).T)
    put("cln_b", inp["conv_ln_b"][0].reshape(8, 128).T)
    put("w4", inp["lru_conv_w"][0].reshape(4, 8, 128).transpose(2, 1, 0).reshape(128, 32))
    put("b4", inp["lru_conv_b"][0].reshape(8, 128).T)
    put("ba", inp["lru_ba"][0].reshape(2, 8, 128).transpose(2, 0, 1).reshape(128, 16))
    put("bx", inp["lru_bx"][0].reshape(2, 8, 128).transpose(2, 0, 1).reshape(128, 16))
    put("lam", inp["lru_lam"][0].reshape(2, 8, 128).transpose(2, 0, 1).reshape(128, 16))
    put("h0", inp["state_rglru"][core, 0].reshape(2, 8, 128).transpose(2, 0, 1).reshape(128, 16))
    g["cols"] = cols
    return g


_SHARED = {}


def _shared_layout(inp):
    f = np.float32
    s = {}
    s["w_ada"] = np.ascontiguousarray(inp["w_ada"][0], dtype=f)
    w_in = inp["w_in"][0]
    s["wih"] = np.ascontiguousarray(w_in.reshape(16, 128, 32, 128).transpose(2, 1, 0, 3).reshape(32, 128, 2048))
    gw = np.zeros((2, 2, 8, 128, 128), f)
    for d in range(2):
        for gi, nm in enumerate(("lru_wa", "lru_wx")):
            w = inp[nm][0, d]
            for ch in range(8):
                gw[d, gi, ch, 0:64, 0:64] = w[2 * ch]
                gw[d, gi, ch, 64:128, 64:128] = w[2 * ch + 1]
    s["gw"] = gw.reshape(32, 128, 128)
    s["w_out"] = np.ascontiguousarray(inp["w_out"][0], dtype=f)
    s["lnp"] = np.ascontiguousarray(np.stack([inp["ln1_g"][0], inp["ln1_b"][0], inp["ln2_g"][0], inp["ln2_b"][0]]))
    wq = inp["w_query"][0]
    s["wqh"] = np.ascontiguousarray(wq.reshape(16, 128, 16, 128).transpose(2, 1, 0, 3).reshape(16, 128, 2048))
    sk = inp["sub_keys"][0].reshape(16, 128, 128)
    s["skt"] = np.ascontiguousarray(sk.transpose(2, 0, 1).reshape(128, 2048))
    U = inp["peer_u"][0]
    s["uth"] = np.ascontiguousarray(U.reshape(128, 128, 16, 128).transpose(0, 3, 2, 1).reshape(128, 128, 2048))
    s["pv"] = np.ascontiguousarray(inp["peer_v"][0], dtype=f)
    s["ident"] = np.eye(128, dtype=f)
    sel = np.zeros((64, 4096), f)
    for ti in range(32):
        for m in range(128):
            sel[2 * ti + (m >> 6), ti * 128 + m] = 1.0
    s["sel"] = sel
    s["iota"] = np.tile(np.arange(128, dtype=f)[None, :], (128, 1))
    return s


def kernel(**inputs):
    inp = {k: np.asarray(v) for k, v in inputs.items()}
    stop_after = int(os.environ.get("K_STOP", "99"))
    debug = os.environ.get("K_DEBUG", "0") == "1"
    nc = build_nc(stop_after=stop_after, debug=debug)
    shared = _shared_layout(inp)
    in_maps = []
    for core in range(NCORES):
        m = dict(shared)
        m.update(_host_layout(inp, core))
        in_maps.append(m)
    res = run_bass_kernel_spmd(nc, in_maps, core_ids=list(range(NCORES)))
    if debug:
        kernel.last = res
    y_p = np.zeros((16, 256, D), np.float32)
    y_s = np.zeros((8, 4096, D), np.float32)
    ns = np.zeros((16, 1, 2, 1024), np.float32)
    for core in range(NCORES):
        r = res.results[core]
        ya = np.asarray(r["y_all"])
        y_s[core] = ya[0:4096]
        y_p[2 * core] = ya[4096:4352]
        y_p[2 * core + 1] = ya[4352:4608]
        nsr = np.asarray(r["ns"]).reshape(2, 2, 1024)
        ns[2 * core, 0] = nsr[0]
        ns[2 * core + 1, 0] = nsr[1]
    return (y_p, y_s, ns)
```
